# Optimizing a Trainium2 kernel written in Bass

```python
import math
import jax, jax.numpy as jnp
from jax import lax
import numpy as np

D_MODEL = 1024
BATCH = 8
SEQ = 4096
DEPTH = 4

HEAD_DIM = 64
N_A_LAYERS = DEPTH // 2
N_B_LAYERS = DEPTH - N_A_LAYERS
DIFF_HEADS = D_MODEL // (2 * HEAD_DIM)
SWA_Q_HEADS = D_MODEL // HEAD_DIM
SWA_KV_HEADS = 2
SWA_GROUP = SWA_Q_HEADS // SWA_KV_HEADS
KV_WIDTH = SWA_KV_HEADS * HEAD_DIM
WINDOW = 128
Q_BLOCK = 128
D_FF = 2816
CONV_WIDTH = 3
NORM_EPS = 1e-6
SUBLN_EPS = 1e-5

kernel_name = "yoco_diffattn_swa_sinks_convffn"


def alibi_slopes(n):
    return np.asarray([2.0 ** (-8.0 * (i + 1) / n) for i in range(n)], dtype=np.float32)


def rmsnorm(x, g, eps=NORM_EPS):
    xf = x.astype(jnp.float32)
    y = xf * lax.rsqrt(jnp.mean(xf * xf, axis=-1, keepdims=True) + eps)
    return (y * g.astype(jnp.float32)).astype(x.dtype)


def diff_attention(h, w_qkv, lam_qk, subln_g, w_o, lambda_init):
    B, S, _ = h.shape
    H, d = DIFF_HEADS, HEAD_DIM
    q, k, v = jnp.split(h @ w_qkv, 3, axis=-1)
    q = q.reshape(B, S, 2 * H, d)
    k = k.reshape(B, S, 2 * H, d)
    v = v.reshape(B, S, H, 2 * d)
    lq = lam_qk.astype(jnp.float32)
    lam = jnp.exp(jnp.sum(lq[0] * lq[1])) - jnp.exp(jnp.sum(lq[2] * lq[3])) + lambda_init
    slopes = jnp.asarray(np.repeat(alibi_slopes(H), 2))
    scale = d ** -0.5
    nblk = S // Q_BLOCK
    qb = q.reshape(B, nblk, Q_BLOCK, 2 * H, d).transpose(1, 0, 2, 3, 4)
    key_pos = jnp.arange(S)

    def block(args):
        q_blk, blk = args
        q_pos = blk * Q_BLOCK + jnp.arange(Q_BLOCK)
        dist = q_pos[:, None] - key_pos[None, :]
        s = jnp.einsum('bqhd,bkhd->bhqk', q_blk, k).astype(jnp.float32) * scale
        s = s - slopes[:, None, None] * dist.astype(jnp.float32)
        s = jnp.where(dist >= 0, s, -jnp.inf)
        p = jax.nn.softmax(s, axis=-1).reshape(B, H, 2, Q_BLOCK, S)
        a = (p[:, :, 0] - lam * p[:, :, 1]).astype(v.dtype)
        return jnp.einsum('bhqk,bkhe->bqhe', a, v)

    o = lax.map(block, (qb, jnp.arange(nblk)))
    o = o.transpose(1, 0, 2, 3, 4).reshape(B, S, H, 2 * d)
    o = rmsnorm(o, subln_g, SUBLN_EPS) * (1.0 - lambda_init)
    return o.reshape(B, S, H * 2 * d) @ w_o


def shared_kv(x, kv_norm, w_kv, b_kv):
    B, S, _ = x.shape
    nblk = S // WINDOW
    kv = rmsnorm(x, kv_norm) @ w_kv + b_kv
    k, v = jnp.split(kv, 2, axis=-1)

    def band(t):
        t = t.reshape(B, nblk, WINDOW, SWA_KV_HEADS, HEAD_DIM)
        prev = jnp.concatenate([jnp.zeros_like(t[:, :1]), t[:, :-1]], axis=1)
        return jnp.concatenate([prev, t], axis=2)

    return band(k), band(v)


def swa_sink_attention(h, k_band, v_band, w_q, b_q, sinks, w_o):
    B, S, _ = h.shape
    nblk = S // WINDOW
    q = (h @ w_q + b_q).reshape(B, nblk, WINDOW, SWA_KV_HEADS, SWA_GROUP, HEAD_DIM)
    s = jnp.einsum('bnqkgd,bnjkd->bnkgqj', q, k_band).astype(jnp.float32) * (HEAD_DIM ** -0.5)
    qi = jnp.arange(WINDOW)
    kj = jnp.arange(2 * WINDOW)
    dist = WINDOW + qi[:, None] - kj[None, :]
    valid = (dist >= 0) & (dist < WINDOW)
    mask = valid[None] & ((jnp.arange(nblk)[:, None, None] > 0) | (kj[None, None, :] >= WINDOW))
    slopes = jnp.asarray(alibi_slopes(SWA_Q_HEADS).reshape(SWA_KV_HEADS, SWA_GROUP))
    s = s - slopes[:, :, None, None] * dist.astype(jnp.float32)
    s = jnp.where(mask[None, :, None, None], s, -jnp.inf)
    sink = jnp.broadcast_to(
        sinks.astype(jnp.float32).reshape(SWA_KV_HEADS, SWA_GROUP)[:, :, None, None],
        s.shape[:-1] + (1,))
    p = jax.nn.softmax(jnp.concatenate([s, sink], axis=-1), axis=-1)[..., :-1]
    o = jnp.einsum('bnkgqj,bnjkd->bnqkgd', p.astype(v_band.dtype), v_band)
    return o.reshape(B, S, SWA_Q_HEADS * HEAD_DIM) @ w_o


def conv_ffn(h, w_up, conv_w, conv_b, w_down):
    gate, up = jnp.split(h @ w_up, 2, axis=-1)
    c = conv_w[CONV_WIDTH - 1] * gate + conv_b
    for sh in range(1, CONV_WIDTH):
        shifted = jnp.pad(gate[:, :-sh], ((0, 0), (sh, 0), (0, 0)))
        c = c + conv_w[CONV_WIDTH - 1 - sh] * shifted
    return (jax.nn.gelu(c, approximate=True) * up) @ w_down


def setup_inputs(seed: int = 0) -> dict:
    key = jax.random.key(seed)
    ks = jax.random.split(key, 20)
    D, d = D_MODEL, HEAD_DIM
    nrm = lambda k, shape, scale: jax.random.normal(k, shape, jnp.float32) * scale
    return {
        "x": nrm(ks[0], (BATCH, SEQ, D), 1.0),
        "norm_gains": 1.0 + nrm(ks[1], (DEPTH, 4, D), 0.1),
        "w_qkv_a": nrm(ks[2], (N_A_LAYERS, D, 3 * D), D ** -0.5),
        "lambda_qk_a": nrm(ks[3], (N_A_LAYERS, 4, d), 0.1),
        "subln_a": 1.0 + nrm(ks[4], (N_A_LAYERS, 2 * d), 0.1),
        "w_o_a": nrm(ks[5], (N_A_LAYERS, D, D), D ** -0.5),
        "kv_norm": 1.0 + nrm(ks[6], (D,), 0.1),
        "w_kv_b": nrm(ks[7], (D, 2 * KV_WIDTH), D ** -0.5),
        "b_kv_b": nrm(ks[8], (2 * KV_WIDTH,), 0.01),
        "w_q_b": nrm(ks[9], (N_B_LAYERS, D, SWA_Q_HEADS * d), D ** -0.5),
        "b_q_b": nrm(ks[10], (N_B_LAYERS, SWA_Q_HEADS * d), 0.01),
        "sinks_b": nrm(ks[11], (N_B_LAYERS, SWA_Q_HEADS), 0.5),
        "w_o_b": nrm(ks[12], (N_B_LAYERS, SWA_Q_HEADS * d, D), (SWA_Q_HEADS * d) ** -0.5),
        "w_up": nrm(ks[13], (DEPTH, D, 2 * D_FF), D ** -0.5),
        "conv_w": nrm(ks[14], (DEPTH, CONV_WIDTH, D_FF), CONV_WIDTH ** -0.5),
        "conv_b": nrm(ks[15], (DEPTH, D_FF), 0.01),
        "w_down": nrm(ks[16], (DEPTH, D_FF, D), D_FF ** -0.5),
    }


def reference(x, norm_gains, w_qkv_a, lambda_qk_a, subln_a, w_o_a, kv_norm, w_kv_b, b_kv_b,
              w_q_b, b_q_b, sinks_b, w_o_b, w_up, conv_w, conv_b, w_down):
    k_band = v_band = None
    for layer in range(DEPTH):
        g = norm_gains[layer]
        h = rmsnorm(x, g[0])
        if layer < N_A_LAYERS:
            lambda_init = 0.8 - 0.6 * math.exp(-0.3 * layer)
            mix = diff_attention(h, w_qkv_a[layer], lambda_qk_a[layer], subln_a[layer],
                                 w_o_a[layer], lambda_init)
        else:
            if layer == N_A_LAYERS:
                k_band, v_band = shared_kv(x, kv_norm, w_kv_b, b_kv_b)
            j = layer - N_A_LAYERS
            mix = swa_sink_attention(h, k_band, v_band, w_q_b[j], b_q_b[j], sinks_b[j], w_o_b[j])
        x = x + rmsnorm(mix, g[1])
        f = conv_ffn(rmsnorm(x, g[2]), w_up[layer], conv_w[layer], conv_b[layer], w_down[layer])
        x = x + rmsnorm(f, g[3])
    return x
```

```python
import math
import numpy as np
import concourse.bass as bass
import concourse.mybir as mybir
from concourse.bass_utils import run_bass_kernel_spmd

F32 = mybir.dt.float32
BF16 = mybir.dt.bfloat16
AF = mybir.ActivationFunctionType
ALU = mybir.AluOpType
AX = mybir.AxisListType

DEBUG_OT = False
DBG_B = 99
DBG_NOQDMA = False
D = 1024
S = 4096
DEPTH = 4
NA = 2
DFF = 2816
NFF = DFF // 128
HD = 64
NC_ = 8
T = 512
NT = S // T
NB = S // 128
NORM_EPS = 1e-6
SUBLN_EPS = 1e-5


def _ap(t):
    try:
        return t[:]
    except Exception:
        return t


def alibi_slopes(n):
    return [2.0 ** (-8.0 * (i + 1) / n) for i in range(n)]


class _Op:
    __slots__ = ("eng", "fn", "deps", "dma", "idx", "flag", "done_sem", "done_val", "ndma")


class Sched:
    def __init__(self):
        self.ops = []
        self.last_w = {}
        self.readers = {}
        self.last_dma = {}
        self.barrier_idx = None

    def barrier(self):
        last = {}
        for op in self.ops:
            last[(op.eng, op.dma)] = op.idx
        self.bar_deps = set(last.values())

    def add(self, eng, fn, r=(), w=(), dma=None, ndma=1):
        op = _Op()
        op.eng, op.fn, op.dma, op.idx, op.flag, op.ndma = eng, fn, dma, len(self.ops), False, ndma
        deps = set(getattr(self, "bar_deps", ()))
        for k in r:
            if k in self.last_w:
                deps.add(self.last_w[k])
        for k in w:
            if k in self.last_w:
                deps.add(self.last_w[k])
            deps.update(self.readers.get(k, ()))
        if dma is not None and dma in self.last_dma:
            deps.add(self.last_dma[dma])
        for k in r:
            self.readers.setdefault(k, []).append(op.idx)
        for k in w:
            self.last_w[k] = op.idx
            self.readers[k] = []
        if dma is not None:
            self.last_dma[dma] = op.idx
        deps.discard(op.idx)
        op.deps = deps
        self.ops.append(op)
        return op.idx

    def emit(self, nc, stack):
        ops = self.ops
        for op in ops:
            for d in op.deps:
                dd = ops[d]
                if dd.eng == "pe" and op.eng == "pe" and dd.dma is None and op.dma is None:
                    continue
                dd.flag = True
        last_of = {}
        for op in ops:
            last_of[(op.eng, op.dma is not None)] = op
        for op in last_of.values():
            op.flag = True
        cnt = {}
        for e in ("pe", "act", "dve", "pool"):
            cnt[e] = stack.enter_context(nc.semaphore("cnt_" + e))
        dkeys = []
        for op in ops:
            if op.dma is not None and op.dma not in dkeys:
                dkeys.append(op.dma)
        dsem = {k: stack.enter_context(nc.semaphore("dma_%d" % i)) for i, k in enumerate(dkeys)}
        seq = {e: 0 for e in cnt}
        dcount = {k: 0 for k in dkeys}
        for op in ops:
            if op.dma is not None:
                dcount[op.dma] += 16 * op.ndma
                op.done_sem, op.done_val = dsem[op.dma], dcount[op.dma]
            elif op.flag:
                seq[op.eng] += 1
                op.done_sem, op.done_val = cnt[op.eng], seq[op.eng]
            else:
                op.done_sem, op.done_val = None, None
        per_eng = {e: [] for e in ("pe", "act", "dve", "pool", "sp")}
        for op in ops:
            per_eng[op.eng].append(op)
        final_waits = [(op.done_sem, op.done_val) for op in last_of.values()]

        def run(engname, eng):
            waited = {}
            for op in per_eng[engname]:
                for d in sorted(op.deps):
                    dd = ops[d]
                    if dd.done_sem is None:
                        continue
                    if dd.eng == "pe" and engname == "pe" and dd.dma is None and op.dma is None:
                        continue
                    key = id(dd.done_sem)
                    if waited.get(key, (None, 0))[1] < dd.done_val:
                        eng.wait_ge(dd.done_sem, dd.done_val)
                        waited[key] = (dd.done_sem, dd.done_val)
                res = op.fn(eng)
                if op.dma is not None:
                    lst = res if isinstance(res, (list, tuple)) else [res]
                    assert len(lst) == op.ndma, (len(lst), op.ndma)
                    for ins in lst:
                        ins.then_inc(op.done_sem, 16)
                elif op.flag:
                    last = res[-1] if isinstance(res, (list, tuple)) else res
                    last.then_inc(op.done_sem, 1)
            if engname == "sp":
                for sem, val in final_waits:
                    eng.wait_ge(sem, val)

        with nc.Block() as block:
            @block.tensor
            def _(e):
                run("pe", e)

            @block.scalar
            def _(e):
                run("act", e)

            @block.vector
            def _(e):
                run("dve", e)

            @block.gpsimd
            def _(e):
                run("pool", e)

            @block.sync
            def _(e):
                run("sp", e)


class Builder:
    def __init__(self, nc, plan):
        self.nc = nc
        self.plan = plan
        self.sc = Sched()
        self.uid = 0

    def op(self, eng, fn, r=(), w=(), dma=None, ndma=1):
        return self.sc.add(eng, fn, r, w, dma, ndma)

    def barrier_keys(self):
        return list(self.sc.last_w.keys())

    def declare(self):
        nc = self.nc
        dt = nc.dram_tensor
        self.x_d = dt("x", [S, D], F32, kind="ExternalInput").ap()
        self.out_d = dt("out", [S, D], F32, kind="ExternalOutput").ap()
        self.ng_d = dt("norm_gains", [DEPTH * 4 * NC_, 128], F32, kind="ExternalInput").ap()
        self.wqkv_d = dt("w_qkv_a", [NA, D, 3 * D], F32, kind="ExternalInput").ap()
        self.lam_d = dt("lambda_qk_a", [1, NA * 4 * HD], F32, kind="ExternalInput").ap()
        self.subln_d = dt("subln_a", [1, NA * 128], F32, kind="ExternalInput").ap()
        self.woa_d = dt("w_o_a", [NA, D, D], F32, kind="ExternalInput").ap()
        self.kvn_d = dt("kv_norm", [NC_, 128], F32, kind="ExternalInput").ap()
        self.wkv_d = dt("w_kv_b", [D, 256], F32, kind="ExternalInput").ap()
        self.bkv_d = dt("b_kv_b", [2, 128], F32, kind="ExternalInput").ap()
        self.wqb_d = dt("w_q_b", [2, D, D], F32, kind="ExternalInput").ap()
        self.bqb_d = dt("b_q_b", [2, 2, 8, HD], F32, kind="ExternalInput").ap()
        self.sinks_d = dt("sinks_b", [1, 32], F32, kind="ExternalInput").ap()
        self.wob_d = dt("w_o_b", [2, D, D], F32, kind="ExternalInput").ap()
        self.wup_d = dt("w_up", [DEPTH, D, 2 * DFF], F32, kind="ExternalInput").ap()
        self.convw_d = dt("conv_w", [DEPTH * 3 * NFF, 128], F32, kind="ExternalInput").ap()
        self.convb_d = dt("conv_b", [DEPTH * NFF, 128], F32, kind="ExternalInput").ap()
        self.wdn_d = dt("w_down", [DEPTH, DFF, D], F32, kind="ExternalInput").ap()
        self.kvT_d = dt("kvT_scratch", [128, S], BF16).ap()
        self.kvV_d = dt("kvV_scratch", [128, NB * 2 * 66], BF16).ap()
        self.ot_d = (dt("ot_scratch", [8, 128, S], BF16, kind="ExternalOutput") if DEBUG_OT else dt("ot_scratch", [8, 128, S], BF16)).ap()


    def alloc(self, stack, name, shape, dtype):
        return stack.enter_context(self.nc.sbuf_tensor("%s_p%d" % (name, self.uid), shape, dtype))

    def palloc(self, stack, name, shape, dtype=F32):
        return stack.enter_context(self.nc.psum_tensor(name, shape, dtype))

    def alloc_perm(self, stack):
        A = lambda n, s, d: self.alloc(stack, n, s, d)
        self.xT = A("xT", [128, NC_, S], F32)
        self.onesf = A("onesf", [128, 128], F32)
        self.identf = A("identf", [128, 128], F32)
        self.identb = A("identb", [128, 128], BF16)
        self.onesb = A("onesb", [128, 128], BF16)
        self.mcur = A("mcur", [128, 128], BF16)
        self.mprev = A("mprev", [128, 128], BF16)
        self.gT = A("gT", [128, 128], F32)
        self.kvg = A("kvg", [128, NC_], F32)
        self.convw = A("convw", [128, DEPTH * 3 * NFF], F32)
        self.convb = A("convb", [128, DEPTH * NFF], F32)
        self.epst = A("epst", [128, 2], F32)
        self.ps = [self.palloc(stack, "ps%d" % i, [128, 512], F32) for i in range(8)]
        self.tabA = A("tabA", [128, 8, 35], F32)
        self.sgbc = A("sgbc", [128, NA, 128], F32)
        self.nlam = A("nlam", [128, 2], F32)
        self.bqT = A("bqT", [128, 2, 8], F32)
        self.bkc = A("bkc", [128, 2], F32)
        self.bvbc = A("bvbc", [128, 128], F32)
        self.biasB = A("biasB", [128, 2, 16], F32)
        self.sinkf = A("sinkf", [128, 2, 16], F32)

    def setup(self, stack, tmp):
        nc = self.nc
        A = lambda n, s, d: self.alloc(stack, n, s, d)
        TA = lambda n, s, d: self.alloc(tmp, n, s, d)
        self.pstage = TA("pstage", [128, 128], F32)

        op = self.op
        op("pool", lambda e: e.memset(self.onesf[:], 1.0), w=["onesf"])
        op("pool", lambda e: e.memset(self.onesb[:], 1.0), w=["onesb"])
        op("pool", lambda e: e.memset(self.epst[:, 0:1], NORM_EPS), w=["epsc"])
        op("pool", lambda e: e.memset(self.epst[:, 1:2], SUBLN_EPS), w=["epsc"])
        op("pool", lambda e: e.affine_select(out=self.identf[:], in_=self.onesf[:], pattern=[[-1, 128]],
                                             compare_op=ALU.is_equal, fill=0.0, base=0, channel_multiplier=1),
           r=["onesf"], w=["identf"])
        op("pool", lambda e: e.affine_select(out=self.mcur[:], in_=self.onesb[:], pattern=[[1, 128]],
                                             compare_op=ALU.is_ge, fill=0.0, base=0, channel_multiplier=-1),
           r=["onesb"], w=["mcur"])
        op("pool", lambda e: e.affine_select(out=self.mprev[:], in_=self.onesb[:], pattern=[[-1, 128]],
                                             compare_op=ALU.is_gt, fill=0.0, base=0, channel_multiplier=1),
           r=["onesb"], w=["mprev"])
        op("dve", lambda e: e.tensor_copy(out=self.identb[:], in_=self.identf[:]), r=["identf"], w=["identb"])
        self.load_cols(self.ng_d, 128, self.gT, 0)
        self.load_cols(self.kvn_d, NC_, self.kvg, 0)
        for i0 in range(0, DEPTH * 3 * NFF, 128):
            n = min(128, DEPTH * 3 * NFF - i0)
            self.load_cols(self.convw_d[i0:i0 + n, :], n, self.convw, i0)
        self.load_cols(self.convb_d, DEPTH * NFF, self.convb, 0)

    def _colkey(self, dst):
        try:
            return dst.name
        except Exception:
            return "anon"

    def load_cols(self, rows_ap, R, dst, col0, st_view=None):
        op = self.op
        st, ps = self.pstage, self.ps[0]
        if st_view is None:
            op("sp", lambda e: e.dma_start(out=st[0:R, :], in_=rows_ap), w=["pstage"], dma="pstage")
        else:
            op("sp", lambda e: e.dma_start(out=st[0:R, :].rearrange("r (a b) -> r a b", a=st_view[1]), in_=rows_ap),
               w=["pstage"], dma="pstage")
        op("pe", lambda e: e.transpose(out=ps[:, 0:R], in_=st[0:R, :], identity=self.identf[0:R, 0:R]),
           r=["pstage", "identf"], w=["ps0"])
        op("dve", lambda e: e.tensor_copy(out=dst[:, col0:col0 + R], in_=ps[:, 0:R]), r=["ps0"], w=[("cols", self._colkey(dst))])

    def load_x(self, stack):
        op = self.op
        xin = [self.alloc(stack, "xin%d" % i, [128, D], F32) for i in range(2)]
        for tb in range(NB):
            st = xin[tb % 2]
            sk = "xin%d" % (tb % 2)
            op("sp", lambda e, st=st, tb=tb: e.dma_start(out=st[:], in_=self.x_d[tb * 128:(tb + 1) * 128, :]),
               w=[sk], dma=sk)
            for half in range(2):
                ps = self.ps[2 * (tb % 2) + half]
                pk = "ps%d" % (2 * (tb % 2) + half)

                def tr(e, st=st, ps=ps, half=half):
                    last = None
                    for i in range(4):
                        c = half * 4 + i
                        last = e.transpose(out=ps[:, i * 128:(i + 1) * 128], in_=st[:, c * 128:(c + 1) * 128],
                                           identity=self.identf[:])
                    return last
                op("pe", tr, r=[sk, "identf"], w=[pk])
                dst = self.xT[:, half * 4:half * 4 + 4, tb * 128:(tb + 1) * 128]
                eng = "act" if half == 0 else "dve"

                def ev(e, ps=ps, dst=dst, eng=eng):
                    src = ps[:].rearrange("p (c t) -> p c t", c=4)
                    if eng == "act":
                        return e.copy(out=dst, in_=src)
                    return e.tensor_copy(out=dst, in_=src)
                op(eng, ev, r=[pk], w=[("xT", tb // 4)])

    def store_x(self, stack):
        op = self.op
        xo = [self.alloc(stack, "xout%d" % i, [128, D], F32) for i in range(2)]
        for tb in range(NB):
            st = xo[tb % 2]
            sk = "xout%d" % (tb % 2)
            for half in range(2):
                ps = self.ps[2 * (tb % 2) + half]
                pk = "ps%d" % (2 * (tb % 2) + half)

                def tr(e, ps=ps, half=half, tb=tb):
                    last = None
                    for i in range(4):
                        c = half * 4 + i
                        last = e.transpose(out=ps[:, i * 128:(i + 1) * 128],
                                           in_=self.xT[:, c, tb * 128:(tb + 1) * 128], identity=self.identf[:])
                    return last
                op("pe", tr, r=[("xT", tb // 4), "identf"], w=[pk])
                eng = "act" if half == 0 else "dve"

                def ev(e, ps=ps, st=st, half=half, eng=eng):
                    if eng == "act":
                        return e.copy(out=st[:, half * 512:(half + 1) * 512], in_=ps[:])
                    return e.tensor_copy(out=st[:, half * 512:(half + 1) * 512], in_=ps[:])
                op(eng, ev, r=[pk], w=[(sk, half)])
            op("sp", lambda e, st=st, tb=tb: e.dma_start(out=self.out_d[tb * 128:(tb + 1) * 128, :], in_=st[:]),
               r=[(sk, 0), (sk, 1)], dma=sk)


    def gcol(self, l, j, c):
        k = l * 32 + j * 8 + c
        return self.gT[:, k:k + 1]

    def rstd_from_ps(self, psb, pkey, rstd, rkey, n, eps):
        self.op("act", lambda e: e.activation(out=_ap(rstd), in_=psb[:], func=AF.Sqrt, bias=self.epsc(eps), scale=1.0 / n),
                r=[pkey, "epsc"], w=[rkey])
        self.op("dve", lambda e: e.reciprocal(out=_ap(rstd), in_=_ap(rstd)), r=[rkey], w=[rkey])

    def epsc(self, eps):
        return self.epst[:, 0:1] if eps == NORM_EPS else self.epst[:, 1:2]

    def stats_xT(self, tt, sqh, psi, rstd, rkey):
        op = self.op
        cols = slice(tt * T, (tt + 1) * T)
        psb, pkey = self.ps[psi], "ps%d" % psi
        for hf in range(2):
            op("act", lambda e, hf=hf: e.activation(out=sqh[:], in_=self.xT[:, hf * 4:hf * 4 + 4, cols], func=AF.Square),
               r=[("xT", tt)], w=[("sqh", i) for i in range(4)])

            def mm(e, hf=hf):
                last = None
                for i in range(4):
                    last = e.matmul(psb[:], lhsT=self.onesb[:], rhs=sqh[:, i, :], start=(hf == 0 and i == 0),
                                    stop=(hf == 1 and i == 3))
                return last
            op("pe", mm, r=[("sqh", i) for i in range(4)] + ["onesb"], w=[pkey])
        self.rstd_from_ps(psb, pkey, rstd, rkey, float(D), NORM_EPS)

    def make_xb(self, tt, l, j, xb, xbkeys, rstd, rkey, eng="dve", gains=None):
        cols = slice(tt * T, (tt + 1) * T)
        for c in range(NC_):
            g = gains[:, c:c + 1] if gains is not None else self.gcol(l, j, c)
            self.op(eng, lambda e, c=c, g=g: e.scalar_tensor_tensor(out=xb[:, c, :], in0=self.xT[:, c, cols], scalar=g,
                                                                     in1=_ap(rstd), op0=ALU.mult, op1=ALU.mult),
                    r=[("xT", tt), rkey, "gT"], w=[xbkeys[c]])

    def postnorm(self, tt, l, j, srcs, sqh, rstd2, psi, rkey="rstd2"):
        op = self.op
        cols = slice(tt * T, (tt + 1) * T)
        psb, pkey = self.ps[psi], "ps%d" % psi
        for hf in range(2):
            for i in range(4):
                ap, key = srcs[hf * 4 + i]
                key = key if isinstance(key, list) else [key]
                op("act", lambda e, ap=ap, i=i: e.activation(out=sqh[:, i, :], in_=ap, func=AF.Square),
                   r=key, w=[("sqh", i)])

            def mm(e, hf=hf):
                last = None
                for i in range(4):
                    last = e.matmul(psb[:], lhsT=self.onesb[:], rhs=sqh[:, i, :], start=(hf == 0 and i == 0),
                                    stop=(hf == 1 and i == 3))
                return last
            op("pe", mm, r=[("sqh", i) for i in range(4)] + ["onesb"], w=[pkey] + [("sqh", i) for i in range(4)])
        self.rstd_from_ps(psb, pkey, rstd2, rkey, float(D), NORM_EPS)
        for c in range(NC_):
            ap, key = srcs[c]
            key = key if isinstance(key, list) else [key]
            op("dve", lambda e, ap=ap, c=c: e.scalar_tensor_tensor(out=self.ptmp[:], in0=ap, scalar=self.gcol(l, j, c),
                                                                    in1=rstd2[:], op0=ALU.mult, op1=ALU.mult),
               r=key + [rkey, "gT"], w=["ptmp"])
            op("dve", lambda e, c=c: e.tensor_tensor(out=self.xT[:, c, cols], in0=self.xT[:, c, cols], in1=self.ptmp[:],
                                                     op=ALU.add),
               r=["ptmp", ("xT", tt)], w=[("xT", tt)])

    def ffn(self, stack, l):
        op = self.op
        A = lambda n, s, d: self.alloc(stack, n, s, d)
        xb = A("f_xb", [128, NC_, T], BF16)
        sqh = A("f_sqh", [128, 4, T], BF16)
        fTa = A("f_fTa", [128, 4, T], F32)
        aT = A("f_aT", [128, NFF, T], BF16)
        wu = [A("f_wu%d" % i, [128, NC_, 512], BF16) for i in range(2)]
        wd = [A("f_wd%d" % i, [128, 512], BF16) for i in range(4)]
        Cb = [A("f_C%d" % i, [128, T], F32) for i in range(2)]
        rstd1 = A("f_rstd1", [128, T], F32)
        rstd2 = A("f_rstd2", [128, T], F32)
        self.ptmp = A("f_ptmp", [128, T], F32)
        halo = A("f_halo", [128, NFF, 2], F32)
        xbkeys = [("xb", c) for c in range(NC_)]
        wcol = lambda k, j: self.convw[:, l * 66 + k * 22 + j: l * 66 + k * 22 + j + 1]
        bcol = lambda j: self.convb[:, l * 22 + j: l * 22 + j + 1]
        wdi = 0
        for tt in range(NT):
            cols = slice(tt * T, (tt + 1) * T)
            self.stats_xT(tt, sqh, 4, rstd1, "rstd1")
            self.make_xb(tt, l, 2, xb, xbkeys, rstd1, "rstd1")
            for jj in range(NFF // 2):
                slot = wu[jj % 2]
                sk = "wu%d" % (jj % 2)

                def ld(e, slot=slot, jj=jj):
                    a = e.dma_start(out=slot[:, :, 0:256],
                                    in_=self.wup_d[l, :, jj * 256:(jj + 1) * 256].rearrange("(c p) n -> p c n", p=128))
                    b_ = e.dma_start(out=slot[:, :, 256:512],
                                     in_=self.wup_d[l, :, DFF + jj * 256:DFF + (jj + 1) * 256].rearrange("(c p) n -> p c n", p=128))
                    return [a, b_]
                op("pool", ld, w=[sk], dma=sk, ndma=2)
                for s in range(2):
                    j = 2 * jj + s
                    gi, ui = j % 2, 2 + j % 2
                    G, U = self.ps[gi], self.ps[ui]
                    gk, uk = "ps%d" % gi, "ps%d" % ui
                    C = Cb[j % 2]
                    ck = "C%d" % (j % 2)

                    def mmg(e, slot=slot, s=s, G=G):
                        last = None
                        for c in range(NC_):
                            last = e.matmul(G[:], lhsT=slot[:, c, s * 128:(s + 1) * 128], rhs=xb[:, c, :],
                                            start=(c == 0), stop=(c == NC_ - 1))
                        return last

                    def mmu(e, slot=slot, s=s, U=U):
                        last = None
                        for c in range(NC_):
                            last = e.matmul(U[:], lhsT=slot[:, c, 256 + s * 128:256 + (s + 1) * 128], rhs=xb[:, c, :],
                                            start=(c == 0), stop=(c == NC_ - 1))
                        return last
                    op("pe", mmg, r=[sk] + xbkeys, w=[gk])
                    op("pe", mmu, r=[sk] + xbkeys, w=[uk])
                    op("act", lambda e, C=C, G=G, j=j: e.activation(out=C[:], in_=G[:], func=AF.Identity,
                                                                    bias=bcol(j), scale=wcol(2, j)),
                       r=[gk, "convw", "convb"], w=[ck])
                    op("dve", lambda e, C=C, G=G, j=j: e.scalar_tensor_tensor(out=C[:, 1:T], in0=G[:, 0:T - 1], scalar=wcol(1, j),
                                                                              in1=C[:, 1:T], op0=ALU.mult, op1=ALU.add),
                       r=[gk, ck, "convw"], w=[ck])
                    op("dve", lambda e, C=C, G=G, j=j: e.scalar_tensor_tensor(out=C[:, 2:T], in0=G[:, 0:T - 2], scalar=wcol(0, j),
                                                                              in1=C[:, 2:T], op0=ALU.mult, op1=ALU.add),
                       r=[gk, ck, "convw"], w=[ck])
                    if tt > 0:
                        op("dve", lambda e, C=C, j=j: e.scalar_tensor_tensor(out=C[:, 0:2], in0=halo[:, j, 0:2], scalar=wcol(0, j),
                                                                             in1=C[:, 0:2], op0=ALU.mult, op1=ALU.add),
                           r=[("halo", j), ck, "convw"], w=[ck])
                        op("dve", lambda e, C=C, j=j: e.scalar_tensor_tensor(out=C[:, 0:1], in0=halo[:, j, 1:2], scalar=wcol(1, j),
                                                                             in1=C[:, 0:1], op0=ALU.mult, op1=ALU.add),
                           r=[("halo", j), ck, "convw"], w=[ck])
                    if tt < NT - 1:
                        op("dve", lambda e, G=G, j=j: e.tensor_copy(out=halo[:, j, :], in_=G[:, T - 2:T]),
                           r=[gk], w=[("halo", j)])
                    op("act", lambda e, C=C: e.activation(out=C[:], in_=C[:], func=AF.Gelu_apprx_tanh), r=[ck], w=[ck])
                    op("dve", lambda e, C=C, U=U, j=j: e.tensor_tensor(out=aT[:, j, :], in0=C[:], in1=U[:], op=ALU.mult),
                       r=[ck, uk], w=[("aT", j)])
            for half in range(2):
                for j in range(NFF):
                    slot = wd[wdi % 4]
                    sk = "wd%d" % (wdi % 4)
                    wdi += 1
                    op("pool", lambda e, slot=slot, j=j, half=half: e.dma_start(
                        out=slot[:], in_=self.wdn_d[l, j * 128:(j + 1) * 128, half * 512:(half + 1) * 512]),
                       w=[sk], dma=sk)

                    def mmd(e, slot=slot, j=j, half=half):
                        last = None
                        for o in range(4):
                            last = e.matmul(self.ps[half * 4 + o][:], lhsT=slot[:, o * 128:(o + 1) * 128], rhs=aT[:, j, :],
                                            start=(j == 0), stop=(j == NFF - 1))
                        return last
                    op("pe", mmd, r=[sk, ("aT", j)], w=["ps%d" % (half * 4 + o) for o in range(4)])
                if half == 0:
                    for o in range(4):
                        op("act", lambda e, o=o: e.copy(out=fTa[:, o, :], in_=self.ps[o][:]), r=["ps%d" % o], w=[("fTa", o)])
            srcs = [(fTa[:, o, :], ("fTa", o)) for o in range(4)] + [(self.ps[4 + o][:], "ps%d" % (4 + o)) for o in range(4)]
            self.postnorm(tt, l, 3, srcs, sqh, rstd2, 0)


    def bcast_row(self, row_ap, n, dst_ap, dkey, rkey, scale=None):
        ps = self.ps[1]
        self.op("pe", lambda e: e.matmul(ps[:, 0:n], lhsT=self.onesf[0:1, :], rhs=row_ap, start=True, stop=True),
                r=[rkey, "onesf"], w=["ps1"])
        if scale is None:
            self.op("dve", lambda e: e.tensor_copy(out=dst_ap, in_=ps[:, 0:n]), r=["ps1"], w=[dkey])
        else:
            self.op("act", lambda e: e.activation(out=dst_ap, in_=ps[:, 0:n], func=AF.Copy, scale=float(scale)),
                    r=["ps1"], w=[dkey])

    def setup_A(self, stack, tmp):
        op = self.op
        A = lambda n, s, d: self.alloc(stack, n, s, d)
        TA = lambda n, s, d: self.alloc(tmp, n, s, d)
        ii = TA("iotaI", [128, 35], mybir.dt.int32)
        fi = TA("iotaF", [128, 35], F32)
        lamr = TA("lamr", [1, 2, 2, 2, HD], F32)
        prod = TA("lamprod", [1, 4, HD], F32)
        red = TA("lamred", [1, 4], F32)
        lrow = TA("lamrow", [1, 2], F32)
        sgrow = TA("sgrow", [1, NA * 128], F32)
        op("pool", lambda e: e.iota(ii[:], pattern=[[-128, 35]], base=384, channel_multiplier=1), w=["iotaI"])
        op("dve", lambda e: e.tensor_copy(out=fi[:], in_=ii[:]), r=["iotaI"], w=["iotaF"])
        sl = alibi_slopes(8)
        for h in range(8):
            W = self.widthA(h)
            op("dve", lambda e, h=h, W=W: e.tensor_scalar(out=self.tabA[:, h, :], in0=fi[:], scalar1=float(W // 2),
                                                          scalar2=float(sl[h]), op0=ALU.subtract, op1=ALU.mult),
               r=["iotaF"], w=["tabA"])
        op("sp", lambda e: e.dma_start(out=lamr[:].rearrange("o l a b d -> o (l a b d)"), in_=self.lam_d), w=["lamr"], dma="lamr")
        op("dve", lambda e: e.tensor_tensor(out=prod[:].rearrange("o (l a) d -> o l a d", l=2), in0=lamr[:, :, :, 0, :],
                                            in1=lamr[:, :, :, 1, :], op=ALU.mult), r=["lamr"], w=["lamprod"])
        op("dve", lambda e: e.reduce_sum(out=red[:], in_=prod[:], axis=AX.X), r=["lamprod"], w=["lamred"])
        op("act", lambda e: e.activation(out=red[:], in_=red[:], func=AF.Exp), r=["lamred"], w=["lamred"])
        rv = red[:].rearrange("o (l a) -> o l a", l=2)
        op("dve", lambda e: e.tensor_tensor(out=lrow[:], in0=rv[:, :, 1], in1=rv[:, :, 0], op=ALU.subtract),
           r=["lamred"], w=["lamrow"])
        for l in range(NA):
            li = 0.8 - 0.6 * math.exp(-0.3 * l)
            op("dve", lambda e, l=l, li=li: e.tensor_scalar(out=lrow[:, l:l + 1], in0=lrow[:, l:l + 1], scalar1=float(li),
                                                            scalar2=None, op0=ALU.subtract), r=["lamrow"], w=["lamrow"])
        self.bcast_row(lrow[:], 2, self.nlam[:], "nlam", "lamrow")
        op("sp", lambda e: e.dma_start(out=sgrow[:], in_=self.subln_d), w=["sgrow"], dma="sgrow")
        for l in range(NA):
            li = 0.8 - 0.6 * math.exp(-0.3 * l)
            self.bcast_row(sgrow[:, l * 128:(l + 1) * 128], 128, self.sgbc[:, l, :], ("sgbc", l), "sgrow", scale=1.0 - li)

    def widthA(self, h):
        s = alibi_slopes(8)[h]
        return int(min(512, max(128, 64.0 / s)))

    def attnA(self, stack, l):
        op = self.op
        A = lambda n, s, d: self.alloc(stack, n, s, d)
        rstdA = A("a_rstd", [128, S], F32)
        sqh = A("a_sqh", [128, 4, T], BF16)
        xbs = [A("a_xb%d" % i, [128, NC_, T], BF16) for i in range(2)]
        wqkv = A("a_wqkv", [128, NC_, 384], BF16)
        KT = A("a_KT", [128, S], BF16)
        Va = A("a_V", [128, NB, 130], BF16)
        QTs = [A("a_QT%d" % i, [128, T], BF16) for i in range(2)]
        PTs = [A("a_PT%d" % i, [128, T], BF16) for i in range(4)]
        rz = A("a_rz", [128, 2], F32)
        ssq = A("a_ssq", [128, 2], F32)
        t0 = A("a_t0", [128, 128], F32)
        osb = A("a_o", [128, 128], F32)
        junk = A("a_junk", [128, 128], F32)
        on = A("a_on", [128, 4, 128], BF16)
        OTst = [A("a_OT%d" % i, [128, T], BF16) for i in range(2)]
        ps7b = self.ps[7][:].bitcast(BF16)

        op("pool", lambda e: e.memset(Va[:, :, 128:130], 1.0), w=[("V", t_) for t_ in range(NT)])
        for tt in range(NT):
            self.stats_xT(tt, sqh, 6, rstdA[:, tt * T:(tt + 1) * T], ("rstdA", tt))
        tile_i = 0
        for h in range(8):
            slope = alibi_slopes(8)[h]
            W = self.widthA(h)

            def ldw(e, h=h):
                out = []
                for i in range(3):
                    out.append(e.dma_start(out=wqkv[:, :, i * 128:(i + 1) * 128],
                                           in_=self.wqkv_d[l, :, i * D + h * 128:i * D + (h + 1) * 128].rearrange("(c p) n -> p c n", p=128)))
                return out
            op("pool", ldw, w=["wqkv"], dma="wqkv", ndma=3)
            for tt in range(NT):
                cols = slice(tt * T, (tt + 1) * T)
                xb = xbs[tile_i % 2]
                xbkeys = [("xb%d" % (tile_i % 2), c) for c in range(NC_)]
                QT = QTs[tile_i % 2]
                qk = "QT%d" % (tile_i % 2)
                OTs = OTst[tile_i % 2]
                otk = "OTst%d" % (tile_i % 2)
                tile_i += 1
                self.make_xb(tt, l, 0, xb, xbkeys, rstdA[:, cols], ("rstdA", tt), eng="dve")
                p6, p7 = self.ps[6], self.ps[7]

                def mmq(e, xb=xb):
                    last = None
                    for c in range(NC_):
                        last = e.matmul(p6[:], lhsT=wqkv[:, c, 0:128], rhs=xb[:, c, :], start=(c == 0), stop=(c == NC_ - 1))
                    return last
                op("pe", mmq, r=["wqkv"] + xbkeys, w=["ps6"])
                op("act", lambda e, QT=QT: e.copy(out=QT[:], in_=p6[:]), r=["ps6"], w=[qk])

                def mmk(e, xb=xb):
                    last = None
                    for c in range(NC_):
                        last = e.matmul(p7[:], lhsT=wqkv[:, c, 128:256], rhs=xb[:, c, :], start=(c == 0), stop=(c == NC_ - 1))
                    return last
                op("pe", mmk, r=["wqkv"] + xbkeys, w=["ps7"])
                op("dve", lambda e, cols=cols: e.tensor_copy(out=KT[:, cols], in_=p7[:]), r=["ps7"], w=[("KT", tt)])

                def mmv(e, xb=xb):
                    last = None
                    for blk in range(4):
                        for c in range(NC_):
                            last = e.matmul(p6[:, blk * 128:(blk + 1) * 128], lhsT=xb[:, c, blk * 128:(blk + 1) * 128],
                                            rhs=wqkv[:, c, 256:384], start=(c == 0), stop=(c == NC_ - 1))
                    return last
                op("pe", mmv, r=["wqkv"] + xbkeys, w=["ps6"])
                op("dve", lambda e, tt=tt: e.tensor_copy(out=Va[:, tt * 4:(tt + 1) * 4, 0:128],
                                                         in_=p6[:].rearrange("p (b e) -> p b e", b=4)),
                   r=["ps6"], w=[("V", tt)])
                items = [(sub, kt) for sub in range(2) for kt in range(4 * tt + 4)]
                pend = None
                for ii, (sub, kt) in enumerate(items):
                    diag = kt >= 4 * tt
                    kb = kt - 4 * tt if diag else 0
                    qlo = kb * 128
                    sbi = ii % 2
                    Sb, sk = self.ps[sbi], "ps%d" % sbi
                    PT = PTs[ii % 4]
                    pk = "PT%d" % (ii % 4)
                    op("pe", lambda e, sub=sub, kt=kt, qlo=qlo, Sb=Sb, QT=QT: e.matmul(
                        Sb[:, qlo:T], lhsT=KT[sub * 64:(sub + 1) * 64, kt * 128:(kt + 1) * 128],
                        rhs=QT[sub * 64:(sub + 1) * 64, qlo:T], start=True, stop=True),
                       r=[("KT", kt // 4), qk], w=[sk])

                    def ex(e, kt=kt, qlo=qlo, Sb=Sb, PT=PT, tt=tt, h=h, W=W):
                        last = None
                        for gs in range(0, T, W):
                            lo = max(gs, qlo)
                            if lo >= gs + W:
                                continue
                            j = 4 * tt + gs // 128 - kt + 3
                            last = e.activation(out=PT[:, lo:gs + W], in_=Sb[:, lo:gs + W], func=AF.Exp,
                                                bias=self.tabA[:, h, j:j + 1], scale=0.125)
                        return last
                    op("act", ex, r=[sk, "tabA"], w=[pk])
                    if diag:
                        op("dve", lambda e, PT=PT, kb=kb: e.tensor_tensor(out=PT[:, kb * 128:(kb + 1) * 128],
                                                                          in0=PT[:, kb * 128:(kb + 1) * 128], in1=self.mcur[:],
                                                                          op=ALU.mult), r=[pk, "mcur"], w=[pk])

                    def pv(e, sub=sub, kt=kt, kb=kb, PT=PT, tt=tt):
                        last = None
                        for qb in range(kb, 4):
                            O = self.ps[2 + 2 * sub + qb // 2][:, (qb % 2) * 132:(qb % 2) * 132 + 129]
                            last = e.matmul(O, lhsT=PT[:, qb * 128:(qb + 1) * 128], rhs=Va[:, kt, 0:129],
                                            start=(kt == 0 and qb % 2 == 0), stop=(kt == 4 * tt + qb), skip_group_check=True)
                        return last
                    this_pv = (pv, [pk, ("V", kt // 4)], ["ps%d" % (2 + 2 * sub), "ps%d" % (3 + 2 * sub)])
                    if pend is not None:
                        op("pe", pend[0], r=pend[1], w=pend[2])
                    pend = this_pv
                op("pe", pend[0], r=pend[1], w=pend[2])
                for qb in range(4):
                    b0, b1 = 2 + qb // 2, 4 + qb // 2
                    c0 = (qb % 2) * 132
                    O0, O1 = self.ps[b0], self.ps[b1]
                    k0, k1 = "ps%d" % b0, "ps%d" % b1
                    op("dve", lambda e, O0=O0, c0=c0: e.reciprocal(out=rz[:, 0:1], in_=O0[:, c0 + 128:c0 + 129]), r=[k0], w=["rz"])
                    op("dve", lambda e, O1=O1, c0=c0: e.reciprocal(out=rz[:, 1:2], in_=O1[:, c0 + 128:c0 + 129]), r=[k1], w=["rz"])
                    op("dve", lambda e: e.tensor_tensor(out=rz[:, 1:2], in0=rz[:, 1:2], in1=self.nlam[:, l:l + 1], op=ALU.mult),
                       r=["rz", "nlam"], w=["rz"])
                    op("dve", lambda e, O0=O0, c0=c0: e.tensor_scalar(out=t0[:], in0=O0[:, c0:c0 + 128], scalar1=rz[:, 0:1],
                                                                      scalar2=None, op0=ALU.mult), r=[k0, "rz"], w=["t0"])
                    op("dve", lambda e, O1=O1, c0=c0: e.scalar_tensor_tensor(out=osb[:], in0=O1[:, c0:c0 + 128], scalar=rz[:, 1:2],
                                                                             in1=t0[:], op0=ALU.mult, op1=ALU.add),
                       r=[k1, "rz", "t0"], w=["osb"])
                    op("dve", lambda e: e.memset(ssq[:, 0:1], 0.0), w=["ssq"])
                    op("act", lambda e: e.activation(out=junk[:], in_=osb[:], func=AF.Square, accum_out=ssq[:, 0:1]),
                       r=["osb", "ssq"], w=["ssq", "junk"])
                    op("act", lambda e: e.activation(out=ssq[:, 1:2], in_=ssq[:, 0:1], func=AF.Sqrt, bias=self.epsc(SUBLN_EPS),
                                                     scale=1.0 / 128), r=["ssq", "epsc"], w=["ssq"])
                    op("dve", lambda e: e.reciprocal(out=ssq[:, 1:2], in_=ssq[:, 1:2]), r=["ssq"], w=["ssq"])
                    op("dve", lambda e, qb=qb: e.scalar_tensor_tensor(out=on[:, qb, :], in0=osb[:], scalar=ssq[:, 1:2],
                                                                      in1=self.sgbc[:, l, :], op0=ALU.mult, op1=ALU.mult),
                       r=["osb", "ssq", ("sgbc", l)], w=[("on", qb)])
                    op("pe", lambda e, qb=qb: e.transpose(out=ps7b[:, qb * 128:(qb + 1) * 128], in_=on[:, qb, :],
                                                          identity=self.identb[:]), r=[("on", qb), "identb"], w=["ps7"])
                op("act", lambda e, OTs=OTs: e.copy(out=OTs[:], in_=ps7b[:, 0:T]), r=["ps7"], w=[otk])
                op("sp", lambda e, OTs=OTs, h=h, cols=cols: e.dma_start(out=self.ot_d[h, :, cols], in_=OTs[:]),
                   r=[otk], w=[("otd", h, tt)], dma=otk)

    def woA(self, stack, l):
        op = self.op
        A = lambda n, s, d: self.alloc(stack, n, s, d)
        wo = A("o_wo", [128, NC_, D], BF16)
        OTin = [A("o_OT%d" % i, [128, NC_, T], BF16) for i in range(2)]
        fTa = A("o_fTa", [128, 4, T], F32)
        sqh = A("o_sqh", [128, 4, T], BF16)
        rstd2 = A("o_rstd2", [128, T], F32)
        self.ptmp = A("o_ptmp", [128, T], F32)

        def ldw(e):
            out = []
            for i in range(4):
                out.append(e.dma_start(out=wo[:, 2 * i:2 * i + 2, :],
                                       in_=self.woa_d[l, i * 256:(i + 1) * 256, :].rearrange("(c p) n -> p c n", p=128)))
            return out
        op("pool", ldw, w=["wo"], dma="wo", ndma=4)
        for tt in range(NT):
            cols = slice(tt * T, (tt + 1) * T)
            OT = OTin[tt % 2]
            ok = "OTin%d" % (tt % 2)
            op("sp", lambda e, OT=OT, cols=cols: [e.dma_start(out=OT[:, h, :], in_=self.ot_d[h, :, cols]) for h in range(8)],
               r=[("otd", h, tt) for h in range(8)], w=[ok], dma=ok, ndma=8)
            for half in range(2):
                for o in range(4):
                    oc = half * 4 + o
                    bank = self.ps[half * 4 + o]

                    def mm(e, OT=OT, oc=oc, bank=bank):
                        last = None
                        for hh in range(NC_):
                            last = e.matmul(bank[:], lhsT=wo[:, hh, oc * 128:(oc + 1) * 128], rhs=OT[:, hh, :],
                                            start=(hh == 0), stop=(hh == NC_ - 1))
                        return last
                    op("pe", mm, r=["wo", ok], w=["ps%d" % (half * 4 + o)])
                if half == 0:
                    for o in range(4):
                        op("act", lambda e, o=o: e.copy(out=fTa[:, o, :], in_=self.ps[o][:]), r=["ps%d" % o], w=[("fTa", o)])
            srcs = [(fTa[:, o, :], ("fTa", o)) for o in range(4)] + [(self.ps[4 + o][:], "ps%d" % (4 + o)) for o in range(4)]
            self.postnorm(tt, l, 1, srcs, sqh, rstd2, 0)


    def setup_B(self, stack, tmp):
        op = self.op
        A = lambda n, s, d: self.alloc(stack, n, s, d)
        TA = lambda n, s, d: self.alloc(tmp, n, s, d)
        pi = TA("posI", [128, 1], mybir.dt.int32)
        pf = TA("posF", [128, 1], F32)
        bvrow = TA("bvrow", [1, 128], F32)
        skrow = TA("skrow", [1, 32], F32)
        skbc = TA("skbc", [128, 32], F32)
        sl = alibi_slopes(16)
        for j in range(2):
            self.load_cols(self.bqb_d[j].rearrange("two g d -> g two d"), 8, self.bqT[:, j, :], 0, st_view=(8, 2, 64))
        self.load_cols(self.bkv_d, 2, self.bkc, 0)
        op("sp", lambda e: e.dma_start(out=bvrow[:], in_=self.bkv_d[1:2, :]), w=["bvrow"], dma="bvrow")
        self.bcast_row(bvrow[:], 128, self.bvbc[:], "bvbc", "bvrow")
        op("pool", lambda e: e.iota(pi[:], pattern=[[0, 1]], base=-64, channel_multiplier=1), w=["posI"])
        op("dve", lambda e: e.tensor_copy(out=pf[:], in_=pi[:]), r=["posI"], w=["posF"])
        for h in range(16):
            op("dve", lambda e, h=h: e.tensor_scalar(out=self.biasB[:, 1, h:h + 1], in0=pf[:], scalar1=float(sl[h]), scalar2=None,
                                                     op0=ALU.mult), r=["posF"], w=["biasB"])
            op("dve", lambda e, h=h: e.tensor_scalar(out=self.biasB[:, 0, h:h + 1], in0=pf[:], scalar1=float(sl[h]),
                                                     scalar2=float(-128.0 * sl[h]), op0=ALU.mult, op1=ALU.add),
               r=["posF"], w=["biasB"])
        op("sp", lambda e: e.dma_start(out=skrow[:], in_=self.sinks_d), w=["skrow"], dma="skrow")
        self.bcast_row(skrow[:], 32, skbc[:], "skbc", "skrow")
        for j in range(2):
            op("dve", lambda e, j=j: e.tensor_tensor(out=self.sinkf[:, j, :], in0=skbc[:, j * 16:(j + 1) * 16], in1=self.biasB[:, 1, :],
                                                     op=ALU.add), r=["skbc", "biasB"], w=["sinkf"])
        op("act", lambda e: e.activation(out=self.sinkf[:], in_=self.sinkf[:], func=AF.Exp), r=["sinkf"], w=["sinkf"])

    def attnB(self, stack, l):
        op = self.op
        j = l - NA
        A = lambda n, s, d: self.alloc(stack, n, s, d)
        KT = A("b_KT", [128, S], BF16)
        Vb = A("b_V", [128, NB, 2, 66], BF16)
        wqs = [A("b_wq%d" % i, [128, NC_, 2, 128], BF16) for i in range(2)]
        wos = [A("b_wo%d" % i, [128, 512], BF16) for i in range(4)]
        xb = A("b_xb", [128, NC_, T], BF16)
        sqh = A("b_sqh", [128, 4, T], BF16)
        QT = A("b_QT", [128, 8, T], BF16)
        PTs = [A("b_PT%d" % i, [128, 2, 2, 128], BF16) for i in range(2)]
        On = A("b_On", [128, 16, HD], BF16)
        OT = A("b_OT", [128, NC_, T], BF16)
        fTa = QT[:].rearrange("p g t -> p (g t)").bitcast(F32).rearrange("p (o t) -> p o t", o=4)
        rstd1 = A("b_rstd1", [128, T], F32)
        self.ptmp = A("b_ptmp", [128, T], F32)
        den = A("b_den", [128, 16], F32)
        xbkeys = [("xb", c) for c in range(NC_)]
        ps4b = self.ps[4][:].bitcast(BF16)
        obank = lambda hd: (2, 3, 7)[hd // 7]
        ooff = lambda hd: (hd % 7) * 65
        if l == NA:
            wkv = A("b_wkv", [128, NC_, 256], BF16)
            op("pool", lambda e: e.dma_start(out=wkv[:], in_=self.wkv_d.rearrange("(c p) n -> p c n", p=128)), w=["wkv"], dma="wkv")
            op("pool", lambda e: e.memset(Vb[:, :, :, 64:66], 1.0), w=[("V", t_) for t_ in range(NT)])
        else:
            op("sp", lambda e: e.dma_start(out=KT[:], in_=self.kvT_d), r=[("kvTd", t_) for t_ in range(NT)],
               w=[("KT", t_) for t_ in range(NT)], dma="KTld")
            op("sp", lambda e: e.dma_start(out=Vb[:].rearrange("p a b c -> p (a b c)"), in_=self.kvV_d),
               r=[("kvVd", t_) for t_ in range(NT)], w=[("V", t_) for t_ in range(NT)], dma="Vld")
        wqi = 0
        woi = 0
        for tt in range(NT):
            cols = slice(tt * T, (tt + 1) * T)
            self.stats_xT(tt, sqh, 4, rstd1, "rstd1")
            if l == NA:
                self.make_xb(tt, l, 0, xb, xbkeys, rstd1, "rstd1", gains=self.kvg)
                p5, p6 = self.ps[5], self.ps[6]

                def mmk(e):
                    last = None
                    for c in range(NC_):
                        last = e.matmul(p5[:], lhsT=wkv[:, c, 0:128], rhs=xb[:, c, :], start=(c == 0), stop=(c == NC_ - 1))
                    return last
                op("pe", mmk, r=["wkv"] + xbkeys, w=["ps5"])
                op("act", lambda e, cols=cols: e.activation(out=KT[:, cols], in_=p5[:], func=AF.Identity, bias=self.bkc[:, 0:1]),
                   r=["ps5", "bkc"], w=[("KT", tt)])

                def mmv(e):
                    last = None
                    for blk in range(4):
                        for c in range(NC_):
                            last = e.matmul(p6[:, blk * 128:(blk + 1) * 128], lhsT=xb[:, c, blk * 128:(blk + 1) * 128],
                                            rhs=wkv[:, c, 128:256], start=(c == 0), stop=(c == NC_ - 1))
                    return last
                op("pe", mmv, r=["wkv"] + xbkeys, w=["ps6"])
                for blk in range(4):
                    op("dve", lambda e, blk=blk, tt=tt: e.tensor_tensor(
                        out=Vb[:, tt * 4 + blk, :, 0:64], in0=p6[:, blk * 128:(blk + 1) * 128].rearrange("p (k d) -> p k d", k=2),
                        in1=self.bvbc[:].rearrange("p (k d) -> p k d", k=2), op=ALU.add), r=["ps6", "bvbc"], w=[("V", tt)])
                op("sp", lambda e, cols=cols: e.dma_start(out=self.kvT_d[:, cols], in_=KT[:, cols]), r=[("KT", tt)],
                   w=[("kvTd", tt)], dma="kvst")
                op("sp", lambda e, tt=tt: e.dma_start(out=self.kvV_d[:, tt * 4 * 132:(tt + 1) * 4 * 132],
                                                     in_=Vb[:, tt * 4:(tt + 1) * 4, :, :].rearrange("p a b c -> p (a b c)")),
                   r=[("V", tt)], w=[("kvVd", tt)], dma="kvst2")
            if DBG_B < 2:
                continue
            self.make_xb(tt, l, 0, xb, xbkeys, rstd1, "rstd1")
            for g in range(8):
                if g % 2 == 0:
                    slot = wqs[wqi % 2]
                    sk = "wq%d" % (wqi % 2)
                    wqi += 1

                    def ldq(e, slot=slot, g=g):
                        a = e.dma_start(out=slot[:, :, 0, :],
                                        in_=(self.wob_d if DBG_NOQDMA == 2 else self.wqb_d)[j, :, g * 64:(g + 2) * 64].rearrange("(c p) n -> p c n", p=128))
                        b_ = e.dma_start(out=slot[:, :, 1, :],
                                         in_=(self.wob_d if DBG_NOQDMA == 2 else self.wqb_d)[j, :, 512 + g * 64:512 + (g + 2) * 64].rearrange("(c p) n -> p c n", p=128))
                        return [a, b_]
                    if DBG_NOQDMA == 1:
                        op("pool", lambda e, slot=slot: e.memset(slot[:], 0.01), w=[sk])
                    else:
                        op("pool", ldq, w=[sk], dma=sk, ndma=2)
                pq = self.ps[5 + g % 2]
                pqk = "ps%d" % (5 + g % 2)

                def mmq(e, slot=slot, pq=pq, g=g):
                    last = None
                    for c in range(NC_):
                        o_ = (g % 2) * 64
                        e.matmul(pq[0:64, :], lhsT=slot[:, c, 0, o_:o_ + 64], rhs=xb[:, c, :],
                                 start=(c == 0), stop=(c == NC_ - 1), tile_position=(0, 0))
                        last = e.matmul(pq[64:128, :], lhsT=slot[:, c, 1, o_:o_ + 64], rhs=xb[:, c, :],
                                        start=(c == 0), stop=(c == NC_ - 1), tile_position=(0, 64))
                    return last
                op("pe", mmq, r=[sk] + xbkeys, w=[pqk])
                op("act", lambda e, g=g, pq=pq: e.activation(out=QT[:, g, :], in_=pq[:], func=AF.Identity, bias=self.bqT[:, j, g:g + 1]),
                   r=[pqk, "bqT"], w=[("QT", g)])
            if DBG_B < 3:
                continue
            for qb in range(4):
                gq = tt * 4 + qb
                kts = ([(0, gq - 1)] if gq > 0 else []) + [(1, gq)]
                started = set()
                for g in range(8):
                    sbanks = (0, 1) if g % 2 == 0 else (5, 6)
                    Sbs = [self.ps[sbanks[0]], self.ps[sbanks[1]]]
                    sks = ["ps%d" % sbanks[0], "ps%d" % sbanks[1]]
                    PT = PTs[g % 2]
                    pk = "bPT%d" % (g % 2)

                    def mms(e, g=g, Sbs=Sbs, qb=qb, kts=kts):
                        last = None
                        for (ki, kt) in kts:
                            for hh in range(2):
                                last = e.matmul(Sbs[hh][:, ki * 128:(ki + 1) * 128],
                                                lhsT=KT[hh * 64:(hh + 1) * 64, kt * 128:(kt + 1) * 128],
                                                rhs=QT[hh * 64:(hh + 1) * 64, g, qb * 128:(qb + 1) * 128], start=True, stop=True)
                        return last
                    op("pe", mms, r=[("KT", kt // 4) for (_, kt) in kts] + [("QT", g)], w=sks)

                    def ex(e, g=g, Sbs=Sbs, PT=PT, kts=kts):
                        last = None
                        for (ki, kt) in kts:
                            for hh in range(2):
                                hd = g + 8 * hh
                                last = e.activation(out=PT[:, ki, hh, :], in_=Sbs[hh][:, ki * 128:(ki + 1) * 128],
                                                    func=AF.Exp, bias=self.biasB[:, ki, hd:hd + 1], scale=0.125)
                        return last
                    op("act", ex, r=sks + ["biasB"], w=[pk])

                    def msk(e, PT=PT, kts=kts):
                        last = None
                        for (ki, kt) in kts:
                            m = self.mprev if ki == 0 else self.mcur
                            last = e.tensor_tensor(out=PT[:, ki, :, :], in0=PT[:, ki, :, :],
                                                   in1=m[:].unsqueeze(1).to_broadcast([128, 2, 128]), op=ALU.mult)
                        return last
                    op("dve", msk, r=[pk, "mcur", "mprev"], w=[pk])

                    if DBG_B < 4:
                        continue
                    pvl = []
                    for hh in range(2):
                        hd = g + 8 * hh
                        bk_ = obank(hd)
                        for n_, (ki, kt) in enumerate(kts):
                            st = (n_ == 0) and (bk_ not in started)
                            if n_ == 0:
                                started.add(bk_)
                            pvl.append((hh, hd, bk_, ki, kt, st, n_ == len(kts) - 1))

                    def pv(e, PT=PT, pvl=pvl):
                        last = None
                        for (hh, hd, bk_, ki, kt, st, sp_) in pvl:
                            last = e.matmul(self.ps[bk_][:, ooff(hd):ooff(hd) + 65], lhsT=PT[:, ki, hh, :],
                                            rhs=Vb[:, kt, hh, 0:65], start=st, stop=sp_, skip_group_check=True)
                        return last
                    op("pe", pv, r=[pk] + [("V", kt // 4) for (_, kt) in kts], w=["ps2", "ps3", "ps7"])
                if DBG_B < 5:
                    continue
                for bk_, h0, h1 in ((2, 0, 7), (3, 7, 14), (7, 14, 16)):
                    nh = h1 - h0
                    ov = self.ps[bk_][:, 0:nh * 65].rearrange("p (h e) -> p h e", e=65)
                    op("dve", lambda e, ov=ov, h0=h0, h1=h1: e.tensor_tensor(out=den[:, h0:h1], in0=ov[:, :, 64],
                                                                             in1=self.sinkf[:, j, h0:h1], op=ALU.add),
                       r=["ps%d" % bk_, "sinkf"], w=["den"])
                op("dve", lambda e: e.reciprocal(out=den[:], in_=den[:]), r=["den"], w=["den"])
                for bk_, h0, h1 in ((2, 0, 7), (3, 7, 14), (7, 14, 16)):
                    nh = h1 - h0
                    ov = self.ps[bk_][:, 0:nh * 65].rearrange("p (h e) -> p h e", e=65)
                    op("dve", lambda e, ov=ov, h0=h0, h1=h1, nh=nh: e.tensor_tensor(
                        out=On[:, h0:h1, :], in0=ov[:, :, 0:64], in1=den[:, h0:h1].unsqueeze(2).to_broadcast([128, nh, 64]),
                        op=ALU.mult), r=["ps%d" % bk_, "den"], w=["On"])

                def trs(e):
                    last = None
                    for c in range(NC_):
                        last = e.transpose(out=ps4b[:, c * 128:(c + 1) * 128], in_=On[:, 2 * c:2 * c + 2, :],
                                           identity=self.identb[:])
                    return last
                op("pe", trs, r=["On", "identb"], w=["ps4"])
                op("act", lambda e, qb=qb: e.copy(out=OT[:, :, qb * 128:(qb + 1) * 128],
                                                  in_=ps4b[:, 0:1024].rearrange("p (c t) -> p c t", c=8)),
                   r=["ps4"], w=[("OT", qb)])
            if DBG_B < 6:
                continue
            for half in range(2):
                for hh in range(NC_):
                    slot = wos[woi % 4]
                    sk = "wo%d" % (woi % 4)
                    woi += 1
                    op("pool", lambda e, slot=slot, hh=hh, half=half: e.dma_start(
                        out=slot[:], in_=self.wob_d[j, hh * 128:(hh + 1) * 128, half * 512:(half + 1) * 512]), w=[sk], dma=sk)

                    def mmo(e, slot=slot, hh=hh, half=half):
                        last = None
                        for o in range(4):
                            last = e.matmul(self.ps[half * 4 + o][:], lhsT=slot[:, o * 128:(o + 1) * 128], rhs=OT[:, hh, :],
                                            start=(hh == 0), stop=(hh == NC_ - 1))
                        return last
                    op("pe", mmo, r=[sk] + [("OT", q_) for q_ in range(4)], w=["ps%d" % (half * 4 + o) for o in range(4)])
                if half == 0:
                    for o in range(4):
                        op("act", lambda e, o=o: e.copy(out=fTa[:, o, :], in_=self.ps[o][:]), r=["ps%d" % o],
                           w=[("QT", 2 * o), ("QT", 2 * o + 1)])
            srcs = [(fTa[:, o, :], [("QT", 2 * o), ("QT", 2 * o + 1)]) for o in range(4)] + \
                   [(self.ps[4 + o][:], "ps%d" % (4 + o)) for o in range(4)]
            self.postnorm(tt, l, 1, srcs, sqh, rstd1, 0, rkey="rstd1")


    def dbgq(self, stack, mode):
        op = self.op
        A = lambda n, s, d: self.alloc(stack, n, s, d)
        slot = A("dq_slot", [128, NC_, 2, 128], BF16)
        slotf = A("dq_slotf", [128, NC_, 128], F32)
        if mode == 0:
            op("pool", lambda e: e.dma_start(out=slot[:, :, 0, :], in_=self.wqb_d[0, :, 0:128].rearrange("(c p) n -> p c n", p=128)),
               w=["dq"], dma="dq")
        elif mode == 1:
            op("sp", lambda e: e.dma_start(out=slotf[:], in_=self.wqb_d[0, :, 0:128].rearrange("(c p) n -> p c n", p=128)),
               w=["dq"], dma="dq")
        elif mode == 2:
            op("sp", lambda e: e.dma_start(out=slotf[:, 0, :], in_=self.wqb_d[0, 0:128, 0:128]), w=["dq"], dma="dq")
        elif mode == 3:
            op("sp", lambda e: e.dma_start(out=slotf[:, 0, :], in_=self.wqb_d[1, 0:128, 0:128]), w=["dq"], dma="dq")

def build_program(plan):
    from contextlib import ExitStack
    nc = bass.Bass("TRN2", target_bir_lowering=False)
    b = Builder(nc, plan)
    b.declare()
    with ExitStack() as stack:
        b.alloc_perm(stack)
        with ExitStack() as tmp:
            b.setup(stack, tmp)
            b.setup_A(stack, tmp)
            b.setup_B(stack, tmp)
        b.sc.barrier()
        with ExitStack() as s2:
            b.load_x(s2)
        for item in plan:
            b.sc.barrier()
            b.uid += 1
            with ExitStack() as s2:
                getattr(b, item[0])(s2, *item[1:])
        b.sc.barrier()
        b.uid += 1
        with ExitStack() as s2:
            b.store_x(s2)
        b.sc.emit(nc, stack)
    return nc


FULL_PLAN = [("attnA", 0), ("woA", 0), ("ffn", 0), ("attnA", 1), ("woA", 1), ("ffn", 1),
             ("attnB", 2), ("ffn", 2), ("attnB", 3), ("ffn", 3)]

_IN_ORDER = ["x", "norm_gains", "w_qkv_a", "lambda_qk_a", "subln_a", "w_o_a", "kv_norm", "w_kv_b", "b_kv_b",
             "w_q_b", "b_q_b", "sinks_b", "w_o_b", "w_up", "conv_w", "conv_b", "w_down"]
_SHAPES = {
    "norm_gains": (DEPTH * 4 * NC_, 128), "lambda_qk_a": (1, NA * 4 * HD), "subln_a": (1, NA * 128),
    "kv_norm": (NC_, 128), "b_kv_b": (2, 128), "b_q_b": (2, 2, 8, HD), "sinks_b": (1, 32),
    "conv_w": (DEPTH * 3 * NFF, 128), "conv_b": (DEPTH * NFF, 128),
}


def run(inputs, plan, n_cores=8, trace=False):
    nc = build_program(plan)
    shared = {}
    for k in _IN_ORDER:
        if k == "x":
            continue
        a = np.ascontiguousarray(np.asarray(inputs[k], dtype=np.float32))
        if k in _SHAPES:
            a = a.reshape(_SHAPES[k])
        shared[k] = a
    x = np.asarray(inputs["x"], dtype=np.float32)
    in_maps = []
    for c in range(n_cores):
        m = dict(shared)
        m["x"] = np.ascontiguousarray(x[c])
        in_maps.append(m)
    res = run_bass_kernel_spmd(nc, in_maps, core_ids=list(range(n_cores)), trace=trace)
    out = np.stack([np.asarray(r["out"]) for r in res.results], axis=0)
    return out.astype(np.float32), res


def kernel(**inputs):
    out, _ = run(inputs, FULL_PLAN)
    return out
```

```python
import math
import numpy as np
import concourse.bass as bass
import concourse.mybir as mybir
from concourse.bass_utils import run_bass_kernel_spmd

F32 = mybir.dt.float32
BF16 = mybir.dt.bfloat16
AF = mybir.ActivationFunctionType
ALU = mybir.AluOpType
AX = mybir.AxisListType

DEBUG_OT = False
DBG_B = 99
DBG_NOQDMA = False
D = 1024
S = 4096
DEPTH = 4
NA = 2
DFF = 2816
NFF = DFF // 128
HD = 64
NC_ = 8
T = 512
NT = S // T
NB = S // 128
NORM_EPS = 1e-6
SUBLN_EPS = 1e-5


def _ap(t):
    try:
        return t[:]
    except Exception:
        return t


def alibi_slopes(n):
    return [2.0 ** (-8.0 * (i + 1) / n) for i in range(n)]


class _Op:
    __slots__ = ("eng", "fn", "deps", "dma", "idx", "flag", "done_sem", "done_val", "ndma")


class Sched:
    def __init__(self):
        self.ops = []
        self.last_w = {}
        self.readers = {}
        self.last_dma = {}
        self.barrier_idx = None

    def barrier(self):
        last = {}
        for op in self.ops:
            last[(op.eng, op.dma)] = op.idx
        self.bar_deps = set(last.values())

    def add(self, eng, fn, r=(), w=(), dma=None, ndma=1, insure=True):
        op = _Op()
        op.eng, op.fn, op.dma, op.idx, op.flag, op.ndma = eng, fn, dma, len(self.ops), False, ndma
        deps = set(getattr(self, "bar_deps", ()))
        for k in r:
            if k in self.last_w:
                deps.add(self.last_w[k])
        for k in w:
            if k in self.last_w:
                deps.add(self.last_w[k])
            deps.update(self.readers.get(k, ()))
        if insure and dma is not None and dma in self.last_dma:
            deps.add(self.last_dma[dma])
        for k in r:
            self.readers.setdefault(k, []).append(op.idx)
        for k in w:
            self.last_w[k] = op.idx
            self.readers[k] = []
        if dma is not None:
            self.last_dma[dma] = op.idx
        deps.discard(op.idx)
        op.deps = deps
        self.ops.append(op)
        return op.idx

    def emit(self, nc, stack):
        ops = self.ops
        for op in ops:
            for d in op.deps:
                dd = ops[d]
                if dd.eng == "pe" and op.eng == "pe" and dd.dma is None and op.dma is None:
                    continue
                dd.flag = True
        last_of = {}
        for op in ops:
            last_of[(op.eng, op.dma is not None)] = op
        for op in last_of.values():
            op.flag = True
        cnt = {}
        for e in ("pe", "act", "dve", "pool"):
            cnt[e] = stack.enter_context(nc.semaphore("cnt_" + e))
        dkeys = []
        for op in ops:
            if op.dma is not None and op.dma not in dkeys:
                dkeys.append(op.dma)
        dsem = {k: stack.enter_context(nc.semaphore("dma_%d" % i)) for i, k in enumerate(dkeys)}
        seq = {e: 0 for e in cnt}
        dcount = {k: 0 for k in dkeys}
        for op in ops:
            if op.dma is not None:
                dcount[op.dma] += 16 * op.ndma
                op.done_sem, op.done_val = dsem[op.dma], dcount[op.dma]
            elif op.flag:
                seq[op.eng] += 1
                op.done_sem, op.done_val = cnt[op.eng], seq[op.eng]
            else:
                op.done_sem, op.done_val = None, None
        per_eng = {e: [] for e in ("pe", "act", "dve", "pool", "sp")}
        for op in ops:
            per_eng[op.eng].append(op)
        final_waits = [(op.done_sem, op.done_val) for op in last_of.values()]

        def run(engname, eng):
            waited = {}
            for op in per_eng[engname]:
                for d in sorted(op.deps):
                    dd = ops[d]
                    if dd.done_sem is None:
                        continue
                    if dd.eng == "pe" and engname == "pe" and dd.dma is None and op.dma is None:
                        continue
                    key = id(dd.done_sem)
                    if waited.get(key, (None, 0))[1] < dd.done_val:
                        eng.wait_ge(dd.done_sem, dd.done_val)
                        waited[key] = (dd.done_sem, dd.done_val)
                res = op.fn(eng)
                if op.dma is not None:
                    lst = res if isinstance(res, (list, tuple)) else [res]
                    assert len(lst) == op.ndma, (len(lst), op.ndma)
                    for ins in lst:
                        ins.then_inc(op.done_sem, 16)
                elif op.flag:
                    last = res[-1] if isinstance(res, (list, tuple)) else res
                    last.then_inc(op.done_sem, 1)
            if engname == "sp":
                for sem, val in final_waits:
                    eng.wait_ge(sem, val)

        with nc.Block() as block:
            @block.tensor
            def _(e):
                run("pe", e)

            @block.scalar
            def _(e):
                run("act", e)

            @block.vector
            def _(e):
                run("dve", e)

            @block.gpsimd
            def _(e):
                run("pool", e)

            @block.sync
            def _(e):
                run("sp", e)


class Builder:
    def __init__(self, nc, plan):
        self.nc = nc
        self.plan = plan
        self.sc = Sched()
        self.uid = 0
        self.pc_list = None

    def op(self, eng, fn, r=(), w=(), dma=None, ndma=1, insure=True):
        return self.sc.add(eng, fn, r, w, dma, ndma, insure)

    def barrier_keys(self):
        return list(self.sc.last_w.keys())

    def declare(self):
        nc = self.nc
        dt = nc.dram_tensor
        self.x_d = dt("x", [S, D], F32, kind="ExternalInput").ap()
        self.out_d = dt("out", [S, D], F32, kind="ExternalOutput").ap()
        self.ng_d = dt("norm_gains", [DEPTH * 4 * NC_, 128], F32, kind="ExternalInput").ap()
        self.wqkv_d = dt("w_qkv_a", [NA, D, 3 * D], F32, kind="ExternalInput").ap()
        self.lam_d = dt("lambda_qk_a", [1, NA * 4 * HD], F32, kind="ExternalInput").ap()
        self.subln_d = dt("subln_a", [1, NA * 128], F32, kind="ExternalInput").ap()
        self.woa_d = dt("w_o_a", [NA, D, D], F32, kind="ExternalInput").ap()
        self.kvn_d = dt("kv_norm", [NC_, 128], F32, kind="ExternalInput").ap()
        self.wkv_d = dt("w_kv_b", [D, 256], F32, kind="ExternalInput").ap()
        self.bkv_d = dt("b_kv_b", [2, 128], F32, kind="ExternalInput").ap()
        self.wqb_d = dt("w_q_b", [2, D, D], F32, kind="ExternalInput").ap()
        self.bqb_d = dt("b_q_b", [2, 2, 8, HD], F32, kind="ExternalInput").ap()
        self.sinks_d = dt("sinks_b", [1, 32], F32, kind="ExternalInput").ap()
        self.wob_d = dt("w_o_b", [2, D, D], F32, kind="ExternalInput").ap()
        self.wup_d = dt("w_up", [DEPTH, D, 2 * DFF], F32, kind="ExternalInput").ap()
        self.convw_d = dt("conv_w", [DEPTH * 3 * NFF, 128], F32, kind="ExternalInput").ap()
        self.convb_d = dt("conv_b", [DEPTH * NFF, 128], F32, kind="ExternalInput").ap()
        self.wdn_d = dt("w_down", [DEPTH, DFF, D], F32, kind="ExternalInput").ap()
        self.wupT_d = dt("wupT_scratch", [DEPTH, NFF // 2, 128, NC_ * 512], BF16).ap()
        self.wdnT_d = dt("wdnT_scratch", [DEPTH, 2, NFF, 128, 512], BF16).ap()
        self.kvT_d = dt("kvT_scratch", [128, S], BF16).ap()
        self.kvV_d = dt("kvV_scratch", [128, NB * 2 * 66], BF16).ap()
        self.ot_d = (dt("ot_scratch", [8, 128, S], BF16, kind="ExternalOutput") if DEBUG_OT else dt("ot_scratch", [8, 128, S], BF16)).ap()


    def alloc(self, stack, name, shape, dtype):
        return stack.enter_context(self.nc.sbuf_tensor("%s_p%d" % (name, self.uid), shape, dtype))

    def palloc(self, stack, name, shape, dtype=F32):
        return stack.enter_context(self.nc.psum_tensor(name, shape, dtype))

    def alloc_perm(self, stack):
        A = lambda n, s, d: self.alloc(stack, n, s, d)
        self.xT = A("xT", [128, NC_, S], F32)
        self.onesf = A("onesf", [128, 128], F32)
        self.identf = A("identf", [128, 128], F32)
        self.identb = A("identb", [128, 128], BF16)
        self.onesb = A("onesb", [128, 128], BF16)
        self.mcur = A("mcur", [128, 128], BF16)
        self.mprev = A("mprev", [128, 128], BF16)
        self.gT = A("gT", [128, 128], F32)
        self.kvg = A("kvg", [128, NC_], F32)
        self.convw = A("convw", [128, DEPTH * 3 * NFF], F32)
        self.convb = A("convb", [128, DEPTH * NFF], F32)
        self.epst = A("epst", [128, 2], F32)
        self.psall = self.palloc(stack, "psall", [128, 8, 512], F32)
        self.ps = [self.psall[:, i, :] for i in range(8)]
        self.tabA = A("tabA", [128, 8, 35], F32)
        self.sgbc = A("sgbc", [128, NA, 128], F32)
        self.nlam = A("nlam", [128, 2], F32)
        self.bqT = A("bqT", [128, 2, 8], F32)
        self.bkc = A("bkc", [128, 2], F32)
        self.bvbc = A("bvbc", [128, 128], F32)
        self.biasB = A("biasB", [128, 2, 16], F32)
        self.sinkf = A("sinkf", [128, 2, 16], F32)

    def setup(self, stack, tmp):
        nc = self.nc
        A = lambda n, s, d: self.alloc(stack, n, s, d)
        TA = lambda n, s, d: self.alloc(tmp, n, s, d)
        self.pstage = TA("pstage", [128, 128], F32)

        op = self.op
        op("pool", lambda e: e.memset(self.onesf[:], 1.0), w=["onesf"])
        op("pool", lambda e: e.memset(self.onesb[:], 1.0), w=["onesb"])
        op("pool", lambda e: e.memset(self.epst[:, 0:1], NORM_EPS), w=["epsc"])
        op("pool", lambda e: e.memset(self.epst[:, 1:2], SUBLN_EPS), w=["epsc"])
        op("pool", lambda e: e.affine_select(out=self.identf[:], in_=self.onesf[:], pattern=[[-1, 128]],
                                             compare_op=ALU.is_equal, fill=0.0, base=0, channel_multiplier=1),
           r=["onesf"], w=["identf"])
        op("pool", lambda e: e.affine_select(out=self.mcur[:], in_=self.onesb[:], pattern=[[1, 128]],
                                             compare_op=ALU.is_ge, fill=0.0, base=0, channel_multiplier=-1),
           r=["onesb"], w=["mcur"])
        op("pool", lambda e: e.affine_select(out=self.mprev[:], in_=self.onesb[:], pattern=[[-1, 128]],
                                             compare_op=ALU.is_gt, fill=0.0, base=0, channel_multiplier=1),
           r=["onesb"], w=["mprev"])
        op("dve", lambda e: e.tensor_copy(out=self.identb[:], in_=self.identf[:]), r=["identf"], w=["identb"])
        self.load_cols(self.ng_d, 128, self.gT, 0)
        self.load_cols(self.kvn_d, NC_, self.kvg, 0)
        for i0 in range(0, DEPTH * 3 * NFF, 128):
            n = min(128, DEPTH * 3 * NFF - i0)
            self.load_cols(self.convw_d[i0:i0 + n, :], n, self.convw, i0)
        self.load_cols(self.convb_d, DEPTH * NFF, self.convb, 0)

    def _colkey(self, dst):
        try:
            return dst.name
        except Exception:
            return "anon"

    def load_cols(self, rows_ap, R, dst, col0, st_view=None):
        op = self.op
        st, ps = self.pstage, self.ps[0]
        if st_view is None:
            op("sp", lambda e: e.dma_start(out=st[0:R, :], in_=rows_ap), w=["pstage"], dma="pstage")
        else:
            op("sp", lambda e: e.dma_start(out=st[0:R, :].rearrange("r (a b) -> r a b", a=st_view[1]), in_=rows_ap),
               w=["pstage"], dma="pstage")
        op("pe", lambda e: e.transpose(out=ps[:, 0:R], in_=st[0:R, :], identity=self.identf[0:R, 0:R]),
           r=["pstage", "identf"], w=["ps0"])
        op("dve", lambda e: e.tensor_copy(out=dst[:, col0:col0 + R], in_=ps[:, 0:R]), r=["ps0"], w=[("cols", self._colkey(dst))])

    def load_x(self, stack):
        op = self.op
        xin = [self.alloc(stack, "xin%d" % i, [128, D], F32) for i in range(2)]
        for tb in range(NB):
            st = xin[tb % 2]
            sk = "xin%d" % (tb % 2)
            op("sp", lambda e, st=st, tb=tb: e.dma_start(out=st[:], in_=self.x_d[tb * 128:(tb + 1) * 128, :]),
               w=[sk], dma=sk)
            for half in range(2):
                ps = self.ps[2 * (tb % 2) + half]
                pk = "ps%d" % (2 * (tb % 2) + half)

                def tr(e, st=st, ps=ps, half=half):
                    last = None
                    for i in range(4):
                        c = half * 4 + i
                        last = e.transpose(out=ps[:, i * 128:(i + 1) * 128], in_=st[:, c * 128:(c + 1) * 128],
                                           identity=self.identf[:])
                    return last
                op("pe", tr, r=[sk, "identf"], w=[pk])
                dst = self.xT[:, half * 4:half * 4 + 4, tb * 128:(tb + 1) * 128]
                eng = "act" if half == 0 else "dve"

                def ev(e, ps=ps, dst=dst, eng=eng):
                    src = ps[:].rearrange("p (c t) -> p c t", c=4)
                    if eng == "act":
                        return e.copy(out=dst, in_=src)
                    return e.tensor_copy(out=dst, in_=src)
                op(eng, ev, r=[pk], w=[("xT", tb // 4)])

    def store_x(self, stack):
        op = self.op
        xo = [self.alloc(stack, "xout%d" % i, [128, D], F32) for i in range(2)]
        for tb in range(NB):
            st = xo[tb % 2]
            sk = "xout%d" % (tb % 2)
            for half in range(2):
                ps = self.ps[2 * (tb % 2) + half]
                pk = "ps%d" % (2 * (tb % 2) + half)

                def tr(e, ps=ps, half=half, tb=tb):
                    last = None
                    for i in range(4):
                        c = half * 4 + i
                        last = e.transpose(out=ps[:, i * 128:(i + 1) * 128],
                                           in_=self.xT[:, c, tb * 128:(tb + 1) * 128], identity=self.identf[:])
                    return last
                op("pe", tr, r=[("xT", tb // 4), "identf"], w=[pk])
                eng = "act" if half == 0 else "dve"

                def ev(e, ps=ps, st=st, half=half, eng=eng):
                    if eng == "act":
                        return e.copy(out=st[:, half * 512:(half + 1) * 512], in_=ps[:])
                    return e.tensor_copy(out=st[:, half * 512:(half + 1) * 512], in_=ps[:])
                op(eng, ev, r=[pk], w=[(sk, half)])
            op("sp", lambda e, st=st, tb=tb: e.dma_start(out=self.out_d[tb * 128:(tb + 1) * 128, :], in_=st[:]),
               r=[(sk, 0), (sk, 1)], dma=sk)


    def gcol(self, l, j, c):
        k = l * 32 + j * 8 + c
        return self.gT[:, k:k + 1]

    def rstd_from_ps(self, psb, pkey, rstd, rkey, n, eps):
        self.op("act", lambda e: e.activation(out=_ap(rstd), in_=psb[:], func=AF.Sqrt, bias=self.epsc(eps), scale=1.0 / n),
                r=[pkey, "epsc"], w=[rkey])
        self.op("dve", lambda e: e.reciprocal(out=_ap(rstd), in_=_ap(rstd)), r=[rkey], w=[rkey])

    def epsc(self, eps):
        return self.epst[:, 0:1] if eps == NORM_EPS else self.epst[:, 1:2]

    def stats_xT(self, tt, sqh, psi, rstd, rkey):
        op = self.op
        cols = slice(tt * T, (tt + 1) * T)
        psb, pkey = self.ps[psi], "ps%d" % psi
        for hf in range(2):
            op("act", lambda e, hf=hf: e.activation(out=sqh[:], in_=self.xT[:, hf * 4:hf * 4 + 4, cols], func=AF.Square),
               r=[("xT", tt)], w=[("sqh", i) for i in range(4)])

            def mm(e, hf=hf):
                last = None
                for i in range(4):
                    last = e.matmul(psb[:], lhsT=self.onesb[:], rhs=sqh[:, i, :], start=(hf == 0 and i == 0),
                                    stop=(hf == 1 and i == 3))
                return last
            op("pe", mm, r=[("sqh", i) for i in range(4)] + ["onesb"], w=[pkey])
        self.rstd_from_ps(psb, pkey, rstd, rkey, float(D), NORM_EPS)

    def make_xb(self, tt, l, j, xb, xbkeys, rstd, rkey, eng="dve", gains=None):
        cols = slice(tt * T, (tt + 1) * T)
        for c in range(NC_):
            g = gains[:, c:c + 1] if gains is not None else self.gcol(l, j, c)
            self.op(eng, lambda e, c=c, g=g: e.scalar_tensor_tensor(out=xb[:, c, :], in0=self.xT[:, c, cols], scalar=g,
                                                                     in1=_ap(rstd), op0=ALU.mult, op1=ALU.mult),
                    r=[("xT", tt), rkey, "gT"], w=[xbkeys[c]])

    def postnorm(self, tt, l, j, srcs, sqh, rstd2, psi, rkey="rstd2"):
        op = self.op
        cols = slice(tt * T, (tt + 1) * T)
        psb, pkey = self.ps[psi], "ps%d" % psi
        for hf in range(2):
            for i in range(4):
                ap, key = srcs[hf * 4 + i]
                key = key if isinstance(key, list) else [key]
                op("act", lambda e, ap=ap, i=i: e.activation(out=sqh[:, i, :], in_=ap, func=AF.Square),
                   r=key, w=[("sqh", i)])

            def mm(e, hf=hf):
                last = None
                for i in range(4):
                    last = e.matmul(psb[:], lhsT=self.onesb[:], rhs=sqh[:, i, :], start=(hf == 0 and i == 0),
                                    stop=(hf == 1 and i == 3))
                return last
            op("pe", mm, r=[("sqh", i) for i in range(4)] + ["onesb"], w=[pkey] + [("sqh", i) for i in range(4)])
        self.rstd_from_ps(psb, pkey, rstd2, rkey, float(D), NORM_EPS)
        for c in range(NC_):
            ap, key = srcs[c]
            key = key if isinstance(key, list) else [key]
            op("dve", lambda e, ap=ap, c=c: e.scalar_tensor_tensor(out=self.ptmp[:], in0=ap, scalar=self.gcol(l, j, c),
                                                                    in1=rstd2[:], op0=ALU.mult, op1=ALU.mult),
               r=key + [rkey, "gT"], w=["ptmp"])
            op("dve", lambda e, c=c: e.tensor_tensor(out=self.xT[:, c, cols], in0=self.xT[:, c, cols], in1=self.ptmp[:],
                                                     op=ALU.add),
               r=["ptmp", ("xT", tt)], w=[("xT", tt)])

    def precast_ops(self):
        ops = []
        for l in range(DEPTH):
            for jj in range(NFF // 2):
                def f(l=l, jj=jj):
                    def dm(e):
                        dst = self.wupT_d[l, jj].rearrange("p (c n) -> p c n", c=NC_)
                        a = e.dma_start(out=dst[:, :, 0:256],
                                        in_=self.wup_d[l, :, jj * 256:(jj + 1) * 256].rearrange("(c p) n -> p c n", p=128))
                        b_ = e.dma_start(out=dst[:, :, 256:512],
                                         in_=self.wup_d[l, :, DFF + jj * 256:DFF + (jj + 1) * 256].rearrange("(c p) n -> p c n", p=128))
                        return [a, b_]
                    self.op("pool", dm, w=[("pc", l, "u", jj)], dma="pc%d" % l, ndma=2, insure=False)
                ops.append(f)
            for half in range(2):
                def f(l=l, half=half):
                    last = (half == 1)
                    self.op("pool", lambda e: e.dma_start(
                        out=self.wdnT_d[l, half],
                        in_=self.wdn_d[l, :, half * 512:(half + 1) * 512].rearrange("(j p) n -> j p n", p=128)),
                        w=[("pc", l, "d", half)] + ([("pcdone", l)] if last else []), dma="pc%d" % l, insure=False)
                ops.append(f)
        return ops

    def precast_take(self, k):
        if self.pc_list is None:
            self.pc_list = self.precast_ops()
        for _ in range(k):
            if self.pc_list:
                self.pc_list.pop(0)()

    def ffn(self, stack, l):
        op = self.op
        self.precast_take(10 ** 6)
        A = lambda n, s, d: self.alloc(stack, n, s, d)
        xb = A("f_xb", [128, NC_, T], BF16)
        sqh = A("f_sqh", [128, 4, T], BF16)
        fTa = A("f_fTa", [128, 4, T], F32)
        aT = A("f_aT", [128, NFF, T], BF16)
        wu = [A("f_wu%d" % i, [128, NC_, 512], BF16) for i in range(2)]
        wd = [A("f_wd%d" % i, [128, 512], BF16) for i in range(4)]
        Cb = [A("f_C%d" % i, [128, T], F32) for i in range(2)]
        rstd1 = A("f_rstd1", [128, T], F32)
        rstd2 = A("f_rstd2", [128, T], F32)
        self.ptmp = A("f_ptmp", [128, T], F32)
        halo = A("f_halo", [128, NFF, 2], F32)
        xbkeys = [("xb", c) for c in range(NC_)]
        wcol = lambda k, j: self.convw[:, l * 66 + k * 22 + j: l * 66 + k * 22 + j + 1]
        bcol = lambda j: self.convb[:, l * 22 + j: l * 22 + j + 1]
        wdi = 0
        self.stats_xT(0, sqh, 4, rstd1, "rstd1")
        self.make_xb(0, l, 2, xb, xbkeys, rstd1, "rstd1")
        for tt in range(NT):
            cols = slice(tt * T, (tt + 1) * T)
            for jj in range(NFF // 2):
                slot = wu[jj % 2]
                sk = "wu%d" % (jj % 2)
                op("sp", lambda e, slot=slot, jj=jj: e.dma_start(out=slot[:].rearrange("p c n -> p (c n)"), in_=self.wupT_d[l, jj]),
                   r=[("pcdone", l)], w=[sk], dma=sk)
                for s in range(2):
                    j = 2 * jj + s
                    gi, ui = j % 2, 2 + j % 2
                    G, U = self.ps[gi], self.ps[ui]
                    gk, uk = "ps%d" % gi, "ps%d" % ui
                    C = Cb[j % 2]
                    ck = "C%d" % (j % 2)

                    def mmg(e, slot=slot, s=s, G=G):
                        last = None
                        for c in range(NC_):
                            last = e.matmul(G[:], lhsT=slot[:, c, s * 128:(s + 1) * 128], rhs=xb[:, c, :],
                                            start=(c == 0), stop=(c == NC_ - 1))
                        return last

                    def mmu(e, slot=slot, s=s, U=U):
                        last = None
                        for c in range(NC_):
                            last = e.matmul(U[:], lhsT=slot[:, c, 256 + s * 128:256 + (s + 1) * 128], rhs=xb[:, c, :],
                                            start=(c == 0), stop=(c == NC_ - 1))
                        return last
                    op("pe", mmg, r=[sk] + xbkeys, w=[gk])
                    op("pe", mmu, r=[sk] + xbkeys, w=[uk])
                    op("act", lambda e, C=C, G=G, j=j: e.activation(out=C[:], in_=G[:], func=AF.Identity,
                                                                    bias=bcol(j), scale=wcol(2, j)),
                       r=[gk, "convw", "convb"], w=[ck])
                    op("dve", lambda e, C=C, G=G, j=j: e.scalar_tensor_tensor(out=C[:, 1:T], in0=G[:, 0:T - 1], scalar=wcol(1, j),
                                                                              in1=C[:, 1:T], op0=ALU.mult, op1=ALU.add),
                       r=[gk, ck, "convw"], w=[ck])
                    op("dve", lambda e, C=C, G=G, j=j: e.scalar_tensor_tensor(out=C[:, 2:T], in0=G[:, 0:T - 2], scalar=wcol(0, j),
                                                                              in1=C[:, 2:T], op0=ALU.mult, op1=ALU.add),
                       r=[gk, ck, "convw"], w=[ck])
                    if tt > 0:
                        op("dve", lambda e, C=C, j=j: e.scalar_tensor_tensor(out=C[:, 0:2], in0=halo[:, j, 0:2], scalar=wcol(0, j),
                                                                             in1=C[:, 0:2], op0=ALU.mult, op1=ALU.add),
                           r=[("halo", j), ck, "convw"], w=[ck])
                        op("dve", lambda e, C=C, j=j: e.scalar_tensor_tensor(out=C[:, 0:1], in0=halo[:, j, 1:2], scalar=wcol(1, j),
                                                                             in1=C[:, 0:1], op0=ALU.mult, op1=ALU.add),
                           r=[("halo", j), ck, "convw"], w=[ck])
                    if tt < NT - 1:
                        op("dve", lambda e, G=G, j=j: e.tensor_copy(out=halo[:, j, :], in_=G[:, T - 2:T]),
                           r=[gk], w=[("halo", j)])
                    op("act", lambda e, C=C: e.activation(out=C[:], in_=C[:], func=AF.Gelu_apprx_tanh), r=[ck], w=[ck])
                    op("dve", lambda e, C=C, U=U, j=j: e.tensor_tensor(out=aT[:, j, :], in0=C[:], in1=U[:], op=ALU.mult),
                       r=[ck, uk], w=[("aT", j)])
            if tt + 1 < NT:
                self.stats_xT(tt + 1, sqh, 4, rstd1, "rstd1")
                self.make_xb(tt + 1, l, 2, xb, xbkeys, rstd1, "rstd1")
            for half in range(2):
                for j in range(NFF):
                    slot = wd[wdi % 4]
                    sk = "wd%d" % (wdi % 4)
                    wdi += 1
                    op("sp", lambda e, slot=slot, j=j, half=half: e.dma_start(out=slot[:], in_=self.wdnT_d[l, half, j]),
                       r=[("pcdone", l)], w=[sk], dma=sk)

                    def mmd(e, slot=slot, j=j, half=half):
                        last = None
                        for o in range(4):
                            last = e.matmul(self.ps[half * 4 + o][:], lhsT=slot[:, o * 128:(o + 1) * 128], rhs=aT[:, j, :],
                                            start=(j == 0), stop=(j == NFF - 1))
                        return last
                    op("pe", mmd, r=[sk, ("aT", j)], w=["ps%d" % (half * 4 + o) for o in range(4)])
                if half == 0:
                    for o in range(4):
                        op("act", lambda e, o=o: e.copy(out=fTa[:, o, :], in_=self.ps[o][:]), r=["ps%d" % o], w=[("fTa", o)])
            srcs = [(fTa[:, o, :], ("fTa", o)) for o in range(4)] + [(self.ps[4 + o][:], "ps%d" % (4 + o)) for o in range(4)]
            self.postnorm(tt, l, 3, srcs, sqh, rstd2, 0)

    def bcast_row(self, row_ap, n, dst_ap, dkey, rkey, scale=None):
        ps = self.ps[1]
        self.op("pe", lambda e: e.matmul(ps[:, 0:n], lhsT=self.onesf[0:1, :], rhs=row_ap, start=True, stop=True),
                r=[rkey, "onesf"], w=["ps1"])
        if scale is None:
            self.op("dve", lambda e: e.tensor_copy(out=dst_ap, in_=ps[:, 0:n]), r=["ps1"], w=[dkey])
        else:
            self.op("act", lambda e: e.activation(out=dst_ap, in_=ps[:, 0:n], func=AF.Copy, scale=float(scale)),
                    r=["ps1"], w=[dkey])

    def setup_A(self, stack, tmp):
        op = self.op
        A = lambda n, s, d: self.alloc(stack, n, s, d)
        TA = lambda n, s, d: self.alloc(tmp, n, s, d)
        ii = TA("iotaI", [128, 35], mybir.dt.int32)
        fi = TA("iotaF", [128, 35], F32)
        lamr = TA("lamr", [1, 2, 2, 2, HD], F32)
        prod = TA("lamprod", [1, 4, HD], F32)
        red = TA("lamred", [1, 4], F32)
        lrow = TA("lamrow", [1, 2], F32)
        sgrow = TA("sgrow", [1, NA * 128], F32)
        op("pool", lambda e: e.iota(ii[:], pattern=[[-128, 35]], base=384, channel_multiplier=1), w=["iotaI"])
        op("dve", lambda e: e.tensor_copy(out=fi[:], in_=ii[:]), r=["iotaI"], w=["iotaF"])
        sl = alibi_slopes(8)
        for h in range(8):
            W = self.widthA(h)
            op("dve", lambda e, h=h, W=W: e.tensor_scalar(out=self.tabA[:, h, :], in0=fi[:], scalar1=float(W // 2),
                                                          scalar2=float(sl[h]), op0=ALU.subtract, op1=ALU.mult),
               r=["iotaF"], w=["tabA"])
        op("sp", lambda e: e.dma_start(out=lamr[:].rearrange("o l a b d -> o (l a b d)"), in_=self.lam_d), w=["lamr"], dma="lamr")
        op("dve", lambda e: e.tensor_tensor(out=prod[:].rearrange("o (l a) d -> o l a d", l=2), in0=lamr[:, :, :, 0, :],
                                            in1=lamr[:, :, :, 1, :], op=ALU.mult), r=["lamr"], w=["lamprod"])
        op("dve", lambda e: e.reduce_sum(out=red[:], in_=prod[:], axis=AX.X), r=["lamprod"], w=["lamred"])
        op("act", lambda e: e.activation(out=red[:], in_=red[:], func=AF.Exp), r=["lamred"], w=["lamred"])
        rv = red[:].rearrange("o (l a) -> o l a", l=2)
        op("dve", lambda e: e.tensor_tensor(out=lrow[:], in0=rv[:, :, 1], in1=rv[:, :, 0], op=ALU.subtract),
           r=["lamred"], w=["lamrow"])
        for l in range(NA):
            li = 0.8 - 0.6 * math.exp(-0.3 * l)
            op("dve", lambda e, l=l, li=li: e.tensor_scalar(out=lrow[:, l:l + 1], in0=lrow[:, l:l + 1], scalar1=float(li),
                                                            scalar2=None, op0=ALU.subtract), r=["lamrow"], w=["lamrow"])
        self.bcast_row(lrow[:], 2, self.nlam[:], "nlam", "lamrow")
        op("sp", lambda e: e.dma_start(out=sgrow[:], in_=self.subln_d), w=["sgrow"], dma="sgrow")
        for l in range(NA):
            li = 0.8 - 0.6 * math.exp(-0.3 * l)
            self.bcast_row(sgrow[:, l * 128:(l + 1) * 128], 128, self.sgbc[:, l, :], ("sgbc", l), "sgrow", scale=1.0 - li)

    def widthA(self, h):
        s = alibi_slopes(8)[h]
        return int(min(512, max(128, 64.0 / s)))

    def attnA(self, stack, l):
        op = self.op
        A = lambda n, s, d: self.alloc(stack, n, s, d)
        rstdA = A("a_rstd", [128, S], F32)
        sqh = A("a_sqh", [128, 4, T], BF16)
        xbs = [A("a_xb%d" % i, [128, NC_, T], BF16) for i in range(2)]
        wqkv = A("a_wqkv", [128, NC_, 384], BF16)
        KT = A("a_KT", [128, S], BF16)
        Va = A("a_V", [128, NB, 130], BF16)
        QTs = [A("a_QT%d" % i, [128, T], BF16) for i in range(2)]
        PTs = [A("a_PT%d" % i, [128, 2, T], BF16) for i in range(2)]
        Osb = A("a_Osb", [128, 1032], F32)
        rz = A("a_rz", [128, 2, 4], F32)
        ssq = A("a_ssq", [128, 2, 4], F32)
        on = A("a_on", [128, 4, 128], BF16)
        OTst = [A("a_OT%d" % i, [128, T], BF16) for i in range(2)]
        ps5b = self.ps[5][:].bitcast(BF16)
        scr = sqh[:].rearrange("p a t -> p (a t)").bitcast(F32)
        t0 = scr[:, 0:512].rearrange("p (b e) -> p b e", b=4)
        osb = scr[:, 512:1024].rearrange("p (b e) -> p b e", b=4)
        Ov = Osb[:].rearrange("p (s b e) -> p s b e", s=2, b=4)

        op("pool", lambda e: e.memset(Va[:, :, 128:130], 1.0), w=[("V", t_) for t_ in range(NT)])
        for tt in range(NT):
            self.stats_xT(tt, sqh, 5, rstdA[:, tt * T:(tt + 1) * T], ("rstdA", tt))
        p5 = self.ps[5]
        seq = [(h, tt) for h in range(8) for tt in range(NT)]

        def ldw(h):
            def f(e):
                out = []
                for i in range(3):
                    out.append(e.dma_start(out=wqkv[:, :, i * 128:(i + 1) * 128],
                                           in_=self.wqkv_d[l, :, i * D + h * 128:i * D + (h + 1) * 128].rearrange("(c p) n -> p c n", p=128)))
                return out
            op("pool", f, w=["wqkv"], dma="wqkv", ndma=3)
            self.precast_take(7)

        def emit_xb(n):
            h, tt = seq[n]
            xb = xbs[n % 2]
            xbkeys = [("xb%d" % (n % 2), c) for c in range(NC_)]
            self.make_xb(tt, l, 0, xb, xbkeys, rstdA[:, tt * T:(tt + 1) * T], ("rstdA", tt), eng="dve")

        def qkv_parts(n):
            h, tt = seq[n]
            cols = slice(tt * T, (tt + 1) * T)
            xb = xbs[n % 2]
            xbkeys = [("xb%d" % (n % 2), c) for c in range(NC_)]
            QT = QTs[n % 2]
            qk = "QT%d" % (n % 2)

            def part_q():
                def mmq(e):
                    last = None
                    for c in range(NC_):
                        last = e.matmul(p5[:], lhsT=wqkv[:, c, 0:128], rhs=xb[:, c, :], start=(c == 0), stop=(c == NC_ - 1))
                    return last
                op("pe", mmq, r=["wqkv"] + xbkeys, w=["ps5"])
                op("dve", lambda e: e.tensor_copy(out=QT[:], in_=p5[:]), r=["ps5"], w=[qk])

            def part_k():
                def mmk(e):
                    last = None
                    for c in range(NC_):
                        last = e.matmul(p5[:], lhsT=wqkv[:, c, 128:256], rhs=xb[:, c, :], start=(c == 0), stop=(c == NC_ - 1))
                    return last
                op("pe", mmk, r=["wqkv"] + xbkeys, w=["ps5"])
                op("dve", lambda e: e.tensor_copy(out=KT[:, cols], in_=p5[:]), r=["ps5"], w=[("KT", tt)])

            def part_v():
                def mmv(e):
                    last = None
                    for blk in range(4):
                        for c in range(NC_):
                            last = e.matmul(p5[:, blk * 128:(blk + 1) * 128], lhsT=xb[:, c, blk * 128:(blk + 1) * 128],
                                            rhs=wqkv[:, c, 256:384], start=(c == 0), stop=(c == NC_ - 1))
                    return last
                op("pe", mmv, r=["wqkv"] + xbkeys, w=["ps5"])
                op("dve", lambda e: e.tensor_copy(out=Va[:, tt * 4:(tt + 1) * 4, 0:128],
                                                  in_=p5[:].rearrange("p (b e) -> p b e", b=4)),
                   r=["ps5"], w=[("V", tt)])
            return [part_q, part_k, part_v]

        def emit_stream(n, fillers):
            h, tt = seq[n]
            W = self.widthA(h)
            slope = alibi_slopes(8)[h]
            QT = QTs[n % 2]
            qk = "QT%d" % (n % 2)
            nk = 4 * tt + 4
            kts = [kt for kt in range(nk) if slope * (tt * 512 - (kt * 128 + 127)) < 160.0]
            first_kt = kts[0]
            pvs = []
            for ii, kt in enumerate(kts):
                diag = kt >= 4 * tt
                kb = kt - 4 * tt if diag else 0
                qlo = kb * 128
                b0 = 0 if ii % 2 == 0 else 6
                sks = ["ps%d" % b0, "ps%d" % (b0 + 1)]
                PT = PTs[ii % 2]
                pk = "PT%d" % (ii % 2)

                def mms(e, kt=kt, qlo=qlo, b0=b0):
                    last = None
                    for sub in range(2):
                        last = e.matmul(self.ps[b0 + sub][:, qlo:T], lhsT=KT[sub * 64:(sub + 1) * 64, kt * 128:(kt + 1) * 128],
                                        rhs=QT[sub * 64:(sub + 1) * 64, qlo:T], start=True, stop=True)
                    return last
                op("pe", mms, r=[("KT", kt // 4), qk], w=sks)
                if ii >= 2:
                    f_, r_, w_ = pvs[ii - 2]
                    op("pe", f_, r=r_, w=w_)
                if ii in fillers:
                    fillers[ii]()

                def ex(e, kt=kt, qlo=qlo, b0=b0, PT=PT):
                    last = None
                    for gs in range(0, T, W):
                        lo = max(gs, qlo)
                        if lo >= gs + W:
                            continue
                        j = 4 * tt + gs // 128 - kt + 3
                        last = e.activation(out=PT[:, :, lo:gs + W], in_=self.psall[:, b0:b0 + 2, lo:gs + W], func=AF.Exp,
                                            bias=self.tabA[:, h, j:j + 1], scale=0.125)
                    return last
                op("act", ex, r=sks + ["tabA"], w=[pk])
                if diag:
                    op("dve", lambda e, PT=PT, kb=kb: e.tensor_tensor(
                        out=PT[:, :, kb * 128:(kb + 1) * 128], in0=PT[:, :, kb * 128:(kb + 1) * 128],
                        in1=self.mcur[:].unsqueeze(1).to_broadcast([128, 2, 128]), op=ALU.mult), r=[pk, "mcur"], w=[pk])

                def pv(e, kt=kt, kb=kb, PT=PT):
                    last = None
                    for sub in range(2):
                        for qb in range(kb, 4):
                            a_ = sub * 4 + qb
                            O = self.ps[2 + a_ // 3][:, (a_ % 3) * 129:(a_ % 3) * 129 + 129]
                            last = e.matmul(O, lhsT=PT[:, sub, qb * 128:(qb + 1) * 128], rhs=Va[:, kt, 0:129],
                                            start=(kt == first_kt and a_ % 3 == 0), stop=(kt == 4 * tt + qb), skip_group_check=True)
                    return last
                pvs.append((pv, [pk, ("V", kt // 4)], ["ps2", "ps3", "ps4"]))
            for kk in range(max(0, len(kts) - 2), len(kts)):
                f_, r_, w_ = pvs[kk]
                op("pe", f_, r=r_, w=w_)
            for b_, wd_ in ((0, 387), (1, 387), (2, 258)):
                op("dve", lambda e, b_=b_, wd_=wd_: e.tensor_copy(out=Osb[:, b_ * 387:b_ * 387 + wd_], in_=self.ps[2 + b_][:, 0:wd_]),
                   r=["ps%d" % (2 + b_)], w=[("Osb", b_)])

        def emit_norm(n):
            h, tt = seq[n]
            cols = slice(tt * T, (tt + 1) * T)
            OTs = OTst[n % 2]
            otk = "OTst%d" % (n % 2)
            okeys = [("Osb", b_) for b_ in range(3)]
            zview = Ov[:, :, :, 128]
            op("dve", lambda e: e.reciprocal(out=rz[:], in_=zview), r=okeys, w=["rz"])
            op("dve", lambda e: e.tensor_scalar(out=rz[:, 1, :], in0=rz[:, 1, :], scalar1=self.nlam[:, l:l + 1], scalar2=None,
                                                op0=ALU.mult), r=["rz", "nlam"], w=["rz"])
            o0 = Ov[:, 0, :, 0:128]
            o1 = Ov[:, 1, :, 0:128]
            op("dve", lambda e: e.tensor_tensor(out=t0, in0=o0, in1=rz[:, 0, :].unsqueeze(2).to_broadcast([128, 4, 128]),
                                                op=ALU.mult), r=okeys + ["rz"], w=["t0"])
            op("dve", lambda e: e.tensor_tensor(out=osb, in0=o1, in1=rz[:, 1, :].unsqueeze(2).to_broadcast([128, 4, 128]),
                                                op=ALU.mult), r=okeys + ["rz"], w=["osb"])
            op("dve", lambda e: e.tensor_tensor(out=osb, in0=osb, in1=t0, op=ALU.add), r=["osb", "t0"], w=["osb"])
            op("dve", lambda e: e.tensor_tensor(out=t0, in0=osb, in1=osb, op=ALU.mult), r=["osb"], w=["t0"])
            op("dve", lambda e: e.reduce_sum(out=ssq[:, 0, :], in_=t0, axis=AX.X), r=["t0"], w=["ssq"])
            op("act", lambda e: e.activation(out=ssq[:, 1, :], in_=ssq[:, 0, :], func=AF.Sqrt, bias=self.epsc(SUBLN_EPS),
                                             scale=1.0 / 128), r=["ssq", "epsc"], w=["ssq"])
            op("dve", lambda e: e.reciprocal(out=ssq[:, 1, :], in_=ssq[:, 1, :]), r=["ssq"], w=["ssq"])
            op("dve", lambda e: e.tensor_tensor(out=osb, in0=osb, in1=ssq[:, 1, :].unsqueeze(2).to_broadcast([128, 4, 128]),
                                                op=ALU.mult), r=["osb", "ssq"], w=["osb"])
            op("dve", lambda e: e.tensor_tensor(out=on[:], in0=osb, in1=self.sgbc[:, l, :].unsqueeze(1).to_broadcast([128, 4, 128]),
                                                op=ALU.mult), r=["osb", ("sgbc", l)], w=["on"])

            def tail():
                def trs(e):
                    last = None
                    for qb in range(4):
                        last = e.transpose(out=ps5b[:, qb * 128:(qb + 1) * 128], in_=on[:, qb, :], identity=self.identb[:])
                    return last
                op("pe", trs, r=["on", "identb"], w=["ps5"])
                op("dve", lambda e: e.tensor_copy(out=OTs[:], in_=ps5b[:, 0:T]), r=["ps5"], w=[otk])
                op("sp", lambda e: e.dma_start(out=self.ot_d[h, :, cols], in_=OTs[:]), r=[otk], w=[("otd", h, tt)], dma=otk)
            return tail

        N = len(seq)
        ldw(0)
        emit_xb(0)
        for part in qkv_parts(0):
            part()
        tail = None
        for n in range(N):
            h, tt = seq[n]
            fillers = {}
            if n + 1 < N:
                emit_xb(n + 1)
                k0 = 0
                if seq[n + 1][0] != h:
                    ldw(seq[n + 1][0])
                    k0 = 4
                for i_, part in enumerate(qkv_parts(n + 1)):
                    fillers[k0 + i_] = part
            if tail is not None:
                fillers[3] = tail
            emit_stream(n, fillers)
            tail = emit_norm(n)
        tail()

    def woA(self, stack, l):
        op = self.op
        A = lambda n, s, d: self.alloc(stack, n, s, d)
        wo = A("o_wo", [128, NC_, D], BF16)
        OTin = [A("o_OT%d" % i, [128, NC_, T], BF16) for i in range(2)]
        fTa = A("o_fTa", [128, 4, T], F32)
        sqh = A("o_sqh", [128, 4, T], BF16)
        rstd2 = A("o_rstd2", [128, T], F32)
        self.ptmp = A("o_ptmp", [128, T], F32)

        def ldw(e):
            out = []
            for i in range(4):
                out.append(e.dma_start(out=wo[:, 2 * i:2 * i + 2, :],
                                       in_=self.woa_d[l, i * 256:(i + 1) * 256, :].rearrange("(c p) n -> p c n", p=128)))
            return out
        op("pool", ldw, w=["wo"], dma="wo", ndma=4)
        for tt in range(NT):
            cols = slice(tt * T, (tt + 1) * T)
            OT = OTin[tt % 2]
            ok = "OTin%d" % (tt % 2)
            op("sp", lambda e, OT=OT, cols=cols: [e.dma_start(out=OT[:, h, :], in_=self.ot_d[h, :, cols]) for h in range(8)],
               r=[("otd", h, tt) for h in range(8)], w=[ok], dma=ok, ndma=8)
            for half in range(2):
                for o in range(4):
                    oc = half * 4 + o
                    bank = self.ps[half * 4 + o]

                    def mm(e, OT=OT, oc=oc, bank=bank):
                        last = None
                        for hh in range(NC_):
                            last = e.matmul(bank[:], lhsT=wo[:, hh, oc * 128:(oc + 1) * 128], rhs=OT[:, hh, :],
                                            start=(hh == 0), stop=(hh == NC_ - 1))
                        return last
                    op("pe", mm, r=["wo", ok], w=["ps%d" % (half * 4 + o)])
                if half == 0:
                    for o in range(4):
                        op("act", lambda e, o=o: e.copy(out=fTa[:, o, :], in_=self.ps[o][:]), r=["ps%d" % o], w=[("fTa", o)])
            srcs = [(fTa[:, o, :], ("fTa", o)) for o in range(4)] + [(self.ps[4 + o][:], "ps%d" % (4 + o)) for o in range(4)]
            self.postnorm(tt, l, 1, srcs, sqh, rstd2, 0)


    def setup_B(self, stack, tmp):
        op = self.op
        A = lambda n, s, d: self.alloc(stack, n, s, d)
        TA = lambda n, s, d: self.alloc(tmp, n, s, d)
        pi = TA("posI", [128, 1], mybir.dt.int32)
        pf = TA("posF", [128, 1], F32)
        bvrow = TA("bvrow", [1, 128], F32)
        skrow = TA("skrow", [1, 32], F32)
        skbc = TA("skbc", [128, 32], F32)
        sl = alibi_slopes(16)
        for j in range(2):
            self.load_cols(self.bqb_d[j].rearrange("two g d -> g two d"), 8, self.bqT[:, j, :], 0, st_view=(8, 2, 64))
        self.load_cols(self.bkv_d, 2, self.bkc, 0)
        op("sp", lambda e: e.dma_start(out=bvrow[:], in_=self.bkv_d[1:2, :]), w=["bvrow"], dma="bvrow")
        self.bcast_row(bvrow[:], 128, self.bvbc[:], "bvbc", "bvrow")
        op("pool", lambda e: e.iota(pi[:], pattern=[[0, 1]], base=-64, channel_multiplier=1), w=["posI"])
        op("dve", lambda e: e.tensor_copy(out=pf[:], in_=pi[:]), r=["posI"], w=["posF"])
        for h in range(16):
            op("dve", lambda e, h=h: e.tensor_scalar(out=self.biasB[:, 1, h:h + 1], in0=pf[:], scalar1=float(sl[h]), scalar2=None,
                                                     op0=ALU.mult), r=["posF"], w=["biasB"])
            op("dve", lambda e, h=h: e.tensor_scalar(out=self.biasB[:, 0, h:h + 1], in0=pf[:], scalar1=float(sl[h]),
                                                     scalar2=float(-128.0 * sl[h]), op0=ALU.mult, op1=ALU.add),
               r=["posF"], w=["biasB"])
        op("sp", lambda e: e.dma_start(out=skrow[:], in_=self.sinks_d), w=["skrow"], dma="skrow")
        self.bcast_row(skrow[:], 32, skbc[:], "skbc", "skrow")
        for j in range(2):
            op("dve", lambda e, j=j: e.tensor_tensor(out=self.sinkf[:, j, :], in0=skbc[:, j * 16:(j + 1) * 16], in1=self.biasB[:, 1, :],
                                                     op=ALU.add), r=["skbc", "biasB"], w=["sinkf"])
        op("act", lambda e: e.activation(out=self.sinkf[:], in_=self.sinkf[:], func=AF.Exp), r=["sinkf"], w=["sinkf"])

    def attnB(self, stack, l):
        op = self.op
        j = l - NA
        A = lambda n, s, d: self.alloc(stack, n, s, d)
        KT = A("b_KT", [128, S], BF16)
        Vb = A("b_V", [128, NB, 2, 66], BF16)
        wqs = [A("b_wq%d" % i, [128, NC_, 2, 128], BF16) for i in range(2)]
        wos = [A("b_wo%d" % i, [128, 512], BF16) for i in range(4)]
        xb = A("b_xb", [128, NC_, T], BF16)
        sqh = A("b_sqh", [128, 4, T], BF16)
        QT = A("b_QT", [128, 8, T], BF16)
        PTs = [A("b_PT%d" % i, [128, 2, 2, 128], BF16) for i in range(2)]
        On = A("b_On", [128, 16, HD], BF16)
        OT = A("b_OT", [128, NC_, T], BF16)
        fTa = QT[:].rearrange("p g t -> p (g t)").bitcast(F32).rearrange("p (o t) -> p o t", o=4)
        rstd1 = A("b_rstd1", [128, T], F32)
        self.ptmp = A("b_ptmp", [128, T], F32)
        den = A("b_den", [128, 16], F32)
        xbkeys = [("xb", c) for c in range(NC_)]
        ps4b = self.ps[4][:].bitcast(BF16)
        obank = lambda hd: (2, 3, 7)[hd // 7]
        ooff = lambda hd: (hd % 7) * 65
        if l == NA:
            wkv = A("b_wkv", [128, NC_, 256], BF16)
            op("pool", lambda e: e.dma_start(out=wkv[:], in_=self.wkv_d.rearrange("(c p) n -> p c n", p=128)), w=["wkv"], dma="wkv")
            op("pool", lambda e: e.memset(Vb[:, :, :, 64:66], 1.0), w=[("V", t_) for t_ in range(NT)])
        else:
            op("sp", lambda e: e.dma_start(out=KT[:], in_=self.kvT_d), r=[("kvTd", t_) for t_ in range(NT)],
               w=[("KT", t_) for t_ in range(NT)], dma="KTld")
            op("sp", lambda e: e.dma_start(out=Vb[:].rearrange("p a b c -> p (a b c)"), in_=self.kvV_d),
               r=[("kvVd", t_) for t_ in range(NT)], w=[("V", t_) for t_ in range(NT)], dma="Vld")
        wqi = 0
        woi = 0
        p4 = self.ps[4]

        def prep(tt):
            cols = slice(tt * T, (tt + 1) * T)
            self.stats_xT(tt, sqh, 4, rstd1, "rstd1")
            if l == NA:
                self.make_xb(tt, l, 0, xb, xbkeys, rstd1, "rstd1", gains=self.kvg)

                def mmk(e):
                    last = None
                    for c in range(NC_):
                        last = e.matmul(p4[:], lhsT=wkv[:, c, 0:128], rhs=xb[:, c, :], start=(c == 0), stop=(c == NC_ - 1))
                    return last
                op("pe", mmk, r=["wkv"] + xbkeys, w=["ps4"])
                op("act", lambda e: e.activation(out=KT[:, cols], in_=p4[:], func=AF.Identity, bias=self.bkc[:, 0:1]),
                   r=["ps4", "bkc"], w=[("KT", tt)])

                def mmv(e):
                    last = None
                    for blk in range(4):
                        for c in range(NC_):
                            last = e.matmul(p4[:, blk * 128:(blk + 1) * 128], lhsT=xb[:, c, blk * 128:(blk + 1) * 128],
                                            rhs=wkv[:, c, 128:256], start=(c == 0), stop=(c == NC_ - 1))
                    return last
                op("pe", mmv, r=["wkv"] + xbkeys, w=["ps4"])
                for blk in range(4):
                    op("dve", lambda e, blk=blk: e.tensor_tensor(
                        out=Vb[:, tt * 4 + blk, :, 0:64], in0=p4[:, blk * 128:(blk + 1) * 128].rearrange("p (k d) -> p k d", k=2),
                        in1=self.bvbc[:].rearrange("p (k d) -> p k d", k=2), op=ALU.add), r=["ps4", "bvbc"], w=[("V", tt)])
                op("sp", lambda e: e.dma_start(out=self.kvT_d[:, cols], in_=KT[:, cols]), r=[("KT", tt)],
                   w=[("kvTd", tt)], dma="kvst")
                op("sp", lambda e: e.dma_start(out=self.kvV_d[:, tt * 4 * 132:(tt + 1) * 4 * 132],
                                               in_=Vb[:, tt * 4:(tt + 1) * 4, :, :].rearrange("p a b c -> p (a b c)")),
                   r=[("V", tt)], w=[("kvVd", tt)], dma="kvst2")
            self.make_xb(tt, l, 0, xb, xbkeys, rstd1, "rstd1")

        prep(0)
        for tt in range(NT):
            cols = slice(tt * T, (tt + 1) * T)
            for g in range(8):
                if g % 2 == 0:
                    slot = wqs[wqi % 2]
                    sk = "wq%d" % (wqi % 2)
                    wqi += 1

                    def ldq(e, slot=slot, g=g):
                        a = e.dma_start(out=slot[:, :, 0, :],
                                        in_=self.wqb_d[j, :, g * 64:(g + 2) * 64].rearrange("(c p) n -> p c n", p=128))
                        b_ = e.dma_start(out=slot[:, :, 1, :],
                                         in_=self.wqb_d[j, :, 512 + g * 64:512 + (g + 2) * 64].rearrange("(c p) n -> p c n", p=128))
                        return [a, b_]
                    op("pool", ldq, w=[sk], dma=sk, ndma=2)
                pq = self.ps[5 + g % 2]
                pqk = "ps%d" % (5 + g % 2)

                def mmq(e, slot=slot, pq=pq, g=g):
                    last = None
                    for c in range(NC_):
                        o_ = (g % 2) * 64
                        e.matmul(pq[0:64, :], lhsT=slot[:, c, 0, o_:o_ + 64], rhs=xb[:, c, :],
                                 start=(c == 0), stop=(c == NC_ - 1), tile_position=(0, 0))
                        last = e.matmul(pq[64:128, :], lhsT=slot[:, c, 1, o_:o_ + 64], rhs=xb[:, c, :],
                                        start=(c == 0), stop=(c == NC_ - 1), tile_position=(0, 64))
                    return last
                op("pe", mmq, r=[sk] + xbkeys, w=[pqk])
                op("act", lambda e, g=g, pq=pq: e.activation(out=QT[:, g, :], in_=pq[:], func=AF.Identity, bias=self.bqT[:, j, g:g + 1]),
                   r=[pqk, "bqT"], w=[("QT", g)])
            if tt + 1 < NT:
                prep(tt + 1)
            tail = None
            for qb in range(4):
                gq = tt * 4 + qb
                kts = ([(0, gq - 1)] if gq > 0 else []) + [(1, gq)]
                started = set()
                pend = None
                for g in range(8):
                    sbanks = (0, 1) if g % 2 == 0 else (5, 6)
                    Sbs = [self.ps[sbanks[0]], self.ps[sbanks[1]]]
                    sks = ["ps%d" % sbanks[0], "ps%d" % sbanks[1]]
                    PT = PTs[g % 2]
                    pk = "bPT%d" % (g % 2)

                    def mms(e, g=g, Sbs=Sbs, qb=qb, kts=kts):
                        last = None
                        for (ki, kt) in kts:
                            for hh in range(2):
                                last = e.matmul(Sbs[hh][:, ki * 128:(ki + 1) * 128],
                                                lhsT=KT[hh * 64:(hh + 1) * 64, kt * 128:(kt + 1) * 128],
                                                rhs=QT[hh * 64:(hh + 1) * 64, g, qb * 128:(qb + 1) * 128], start=True, stop=True)
                        return last
                    op("pe", mms, r=[("KT", kt // 4) for (_, kt) in kts] + [("QT", g)], w=sks)
                    if pend is not None:
                        op("pe", pend[0], r=pend[1], w=pend[2])
                        pend = None
                    if g == 1 and tail is not None:
                        tail()
                        tail = None

                    def ex(e, g=g, Sbs=Sbs, PT=PT, kts=kts):
                        last = None
                        for (ki, kt) in kts:
                            for hh in range(2):
                                hd = g + 8 * hh
                                last = e.activation(out=PT[:, ki, hh, :], in_=Sbs[hh][:, ki * 128:(ki + 1) * 128],
                                                    func=AF.Exp, bias=self.biasB[:, ki, hd:hd + 1], scale=0.125)
                        return last
                    op("act", ex, r=sks + ["biasB"], w=[pk])

                    def msk(e, PT=PT, kts=kts):
                        last = None
                        for (ki, kt) in kts:
                            m = self.mprev if ki == 0 else self.mcur
                            last = e.tensor_tensor(out=PT[:, ki, :, :], in0=PT[:, ki, :, :],
                                                   in1=m[:].unsqueeze(1).to_broadcast([128, 2, 128]), op=ALU.mult)
                        return last
                    op("dve", msk, r=[pk, "mcur", "mprev"], w=[pk])
                    pvl = []
                    for hh in range(2):
                        hd = g + 8 * hh
                        bk_ = obank(hd)
                        for n_, (ki, kt) in enumerate(kts):
                            st = (n_ == 0) and (bk_ not in started)
                            if n_ == 0:
                                started.add(bk_)
                            pvl.append((hh, hd, bk_, ki, kt, st, n_ == len(kts) - 1))

                    def pv(e, PT=PT, pvl=pvl):
                        last = None
                        for (hh, hd, bk_, ki, kt, st, sp_) in pvl:
                            last = e.matmul(self.ps[bk_][:, ooff(hd):ooff(hd) + 65], lhsT=PT[:, ki, hh, :],
                                            rhs=Vb[:, kt, hh, 0:65], start=st, stop=sp_, skip_group_check=True)
                        return last
                    pend = (pv, [pk] + [("V", kt // 4) for (_, kt) in kts], ["ps2", "ps3", "ps7"])
                op("pe", pend[0], r=pend[1], w=pend[2])
                for bk_, h0, h1 in ((2, 0, 7), (3, 7, 14), (7, 14, 16)):
                    nh = h1 - h0
                    ov = self.ps[bk_][:, 0:nh * 65].rearrange("p (h e) -> p h e", e=65)
                    op("dve", lambda e, ov=ov, h0=h0, h1=h1: e.tensor_tensor(out=den[:, h0:h1], in0=ov[:, :, 64],
                                                                             in1=self.sinkf[:, j, h0:h1], op=ALU.add),
                       r=["ps%d" % bk_, "sinkf"], w=["den"])
                op("dve", lambda e: e.reciprocal(out=den[:], in_=den[:]), r=["den"], w=["den"])
                for bk_, h0, h1 in ((2, 0, 7), (3, 7, 14), (7, 14, 16)):
                    nh = h1 - h0
                    ov = self.ps[bk_][:, 0:nh * 65].rearrange("p (h e) -> p h e", e=65)
                    op("dve", lambda e, ov=ov, h0=h0, h1=h1, nh=nh: e.tensor_tensor(
                        out=On[:, h0:h1, :], in0=ov[:, :, 0:64], in1=den[:, h0:h1].unsqueeze(2).to_broadcast([128, nh, 64]),
                        op=ALU.mult), r=["ps%d" % bk_, "den"], w=["On"])

                def mk_tail(qb=qb):
                    def tail_():
                        def trs(e):
                            last = None
                            for c in range(NC_):
                                last = e.transpose(out=ps4b[:, c * 128:(c + 1) * 128], in_=On[:, 2 * c:2 * c + 2, :],
                                                   identity=self.identb[:])
                            return last
                        op("pe", trs, r=["On", "identb"], w=["ps4"])
                        op("act", lambda e: e.copy(out=OT[:, :, qb * 128:(qb + 1) * 128],
                                                   in_=ps4b[:, 0:1024].rearrange("p (c t) -> p c t", c=8)),
                           r=["ps4"], w=[("OT", qb)])
                    return tail_
                tail = mk_tail()
            tail()
            for half in range(2):
                for hh in range(NC_):
                    slot = wos[woi % 4]
                    sk = "wo%d" % (woi % 4)
                    woi += 1
                    op("pool", lambda e, slot=slot, hh=hh, half=half: e.dma_start(
                        out=slot[:], in_=self.wob_d[j, hh * 128:(hh + 1) * 128, half * 512:(half + 1) * 512]), w=[sk], dma=sk)

                    def mmo(e, slot=slot, hh=hh, half=half):
                        last = None
                        for o in range(4):
                            last = e.matmul(self.ps[half * 4 + o][:], lhsT=slot[:, o * 128:(o + 1) * 128], rhs=OT[:, hh, :],
                                            start=(hh == 0), stop=(hh == NC_ - 1))
                        return last
                    op("pe", mmo, r=[sk] + [("OT", q_) for q_ in range(4)], w=["ps%d" % (half * 4 + o) for o in range(4)])
                if half == 0:
                    for o in range(4):
                        op("act", lambda e, o=o: e.copy(out=fTa[:, o, :], in_=self.ps[o][:]), r=["ps%d" % o],
                           w=[("QT", 2 * o), ("QT", 2 * o + 1)])
            srcs = [(fTa[:, o, :], [("QT", 2 * o), ("QT", 2 * o + 1)]) for o in range(4)] + \
                   [(self.ps[4 + o][:], "ps%d" % (4 + o)) for o in range(4)]
            self.postnorm(tt, l, 1, srcs, sqh, rstd1, 0, rkey="rstd1")

    def dbgq(self, stack, mode):
        op = self.op
        A = lambda n, s, d: self.alloc(stack, n, s, d)
        slot = A("dq_slot", [128, NC_, 2, 128], BF16)
        slotf = A("dq_slotf", [128, NC_, 128], F32)
        if mode == 0:
            op("pool", lambda e: e.dma_start(out=slot[:, :, 0, :], in_=self.wqb_d[0, :, 0:128].rearrange("(c p) n -> p c n", p=128)),
               w=["dq"], dma="dq")
        elif mode == 1:
            op("sp", lambda e: e.dma_start(out=slotf[:], in_=self.wqb_d[0, :, 0:128].rearrange("(c p) n -> p c n", p=128)),
               w=["dq"], dma="dq")
        elif mode == 2:
            op("sp", lambda e: e.dma_start(out=slotf[:, 0, :], in_=self.wqb_d[0, 0:128, 0:128]), w=["dq"], dma="dq")
        elif mode == 3:
            op("sp", lambda e: e.dma_start(out=slotf[:, 0, :], in_=self.wqb_d[1, 0:128, 0:128]), w=["dq"], dma="dq")

def build_program(plan):
    from contextlib import ExitStack
    nc = bass.Bass("TRN2", target_bir_lowering=False)
    b = Builder(nc, plan)
    b.declare()
    with ExitStack() as stack:
        b.alloc_perm(stack)
        with ExitStack() as tmp:
            b.setup(stack, tmp)
            b.setup_A(stack, tmp)
            b.setup_B(stack, tmp)
        b.sc.barrier()
        with ExitStack() as s2:
            b.load_x(s2)
        for item in plan:
            b.sc.barrier()
            b.uid += 1
            with ExitStack() as s2:
                getattr(b, item[0])(s2, *item[1:])
        b.sc.barrier()
        b.uid += 1
        with ExitStack() as s2:
            b.store_x(s2)
        b.sc.emit(nc, stack)
    return nc


FULL_PLAN = [("attnA", 0), ("woA", 0), ("ffn", 0), ("attnA", 1), ("woA", 1), ("ffn", 1),
             ("attnB", 2), ("ffn", 2), ("attnB", 3), ("ffn", 3)]

_IN_ORDER = ["x", "norm_gains", "w_qkv_a", "lambda_qk_a", "subln_a", "w_o_a", "kv_norm", "w_kv_b", "b_kv_b",
             "w_q_b", "b_q_b", "sinks_b", "w_o_b", "w_up", "conv_w", "conv_b", "w_down"]
_SHAPES = {
    "norm_gains": (DEPTH * 4 * NC_, 128), "lambda_qk_a": (1, NA * 4 * HD), "subln_a": (1, NA * 128),
    "kv_norm": (NC_, 128), "b_kv_b": (2, 128), "b_q_b": (2, 2, 8, HD), "sinks_b": (1, 32),
    "conv_w": (DEPTH * 3 * NFF, 128), "conv_b": (DEPTH * NFF, 128),
}


def run(inputs, plan, n_cores=8, trace=False):
    nc = build_program(plan)
    shared = {}
    for k in _IN_ORDER:
        if k == "x":
            continue
        a = np.ascontiguousarray(np.asarray(inputs[k], dtype=np.float32))
        if k in _SHAPES:
            a = a.reshape(_SHAPES[k])
        shared[k] = a
    x = np.asarray(inputs["x"], dtype=np.float32)
    in_maps = []
    for c in range(n_cores):
        m = dict(shared)
        m["x"] = np.ascontiguousarray(x[c])
        in_maps.append(m)
    res = run_bass_kernel_spmd(nc, in_maps, core_ids=list(range(n_cores)), trace=trace)
    out = np.stack([np.asarray(r["out"]) for r in res.results], axis=0)
    return out.astype(np.float32), res


def kernel(**inputs):
    out, _ = run(inputs, FULL_PLAN)
    return out
```

```python
import math
import numpy as np
import concourse.bass as bass
import concourse.mybir as mybir
from concourse.bass_utils import run_bass_kernel_spmd

F32 = mybir.dt.float32
BF16 = mybir.dt.bfloat16
AF = mybir.ActivationFunctionType
ALU = mybir.AluOpType
AX = mybir.AxisListType

DEBUG_OT = False
DBG_B = 99
DBG_NOQDMA = False
D = 1024
S = 4096
DEPTH = 4
NA = 2
DFF = 2816
NFF = DFF // 128
HD = 64
NC_ = 8
T = 512
NT = S // T
NB = S // 128
NORM_EPS = 1e-6
SUBLN_EPS = 1e-5


def _ap(t):
    try:
        return t[:]
    except Exception:
        return t


def alibi_slopes(n):
    return [2.0 ** (-8.0 * (i + 1) / n) for i in range(n)]


class _Op:
    __slots__ = ("eng", "fn", "deps", "dma", "idx", "flag", "done_sem", "done_val", "ndma")


class Sched:
    def __init__(self):
        self.ops = []
        self.last_w = {}
        self.readers = {}
        self.last_dma = {}
        self.barrier_idx = None

    def barrier(self):
        last = {}
        for op in self.ops:
            last[(op.eng, op.dma)] = op.idx
        self.bar_deps = set(last.values())

    def add(self, eng, fn, r=(), w=(), dma=None, ndma=1, insure=True):
        op = _Op()
        op.eng, op.fn, op.dma, op.idx, op.flag, op.ndma = eng, fn, dma, len(self.ops), False, ndma
        deps = set(getattr(self, "bar_deps", ()))
        for k in r:
            if k in self.last_w:
                deps.add(self.last_w[k])
        for k in w:
            if k in self.last_w:
                deps.add(self.last_w[k])
            deps.update(self.readers.get(k, ()))
        if insure and dma is not None and dma in self.last_dma:
            deps.add(self.last_dma[dma])
        for k in r:
            self.readers.setdefault(k, []).append(op.idx)
        for k in w:
            self.last_w[k] = op.idx
            self.readers[k] = []
        if dma is not None:
            self.last_dma[dma] = op.idx
        deps.discard(op.idx)
        op.deps = deps
        self.ops.append(op)
        return op.idx

    def emit(self, nc, stack):
        ops = self.ops
        for op in ops:
            for d in op.deps:
                dd = ops[d]
                if dd.eng == "pe" and op.eng == "pe" and dd.dma is None and op.dma is None:
                    continue
                dd.flag = True
        last_of = {}
        for op in ops:
            last_of[(op.eng, op.dma is not None)] = op
        for op in last_of.values():
            op.flag = True
        cnt = {}
        for e in ("pe", "act", "dve", "pool"):
            cnt[e] = stack.enter_context(nc.semaphore("cnt_" + e))
        dkeys = []
        for op in ops:
            if op.dma is not None and op.dma not in dkeys:
                dkeys.append(op.dma)
        dsem = {k: stack.enter_context(nc.semaphore("dma_%d" % i)) for i, k in enumerate(dkeys)}
        seq = {e: 0 for e in cnt}
        dcount = {k: 0 for k in dkeys}
        for op in ops:
            if op.dma is not None:
                dcount[op.dma] += 16 * op.ndma
                op.done_sem, op.done_val = dsem[op.dma], dcount[op.dma]
            elif op.flag:
                seq[op.eng] += 1
                op.done_sem, op.done_val = cnt[op.eng], seq[op.eng]
            else:
                op.done_sem, op.done_val = None, None
        per_eng = {e: [] for e in ("pe", "act", "dve", "pool", "sp")}
        for op in ops:
            per_eng[op.eng].append(op)
        final_waits = [(op.done_sem, op.done_val) for op in last_of.values()]

        def run(engname, eng):
            waited = {}
            for op in per_eng[engname]:
                for d in sorted(op.deps):
                    dd = ops[d]
                    if dd.done_sem is None:
                        continue
                    if dd.eng == "pe" and engname == "pe" and dd.dma is None and op.dma is None:
                        continue
                    key = id(dd.done_sem)
                    if waited.get(key, (None, 0))[1] < dd.done_val:
                        eng.wait_ge(dd.done_sem, dd.done_val)
                        waited[key] = (dd.done_sem, dd.done_val)
                res = op.fn(eng)
                if op.dma is not None:
                    lst = res if isinstance(res, (list, tuple)) else [res]
                    assert len(lst) == op.ndma, (len(lst), op.ndma)
                    for ins in lst:
                        ins.then_inc(op.done_sem, 16)
                elif op.flag:
                    last = res[-1] if isinstance(res, (list, tuple)) else res
                    last.then_inc(op.done_sem, 1)
            if engname == "sp":
                for sem, val in final_waits:
                    eng.wait_ge(sem, val)

        with nc.Block() as block:
            @block.tensor
            def _(e):
                run("pe", e)

            @block.scalar
            def _(e):
                run("act", e)

            @block.vector
            def _(e):
                run("dve", e)

            @block.gpsimd
            def _(e):
                run("pool", e)

            @block.sync
            def _(e):
                run("sp", e)


class Builder:
    def __init__(self, nc, plan):
        self.nc = nc
        self.plan = plan
        self.sc = Sched()
        self.uid = 0
        self.pc_list = None
        self.use_ln = False

    def op(self, eng, fn, r=(), w=(), dma=None, ndma=1, insure=True):
        return self.sc.add(eng, fn, r, w, dma, ndma, insure)

    def barrier_keys(self):
        return list(self.sc.last_w.keys())

    def declare(self):
        nc = self.nc
        dt = nc.dram_tensor
        self.x_d = dt("x", [S, D], F32, kind="ExternalInput").ap()
        self.out_d = dt("out", [S, D], F32, kind="ExternalOutput").ap()
        self.ng_d = dt("norm_gains", [DEPTH * 4 * NC_, 128], F32, kind="ExternalInput").ap()
        self.wqkv_d = dt("w_qkv_a", [NA, D, 3 * D], F32, kind="ExternalInput").ap()
        self.lam_d = dt("lambda_qk_a", [1, NA * 4 * HD], F32, kind="ExternalInput").ap()
        self.subln_d = dt("subln_a", [1, NA * 128], F32, kind="ExternalInput").ap()
        self.woa_d = dt("w_o_a", [NA, D, D], F32, kind="ExternalInput").ap()
        self.kvn_d = dt("kv_norm", [NC_, 128], F32, kind="ExternalInput").ap()
        self.wkv_d = dt("w_kv_b", [D, 256], F32, kind="ExternalInput").ap()
        self.bkv_d = dt("b_kv_b", [2, 128], F32, kind="ExternalInput").ap()
        self.wqb_d = dt("w_q_b", [2, D, D], F32, kind="ExternalInput").ap()
        self.bqb_d = dt("b_q_b", [2, 2, 8, HD], F32, kind="ExternalInput").ap()
        self.sinks_d = dt("sinks_b", [1, 32], F32, kind="ExternalInput").ap()
        self.wob_d = dt("w_o_b", [2, D, D], F32, kind="ExternalInput").ap()
        self.wup_d = dt("w_up", [DEPTH, D, 2 * DFF], F32, kind="ExternalInput").ap()
        self.convw_d = dt("conv_w", [DEPTH * 3 * NFF, 128], F32, kind="ExternalInput").ap()
        self.convb_d = dt("conv_b", [DEPTH * NFF, 128], F32, kind="ExternalInput").ap()
        self.wdn_d = dt("w_down", [DEPTH, DFF, D], F32, kind="ExternalInput").ap()
        self.wupT_d = dt("wupT_scratch", [DEPTH, NFF // 2, 128, NC_ * 512], BF16).ap()
        self.wdnT_d = dt("wdnT_scratch", [DEPTH, 2, NFF, 128, 512], BF16).ap()
        self.kvT_d = dt("kvT_scratch", [128, S], BF16).ap()
        self.kvV_d = dt("kvV_scratch", [128, NB * 2 * 66], BF16).ap()
        self.ot_d = (dt("ot_scratch", [8, 128, S], BF16, kind="ExternalOutput") if DEBUG_OT else dt("ot_scratch", [8, 128, S], BF16)).ap()


    def alloc(self, stack, name, shape, dtype):
        return stack.enter_context(self.nc.sbuf_tensor("%s_p%d" % (name, self.uid), shape, dtype))

    def palloc(self, stack, name, shape, dtype=F32):
        return stack.enter_context(self.nc.psum_tensor(name, shape, dtype))

    def alloc_perm(self, stack):
        A = lambda n, s, d: self.alloc(stack, n, s, d)
        self.xT = A("xT", [128, NC_, S], F32)
        self.onesf = A("onesf", [128, 128], F32)
        self.identf = A("identf", [128, 128], F32)
        self.identb = A("identb", [128, 128], BF16)
        self.onesb = A("onesb", [128, 128], BF16)
        self.mcur = A("mcur", [128, 128], BF16)
        self.mprev = A("mprev", [128, 128], BF16)
        self.gT = A("gT", [128, 128], F32)
        self.kvg = A("kvg", [128, NC_], F32)
        self.convw = A("convw", [128, DEPTH * 3 * NFF], F32)
        self.convb = A("convb", [128, DEPTH * NFF], F32)
        self.epst = A("epst", [128, 2], F32)
        self.psall = self.palloc(stack, "psall", [128, 8, 512], F32)
        self.ps = [self.psall[:, i, :] for i in range(8)]
        self.tabA = A("tabA", [128, 8, 35], F32)
        self.sgbc = A("sgbc", [128, NA, 128], F32)
        self.nlam = A("nlam", [128, 2], F32)
        self.bqT = A("bqT", [128, 2, 8], F32)
        self.bkc = A("bkc", [128, 2], F32)
        self.bvbc = A("bvbc", [128, 128], F32)
        self.biasB = A("biasB", [128, 2, 16], F32)
        self.sinkf = A("sinkf", [128, 2, 16], F32)

    def setup(self, stack, tmp):
        nc = self.nc
        A = lambda n, s, d: self.alloc(stack, n, s, d)
        TA = lambda n, s, d: self.alloc(tmp, n, s, d)
        self.pstage = TA("pstage", [128, 128], F32)

        op = self.op
        op("pool", lambda e: e.memset(self.onesf[:], 1.0), w=["onesf"])
        op("pool", lambda e: e.memset(self.onesb[:], 1.0), w=["onesb"])
        op("pool", lambda e: e.memset(self.epst[:, 0:1], NORM_EPS), w=["epsc"])
        op("pool", lambda e: e.memset(self.epst[:, 1:2], SUBLN_EPS), w=["epsc"])
        op("pool", lambda e: e.affine_select(out=self.identf[:], in_=self.onesf[:], pattern=[[-1, 128]],
                                             compare_op=ALU.is_equal, fill=0.0, base=0, channel_multiplier=1),
           r=["onesf"], w=["identf"])
        op("pool", lambda e: e.affine_select(out=self.mcur[:], in_=self.onesb[:], pattern=[[1, 128]],
                                             compare_op=ALU.is_ge, fill=0.0, base=0, channel_multiplier=-1),
           r=["onesb"], w=["mcur"])
        op("pool", lambda e: e.affine_select(out=self.mprev[:], in_=self.onesb[:], pattern=[[-1, 128]],
                                             compare_op=ALU.is_gt, fill=0.0, base=0, channel_multiplier=1),
           r=["onesb"], w=["mprev"])
        op("dve", lambda e: e.tensor_copy(out=self.identb[:], in_=self.identf[:]), r=["identf"], w=["identb"])
        self.load_cols(self.ng_d, 128, self.gT, 0)
        self.load_cols(self.kvn_d, NC_, self.kvg, 0)
        for i0 in range(0, DEPTH * 3 * NFF, 128):
            n = min(128, DEPTH * 3 * NFF - i0)
            self.load_cols(self.convw_d[i0:i0 + n, :], n, self.convw, i0)
        self.load_cols(self.convb_d, DEPTH * NFF, self.convb, 0)

    def _colkey(self, dst):
        try:
            return dst.name
        except Exception:
            return "anon"

    def load_cols(self, rows_ap, R, dst, col0, st_view=None):
        op = self.op
        st, ps = self.pstage, self.ps[0]
        if st_view is None:
            op("sp", lambda e: e.dma_start(out=st[0:R, :], in_=rows_ap), w=["pstage"], dma="pstage")
        else:
            op("sp", lambda e: e.dma_start(out=st[0:R, :].rearrange("r (a b) -> r a b", a=st_view[1]), in_=rows_ap),
               w=["pstage"], dma="pstage")
        op("pe", lambda e: e.transpose(out=ps[:, 0:R], in_=st[0:R, :], identity=self.identf[0:R, 0:R]),
           r=["pstage", "identf"], w=["ps0"])
        op("dve", lambda e: e.tensor_copy(out=dst[:, col0:col0 + R], in_=ps[:, 0:R]), r=["ps0"], w=[("cols", self._colkey(dst))])

    def load_x(self, stack):
        op = self.op
        xin = [self.alloc(stack, "xin%d" % i, [128, D], F32) for i in range(2)]
        for tb in range(NB):
            st = xin[tb % 2]
            sk = "xin%d" % (tb % 2)
            op("sp", lambda e, st=st, tb=tb: e.dma_start(out=st[:], in_=self.x_d[tb * 128:(tb + 1) * 128, :]),
               w=[sk], dma=sk)
            for half in range(2):
                ps = self.ps[2 * (tb % 2) + half]
                pk = "ps%d" % (2 * (tb % 2) + half)

                def tr(e, st=st, ps=ps, half=half):
                    last = None
                    for i in range(4):
                        c = half * 4 + i
                        last = e.transpose(out=ps[:, i * 128:(i + 1) * 128], in_=st[:, c * 128:(c + 1) * 128],
                                           identity=self.identf[:])
                    return last
                op("pe", tr, r=[sk, "identf"], w=[pk])
                dst = self.xT[:, half * 4:half * 4 + 4, tb * 128:(tb + 1) * 128]
                eng = "act" if half == 0 else "dve"

                def ev(e, ps=ps, dst=dst, eng=eng):
                    src = ps[:].rearrange("p (c t) -> p c t", c=4)
                    if eng == "act":
                        return e.copy(out=dst, in_=src)
                    return e.tensor_copy(out=dst, in_=src)
                op(eng, ev, r=[pk], w=[("xT", tb // 4)])

    def store_x(self, stack):
        op = self.op
        xo = [self.alloc(stack, "xout%d" % i, [128, D], F32) for i in range(2)]
        for tb in range(NB):
            st = xo[tb % 2]
            sk = "xout%d" % (tb % 2)
            for half in range(2):
                ps = self.ps[2 * (tb % 2) + half]
                pk = "ps%d" % (2 * (tb % 2) + half)

                def tr(e, ps=ps, half=half, tb=tb):
                    last = None
                    for i in range(4):
                        c = half * 4 + i
                        last = e.transpose(out=ps[:, i * 128:(i + 1) * 128],
                                           in_=self.xT[:, c, tb * 128:(tb + 1) * 128], identity=self.identf[:])
                    return last
                op("pe", tr, r=[("xT", tb // 4), "identf"], w=[pk])
                eng = "act" if half == 0 else "dve"

                def ev(e, ps=ps, st=st, half=half, eng=eng):
                    if eng == "act":
                        return e.copy(out=st[:, half * 512:(half + 1) * 512], in_=ps[:])
                    return e.tensor_copy(out=st[:, half * 512:(half + 1) * 512], in_=ps[:])
                op(eng, ev, r=[pk], w=[(sk, half)])
            op("sp", lambda e, st=st, tb=tb: e.dma_start(out=self.out_d[tb * 128:(tb + 1) * 128, :], in_=st[:]),
               r=[(sk, 0), (sk, 1)], dma=sk)


    def gcol(self, l, j, c):
        k = l * 32 + j * 8 + c
        return self.gT[:, k:k + 1]

    def rstd_from_ps(self, psb, pkey, rstd, rkey, n, eps):
        if self.use_ln:
            self.op("act", lambda e: e.activation(out=_ap(rstd), in_=psb[:], func=AF.Ln, bias=self.epsc(eps), scale=1.0 / n),
                    r=[pkey, "epsc"], w=[rkey])
            self.op("act", lambda e: e.activation(out=_ap(rstd), in_=_ap(rstd), func=AF.Exp, scale=-0.5), r=[rkey], w=[rkey])
            return
        self.op("act", lambda e: e.activation(out=_ap(rstd), in_=psb[:], func=AF.Sqrt, bias=self.epsc(eps), scale=1.0 / n),
                r=[pkey, "epsc"], w=[rkey])
        self.op("dve", lambda e: e.reciprocal(out=_ap(rstd), in_=_ap(rstd)), r=[rkey], w=[rkey])

    def epsc(self, eps):
        return self.epst[:, 0:1] if eps == NORM_EPS else self.epst[:, 1:2]

    def stats_xT(self, tt, sqh, psi, rstd, rkey):
        op = self.op
        cols = slice(tt * T, (tt + 1) * T)
        psb, pkey = self.ps[psi], "ps%d" % psi
        for hf in range(2):
            op("act", lambda e, hf=hf: e.activation(out=sqh[:], in_=self.xT[:, hf * 4:hf * 4 + 4, cols], func=AF.Square),
               r=[("xT", tt)], w=[("sqh", i) for i in range(4)])

            def mm(e, hf=hf):
                last = None
                for i in range(4):
                    last = e.matmul(psb[:], lhsT=self.onesb[:], rhs=sqh[:, i, :], start=(hf == 0 and i == 0),
                                    stop=(hf == 1 and i == 3))
                return last
            op("pe", mm, r=[("sqh", i) for i in range(4)] + ["onesb"], w=[pkey])
        self.rstd_from_ps(psb, pkey, rstd, rkey, float(D), NORM_EPS)

    def make_xb(self, tt, l, j, xb, xbkeys, rstd, rkey, eng="dve", gains=None):
        cols = slice(tt * T, (tt + 1) * T)
        for c in range(NC_):
            g = gains[:, c:c + 1] if gains is not None else self.gcol(l, j, c)
            self.op(eng, lambda e, c=c, g=g: e.scalar_tensor_tensor(out=xb[:, c, :], in0=self.xT[:, c, cols], scalar=g,
                                                                     in1=_ap(rstd), op0=ALU.mult, op1=ALU.mult),
                    r=[("xT", tt), rkey, "gT"], w=[xbkeys[c]])

    def postnorm(self, tt, l, j, srcs, sqh, rstd2, psi, rkey="rstd2"):
        op = self.op
        cols = slice(tt * T, (tt + 1) * T)
        psb, pkey = self.ps[psi], "ps%d" % psi
        for hf in range(2):
            for i in range(4):
                ap, key = srcs[hf * 4 + i]
                key = key if isinstance(key, list) else [key]
                in_psum = isinstance(key[0], str) and key[0].startswith("ps")
                if in_psum or i % 2 == 0:
                    op("act", lambda e, ap=ap, i=i: e.activation(out=sqh[:, i, :], in_=ap, func=AF.Square),
                       r=key, w=[("sqh", i)])
                else:
                    op("dve", lambda e, ap=ap, i=i: e.tensor_tensor(out=sqh[:, i, :], in0=ap, in1=ap, op=ALU.mult),
                       r=key, w=[("sqh", i)])

            def mm(e, hf=hf):
                last = None
                for i in range(4):
                    last = e.matmul(psb[:], lhsT=self.onesb[:], rhs=sqh[:, i, :], start=(hf == 0 and i == 0),
                                    stop=(hf == 1 and i == 3))
                return last
            op("pe", mm, r=[("sqh", i) for i in range(4)] + ["onesb"], w=[pkey] + [("sqh", i) for i in range(4)])
        self.rstd_from_ps(psb, pkey, rstd2, rkey, float(D), NORM_EPS)
        for c in range(NC_):
            ap, key = srcs[c]
            key = key if isinstance(key, list) else [key]
            op("dve", lambda e, ap=ap, c=c: e.scalar_tensor_tensor(out=self.ptmp[:], in0=ap, scalar=self.gcol(l, j, c),
                                                                    in1=rstd2[:], op0=ALU.mult, op1=ALU.mult),
               r=key + [rkey, "gT"], w=["ptmp"])
            op("dve", lambda e, c=c: e.tensor_tensor(out=self.xT[:, c, cols], in0=self.xT[:, c, cols], in1=self.ptmp[:],
                                                     op=ALU.add),
               r=["ptmp", ("xT", tt)], w=[("xT", tt)])

    def precast_ops(self):
        ops = []
        for l in range(DEPTH):
            for jj in range(NFF // 2):
                def f(l=l, jj=jj):
                    def dm(e):
                        dst = self.wupT_d[l, jj].rearrange("p (c n) -> p c n", c=NC_)
                        a = e.dma_start(out=dst[:, :, 0:256],
                                        in_=self.wup_d[l, :, jj * 256:(jj + 1) * 256].rearrange("(c p) n -> p c n", p=128))
                        b_ = e.dma_start(out=dst[:, :, 256:512],
                                         in_=self.wup_d[l, :, DFF + jj * 256:DFF + (jj + 1) * 256].rearrange("(c p) n -> p c n", p=128))
                        return [a, b_]
                    self.op("pool", dm, w=[("pc", l, "u", jj)], dma="pc%d" % l, ndma=2, insure=False)
                ops.append(f)
            for half in range(2):
                def f(l=l, half=half):
                    last = (half == 1)
                    self.op("pool", lambda e: e.dma_start(
                        out=self.wdnT_d[l, half],
                        in_=self.wdn_d[l, :, half * 512:(half + 1) * 512].rearrange("(j p) n -> j p n", p=128)),
                        w=[("pc", l, "d", half)] + ([("pcdone", l)] if last else []), dma="pc%d" % l, insure=False)
                ops.append(f)
        return ops

    def precast_take(self, k):
        if self.pc_list is None:
            self.pc_list = self.precast_ops()
        for _ in range(k):
            if self.pc_list:
                self.pc_list.pop(0)()

    def ffn(self, stack, l):
        op = self.op
        self.precast_take(10 ** 6)
        A = lambda n, s, d: self.alloc(stack, n, s, d)
        xb = A("f_xb", [128, NC_, T], BF16)
        sqh = A("f_sqh", [128, 4, T], BF16)
        fTa = A("f_fTa", [128, 4, T], F32)
        aT = A("f_aT", [128, NFF, T], BF16)
        wu = [A("f_wu%d" % i, [128, NC_, 512], BF16) for i in range(2)]
        wd = [A("f_wd%d" % i, [128, 512], BF16) for i in range(4)]
        Cb = [A("f_C%d" % i, [128, T], F32) for i in range(2)]
        rstd1 = A("f_rstd1", [128, T], F32)
        rstd2 = A("f_rstd2", [128, T], F32)
        self.ptmp = A("f_ptmp", [128, T], F32)
        halo = A("f_halo", [128, NFF, 2], F32)
        xbkeys = [("xb", c) for c in range(NC_)]
        wcol = lambda k, j: self.convw[:, l * 66 + k * 22 + j: l * 66 + k * 22 + j + 1]
        bcol = lambda j: self.convb[:, l * 22 + j: l * 22 + j + 1]
        wdi = 0
        self.stats_xT(0, sqh, 4, rstd1, "rstd1")
        self.make_xb(0, l, 2, xb, xbkeys, rstd1, "rstd1")
        for tt in range(NT):
            cols = slice(tt * T, (tt + 1) * T)
            for jj in range(NFF // 2):
                slot = wu[jj % 2]
                sk = "wu%d" % (jj % 2)
                op("sp", lambda e, slot=slot, jj=jj: e.dma_start(out=slot[:].rearrange("p c n -> p (c n)"), in_=self.wupT_d[l, jj]),
                   r=[("pcdone", l)], w=[sk], dma=sk)
                for s in range(2):
                    j = 2 * jj + s
                    gi, ui = j % 2, 2 + j % 2
                    G, U = self.ps[gi], self.ps[ui]
                    gk, uk = "ps%d" % gi, "ps%d" % ui
                    C = Cb[j % 2]
                    ck = "C%d" % (j % 2)

                    def mmg(e, slot=slot, s=s, G=G):
                        last = None
                        for c in range(NC_):
                            last = e.matmul(G[:], lhsT=slot[:, c, s * 128:(s + 1) * 128], rhs=xb[:, c, :],
                                            start=(c == 0), stop=(c == NC_ - 1))
                        return last

                    def mmu(e, slot=slot, s=s, U=U):
                        last = None
                        for c in range(NC_):
                            last = e.matmul(U[:], lhsT=slot[:, c, 256 + s * 128:256 + (s + 1) * 128], rhs=xb[:, c, :],
                                            start=(c == 0), stop=(c == NC_ - 1))
                        return last
                    op("pe", mmg, r=[sk] + xbkeys, w=[gk])
                    op("pe", mmu, r=[sk] + xbkeys, w=[uk])
                    op("act", lambda e, C=C, G=G, j=j: e.activation(out=C[:], in_=G[:], func=AF.Identity,
                                                                    bias=bcol(j), scale=wcol(2, j)),
                       r=[gk, "convw", "convb"], w=[ck])
                    op("dve", lambda e, C=C, G=G, j=j: e.scalar_tensor_tensor(out=C[:, 1:T], in0=G[:, 0:T - 1], scalar=wcol(1, j),
                                                                              in1=C[:, 1:T], op0=ALU.mult, op1=ALU.add),
                       r=[gk, ck, "convw"], w=[ck])
                    op("dve", lambda e, C=C, G=G, j=j: e.scalar_tensor_tensor(out=C[:, 2:T], in0=G[:, 0:T - 2], scalar=wcol(0, j),
                                                                              in1=C[:, 2:T], op0=ALU.mult, op1=ALU.add),
                       r=[gk, ck, "convw"], w=[ck])
                    if tt > 0:
                        op("dve", lambda e, C=C, j=j: e.scalar_tensor_tensor(out=C[:, 0:2], in0=halo[:, j, 0:2], scalar=wcol(0, j),
                                                                             in1=C[:, 0:2], op0=ALU.mult, op1=ALU.add),
                           r=[("halo", j), ck, "convw"], w=[ck])
                        op("dve", lambda e, C=C, j=j: e.scalar_tensor_tensor(out=C[:, 0:1], in0=halo[:, j, 1:2], scalar=wcol(1, j),
                                                                             in1=C[:, 0:1], op0=ALU.mult, op1=ALU.add),
                           r=[("halo", j), ck, "convw"], w=[ck])
                    if tt < NT - 1:
                        op("dve", lambda e, G=G, j=j: e.tensor_copy(out=halo[:, j, :], in_=G[:, T - 2:T]),
                           r=[gk], w=[("halo", j)])
                    op("act", lambda e, C=C: e.activation(out=C[:], in_=C[:], func=AF.Gelu_apprx_tanh), r=[ck], w=[ck])
                    op("dve", lambda e, C=C, U=U, j=j: e.tensor_tensor(out=aT[:, j, :], in0=C[:], in1=U[:], op=ALU.mult),
                       r=[ck, uk], w=[("aT", j)])
                if tt + 1 < NT and jj in (4, 5, 6):
                    nt_ = tt + 1
                    ncols = slice(nt_ * T, (nt_ + 1) * T)
                    p4 = self.ps[4]

                    def sq_half(hf):
                        op("act", lambda e, hf=hf, ncols=ncols: e.activation(out=sqh[:], in_=self.xT[:, hf * 4:hf * 4 + 4, ncols],
                                                                           func=AF.Square),
                           r=[("xT", nt_)], w=[("sqh", i) for i in range(4)])

                    def mm_half(hf):
                        def mm(e, hf=hf):
                            last = None
                            for i in range(4):
                                last = e.matmul(p4[:], lhsT=self.onesb[:], rhs=sqh[:, i, :], start=(hf == 0 and i == 0),
                                                stop=(hf == 1 and i == 3))
                            return last
                        op("pe", mm, r=[("sqh", i) for i in range(4)] + ["onesb"], w=["ps4"])
                    if jj == 4:
                        sq_half(0)
                    elif jj == 5:
                        mm_half(0)
                        sq_half(1)
                    else:
                        mm_half(1)
                        self.rstd_from_ps(p4, "ps4", rstd1, "rstd1", float(D), NORM_EPS)
            if tt + 1 < NT:
                self.make_xb(tt + 1, l, 2, xb, xbkeys, rstd1, "rstd1")
            for half in range(2):
                for j in range(NFF):
                    slot = wd[wdi % 4]
                    sk = "wd%d" % (wdi % 4)
                    wdi += 1
                    op("sp", lambda e, slot=slot, j=j, half=half: e.dma_start(out=slot[:], in_=self.wdnT_d[l, half, j]),
                       r=[("pcdone", l)], w=[sk], dma=sk)

                    def mmd(e, slot=slot, j=j, half=half):
                        last = None
                        for o in range(4):
                            last = e.matmul(self.ps[half * 4 + o][:], lhsT=slot[:, o * 128:(o + 1) * 128], rhs=aT[:, j, :],
                                            start=(j == 0), stop=(j == NFF - 1))
                        return last
                    op("pe", mmd, r=[sk, ("aT", j)], w=["ps%d" % (half * 4 + o) for o in range(4)])
                if half == 0:
                    for o in range(4):
                        op("act", lambda e, o=o: e.copy(out=fTa[:, o, :], in_=self.ps[o][:]), r=["ps%d" % o], w=[("fTa", o)])
            srcs = [(fTa[:, o, :], ("fTa", o)) for o in range(4)] + [(self.ps[4 + o][:], "ps%d" % (4 + o)) for o in range(4)]
            self.postnorm(tt, l, 3, srcs, sqh, rstd2, 0)

    def bcast_row(self, row_ap, n, dst_ap, dkey, rkey, scale=None):
        ps = self.ps[1]
        self.op("pe", lambda e: e.matmul(ps[:, 0:n], lhsT=self.onesf[0:1, :], rhs=row_ap, start=True, stop=True),
                r=[rkey, "onesf"], w=["ps1"])
        if scale is None:
            self.op("dve", lambda e: e.tensor_copy(out=dst_ap, in_=ps[:, 0:n]), r=["ps1"], w=[dkey])
        else:
            self.op("act", lambda e: e.activation(out=dst_ap, in_=ps[:, 0:n], func=AF.Copy, scale=float(scale)),
                    r=["ps1"], w=[dkey])

    def setup_A(self, stack, tmp):
        op = self.op
        A = lambda n, s, d: self.alloc(stack, n, s, d)
        TA = lambda n, s, d: self.alloc(tmp, n, s, d)
        ii = TA("iotaI", [128, 35], mybir.dt.int32)
        fi = TA("iotaF", [128, 35], F32)
        lamr = TA("lamr", [1, 2, 2, 2, HD], F32)
        prod = TA("lamprod", [1, 4, HD], F32)
        red = TA("lamred", [1, 4], F32)
        lrow = TA("lamrow", [1, 2], F32)
        sgrow = TA("sgrow", [1, NA * 128], F32)
        op("pool", lambda e: e.iota(ii[:], pattern=[[-128, 35]], base=384, channel_multiplier=1), w=["iotaI"])
        op("dve", lambda e: e.tensor_copy(out=fi[:], in_=ii[:]), r=["iotaI"], w=["iotaF"])
        sl = alibi_slopes(8)
        for h in range(8):
            W = self.widthA(h)
            op("dve", lambda e, h=h, W=W: e.tensor_scalar(out=self.tabA[:, h, :], in0=fi[:], scalar1=float(W // 2),
                                                          scalar2=float(sl[h]), op0=ALU.subtract, op1=ALU.mult),
               r=["iotaF"], w=["tabA"])
        op("sp", lambda e: e.dma_start(out=lamr[:].rearrange("o l a b d -> o (l a b d)"), in_=self.lam_d), w=["lamr"], dma="lamr")
        op("dve", lambda e: e.tensor_tensor(out=prod[:].rearrange("o (l a) d -> o l a d", l=2), in0=lamr[:, :, :, 0, :],
                                            in1=lamr[:, :, :, 1, :], op=ALU.mult), r=["lamr"], w=["lamprod"])
        op("dve", lambda e: e.reduce_sum(out=red[:], in_=prod[:], axis=AX.X), r=["lamprod"], w=["lamred"])
        op("act", lambda e: e.activation(out=red[:], in_=red[:], func=AF.Exp), r=["lamred"], w=["lamred"])
        rv = red[:].rearrange("o (l a) -> o l a", l=2)
        op("dve", lambda e: e.tensor_tensor(out=lrow[:], in0=rv[:, :, 1], in1=rv[:, :, 0], op=ALU.subtract),
           r=["lamred"], w=["lamrow"])
        for l in range(NA):
            li = 0.8 - 0.6 * math.exp(-0.3 * l)
            op("dve", lambda e, l=l, li=li: e.tensor_scalar(out=lrow[:, l:l + 1], in0=lrow[:, l:l + 1], scalar1=float(li),
                                                            scalar2=None, op0=ALU.subtract), r=["lamrow"], w=["lamrow"])
        self.bcast_row(lrow[:], 2, self.nlam[:], "nlam", "lamrow")
        op("sp", lambda e: e.dma_start(out=sgrow[:], in_=self.subln_d), w=["sgrow"], dma="sgrow")
        for l in range(NA):
            li = 0.8 - 0.6 * math.exp(-0.3 * l)
            self.bcast_row(sgrow[:, l * 128:(l + 1) * 128], 128, self.sgbc[:, l, :], ("sgbc", l), "sgrow", scale=1.0 - li)

    def widthA(self, h):
        s = alibi_slopes(8)[h]
        return int(min(512, max(128, 64.0 / s)))

    def attnA(self, stack, l):
        op = self.op
        A = lambda n, s, d: self.alloc(stack, n, s, d)
        rstdA = A("a_rstd", [128, S], F32)
        sqh = A("a_sqh", [128, 4, T], BF16)
        xbs = [A("a_xb%d" % i, [128, NC_, T], BF16) for i in range(2)]
        wqkv = A("a_wqkv", [128, NC_, 384], BF16)
        KT = A("a_KT", [128, S], BF16)
        Va = A("a_V", [128, NB, 130], BF16)
        QTs = [A("a_QT%d" % i, [128, T], BF16) for i in range(2)]
        PTs = [A("a_PT%d" % i, [128, 2, T], BF16) for i in range(3)]
        Osb = A("a_Osb", [128, 1032], F32)
        rz = A("a_rz", [128, 2, 4], F32)
        ssq = A("a_ssq", [128, 2, 4], F32)
        on = A("a_on", [128, 4, 128], BF16)
        OTst = [A("a_OT%d" % i, [128, T], BF16) for i in range(1)]
        ps5b = self.ps[5][:].bitcast(BF16)
        scr = sqh[:].rearrange("p a t -> p (a t)").bitcast(F32)
        t0 = scr[:, 0:512].rearrange("p (b e) -> p b e", b=4)
        osb = scr[:, 512:1024].rearrange("p (b e) -> p b e", b=4)
        Ov = Osb[:].rearrange("p (s b e) -> p s b e", s=2, b=4)

        op("pool", lambda e: e.memset(Va[:, :, 128:130], 1.0), w=[("V", t_) for t_ in range(NT)])
        for tt in range(NT):
            self.stats_xT(tt, sqh, 5, rstdA[:, tt * T:(tt + 1) * T], ("rstdA", tt))
        p5 = self.ps[5]
        seq = [(h, tt) for h in range(8) for tt in range(NT)]

        def ldw(h):
            def f(e):
                out = []
                for i in range(3):
                    out.append(e.dma_start(out=wqkv[:, :, i * 128:(i + 1) * 128],
                                           in_=self.wqkv_d[l, :, i * D + h * 128:i * D + (h + 1) * 128].rearrange("(c p) n -> p c n", p=128)))
                return out
            op("pool", f, w=["wqkv"], dma="wqkv", ndma=3)
            self.precast_take(7)

        def emit_xb(n):
            h, tt = seq[n]
            xb = xbs[n % 2]
            xbkeys = [("xb%d" % (n % 2), c) for c in range(NC_)]
            self.make_xb(tt, l, 0, xb, xbkeys, rstdA[:, tt * T:(tt + 1) * T], ("rstdA", tt), eng="dve")

        def qkv_parts(n):
            h, tt = seq[n]
            cols = slice(tt * T, (tt + 1) * T)
            xb = xbs[n % 2]
            xbkeys = [("xb%d" % (n % 2), c) for c in range(NC_)]
            QT = QTs[n % 2]
            qk = "QT%d" % (n % 2)

            def part_q():
                def mmq(e):
                    last = None
                    for c in range(NC_):
                        last = e.matmul(p5[:], lhsT=wqkv[:, c, 0:128], rhs=xb[:, c, :], start=(c == 0), stop=(c == NC_ - 1))
                    return last
                op("pe", mmq, r=["wqkv"] + xbkeys, w=["ps5"])
                op("dve", lambda e: e.tensor_copy(out=QT[:], in_=p5[:]), r=["ps5"], w=[qk])

            def part_k():
                def mmk(e):
                    last = None
                    for c in range(NC_):
                        last = e.matmul(p5[:], lhsT=wqkv[:, c, 128:256], rhs=xb[:, c, :], start=(c == 0), stop=(c == NC_ - 1))
                    return last
                op("pe", mmk, r=["wqkv"] + xbkeys, w=["ps5"])
                op("dve", lambda e: e.tensor_copy(out=KT[:, cols], in_=p5[:]), r=["ps5"], w=[("KT", tt)])

            def part_v():
                def mmv(e):
                    last = None
                    for blk in range(4):
                        for c in range(NC_):
                            last = e.matmul(p5[:, blk * 128:(blk + 1) * 128], lhsT=xb[:, c, blk * 128:(blk + 1) * 128],
                                            rhs=wqkv[:, c, 256:384], start=(c == 0), stop=(c == NC_ - 1))
                    return last
                op("pe", mmv, r=["wqkv"] + xbkeys, w=["ps5"])
                op("dve", lambda e: e.tensor_copy(out=Va[:, tt * 4:(tt + 1) * 4, 0:128],
                                                  in_=p5[:].rearrange("p (b e) -> p b e", b=4)),
                   r=["ps5"], w=[("V", tt)])
            return [part_q, part_k, part_v]

        def emit_stream(n, fillers):
            h, tt = seq[n]
            W = self.widthA(h)
            slope = alibi_slopes(8)[h]
            QT = QTs[n % 2]
            qk = "QT%d" % (n % 2)
            nk = 4 * tt + 4
            kts = [kt for kt in range(nk) if slope * (tt * 512 - (kt * 128 + 127)) < 160.0]
            first_kt = kts[0]
            pvs = []
            for ii, kt in enumerate(kts):
                diag = kt >= 4 * tt
                kb = kt - 4 * tt if diag else 0
                qlo = kb * 128
                b0 = 0 if ii % 2 == 0 else 6
                sks = ["ps%d" % b0, "ps%d" % (b0 + 1)]
                PT = PTs[ii % 3]
                pk = "PT%d" % (ii % 3)

                def mms(e, kt=kt, qlo=qlo, b0=b0):
                    last = None
                    for sub in range(2):
                        last = e.matmul(self.ps[b0 + sub][:, qlo:T], lhsT=KT[sub * 64:(sub + 1) * 64, kt * 128:(kt + 1) * 128],
                                        rhs=QT[sub * 64:(sub + 1) * 64, qlo:T], start=True, stop=True)
                    return last
                op("pe", mms, r=[("KT", kt // 4), qk], w=sks)
                if ii >= 2:
                    f_, r_, w_ = pvs[ii - 2]
                    op("pe", f_, r=r_, w=w_)
                if ii in fillers:
                    fillers[ii]()

                def ex(e, kt=kt, qlo=qlo, b0=b0, PT=PT):
                    last = None
                    for gs in range(0, T, W):
                        lo = max(gs, qlo)
                        if lo >= gs + W:
                            continue
                        j = 4 * tt + gs // 128 - kt + 3
                        last = e.activation(out=PT[:, :, lo:gs + W], in_=self.psall[:, b0:b0 + 2, lo:gs + W], func=AF.Exp,
                                            bias=self.tabA[:, h, j:j + 1], scale=0.125)
                    return last
                op("act", ex, r=sks + ["tabA"], w=[pk])
                if diag:
                    op("dve", lambda e, PT=PT, kb=kb: e.tensor_tensor(
                        out=PT[:, :, kb * 128:(kb + 1) * 128], in0=PT[:, :, kb * 128:(kb + 1) * 128],
                        in1=self.mcur[:].unsqueeze(1).to_broadcast([128, 2, 128]), op=ALU.mult), r=[pk, "mcur"], w=[pk])

                def pv(e, kt=kt, kb=kb, PT=PT):
                    last = None
                    for sub in range(2):
                        for qb in range(kb, 4):
                            a_ = sub * 4 + qb
                            O = self.ps[2 + a_ // 3][:, (a_ % 3) * 129:(a_ % 3) * 129 + 129]
                            last = e.matmul(O, lhsT=PT[:, sub, qb * 128:(qb + 1) * 128], rhs=Va[:, kt, 0:129],
                                            start=(kt == first_kt and a_ % 3 == 0), stop=(kt == 4 * tt + qb), skip_group_check=True)
                    return last
                pvs.append((pv, [pk, ("V", kt // 4)], ["ps2", "ps3", "ps4"]))
            for kk in range(max(0, len(kts) - 2), len(kts)):
                f_, r_, w_ = pvs[kk]
                op("pe", f_, r=r_, w=w_)
            for b_, wd_ in ((0, 387), (1, 387), (2, 258)):
                op("dve", lambda e, b_=b_, wd_=wd_: e.tensor_copy(out=Osb[:, b_ * 387:b_ * 387 + wd_], in_=self.ps[2 + b_][:, 0:wd_]),
                   r=["ps%d" % (2 + b_)], w=[("Osb", b_)])

        def emit_norm(n):
            h, tt = seq[n]
            cols = slice(tt * T, (tt + 1) * T)
            OTs = OTst[0]
            otk = "OTst0"
            okeys = [("Osb", b_) for b_ in range(3)]
            zview = Ov[:, :, :, 128]
            op("dve", lambda e: e.reciprocal(out=rz[:], in_=zview), r=okeys, w=["rz"])
            op("dve", lambda e: e.tensor_scalar(out=rz[:, 1, :], in0=rz[:, 1, :], scalar1=self.nlam[:, l:l + 1], scalar2=None,
                                                op0=ALU.mult), r=["rz", "nlam"], w=["rz"])
            o0 = Ov[:, 0, :, 0:128]
            o1 = Ov[:, 1, :, 0:128]
            op("dve", lambda e: e.tensor_tensor(out=t0, in0=o0, in1=rz[:, 0, :].unsqueeze(2).to_broadcast([128, 4, 128]),
                                                op=ALU.mult), r=okeys + ["rz"], w=["t0"])
            op("dve", lambda e: e.tensor_tensor(out=osb, in0=o1, in1=rz[:, 1, :].unsqueeze(2).to_broadcast([128, 4, 128]),
                                                op=ALU.mult), r=okeys + ["rz"], w=["osb"])
            op("dve", lambda e: e.tensor_tensor(out=osb, in0=osb, in1=t0, op=ALU.add), r=["osb", "t0"], w=["osb"])
            op("dve", lambda e: e.tensor_tensor(out=t0, in0=osb, in1=osb, op=ALU.mult), r=["osb"], w=["t0"])
            op("dve", lambda e: e.reduce_sum(out=ssq[:, 0, :], in_=t0, axis=AX.X), r=["t0"], w=["ssq"])
            op("act", lambda e: e.activation(out=ssq[:, 1, :], in_=ssq[:, 0, :], func=AF.Ln, bias=self.epsc(SUBLN_EPS),
                                             scale=1.0 / 128), r=["ssq", "epsc"], w=["ssq"])
            op("act", lambda e: e.activation(out=ssq[:, 1, :], in_=ssq[:, 1, :], func=AF.Exp, scale=-0.5), r=["ssq"], w=["ssq"])
            op("dve", lambda e: e.tensor_tensor(out=osb, in0=osb, in1=ssq[:, 1, :].unsqueeze(2).to_broadcast([128, 4, 128]),
                                                op=ALU.mult), r=["osb", "ssq"], w=["osb"])
            op("dve", lambda e: e.tensor_tensor(out=on[:], in0=osb, in1=self.sgbc[:, l, :].unsqueeze(1).to_broadcast([128, 4, 128]),
                                                op=ALU.mult), r=["osb", ("sgbc", l)], w=["on"])

            def tail():
                def trs(e):
                    last = None
                    for qb in range(4):
                        last = e.transpose(out=ps5b[:, qb * 128:(qb + 1) * 128], in_=on[:, qb, :], identity=self.identb[:])
                    return last
                op("pe", trs, r=["on", "identb"], w=["ps5"])
                op("dve", lambda e: e.tensor_copy(out=OTs[:], in_=ps5b[:, 0:T]), r=["ps5"], w=[otk])
                op("sp", lambda e: e.dma_start(out=self.ot_d[h, :, cols], in_=OTs[:]), r=[otk], w=[("otd", h, tt)], dma=otk)
            return tail

        N = len(seq)
        self.use_ln = True
        ldw(0)
        emit_xb(0)
        for part in qkv_parts(0):
            part()
        emit_xb(1)
        tail = None
        for n in range(N):
            h, tt = seq[n]
            nitems = len([kt for kt in range(4 * tt + 4) if alibi_slopes(8)[h] * (tt * 512 - (kt * 128 + 127)) < 160.0])
            fillers = {}
            if n + 1 < N:
                k0 = 1 if nitems > 4 else 0
                if seq[n + 1][0] != h:
                    ldw(seq[n + 1][0])
                    k0 = 8 if nitems >= 12 else 3
                for i_, part in enumerate(qkv_parts(n + 1)):
                    fillers[k0 + i_] = part
            if tail is not None:
                ts_ = min(6, nitems - 1)
                while ts_ in fillers:
                    ts_ += 1
                if ts_ >= nitems:
                    ts_ = nitems - 1
                    prev = fillers.get(ts_)
                    t_ = tail
                    fillers[ts_] = (lambda a=prev, b=t_: (a(), b())) if prev is not None else t_
                else:
                    fillers[ts_] = tail
            if n + 2 < N:
                xs_ = min(5, nitems - 1)
                prev_ = fillers.get(xs_)
                fillers[xs_] = (lambda a=prev_, m=n + 2: ((a() if a is not None else None), emit_xb(m)))
            emit_stream(n, fillers)
            tail = emit_norm(n)
        tail()
        self.use_ln = False

    def woA(self, stack, l):
        op = self.op
        A = lambda n, s, d: self.alloc(stack, n, s, d)
        wo = A("o_wo", [128, NC_, D], BF16)
        OTin = [A("o_OT%d" % i, [128, NC_, T], BF16) for i in range(2)]
        fTa = A("o_fTa", [128, 4, T], F32)
        fTb = A("o_fTb", [128, 4, T], F32)
        sqh = A("o_sqh", [128, 4, T], BF16)
        rstd2 = A("o_rstd2", [128, T], F32)
        self.ptmp = A("o_ptmp", [128, T], F32)

        def ldw(e):
            out = []
            for i in range(4):
                out.append(e.dma_start(out=wo[:, 2 * i:2 * i + 2, :],
                                       in_=self.woa_d[l, i * 256:(i + 1) * 256, :].rearrange("(c p) n -> p c n", p=128)))
            return out
        op("pool", ldw, w=["wo"], dma="wo", ndma=4)
        for tt in range(NT):
            cols = slice(tt * T, (tt + 1) * T)
            OT = OTin[tt % 2]
            ok = "OTin%d" % (tt % 2)
            op("sp", lambda e, OT=OT, cols=cols: [e.dma_start(out=OT[:, h, :], in_=self.ot_d[h, :, cols]) for h in range(8)],
               r=[("otd", h, tt) for h in range(8)], w=[ok], dma=ok, ndma=8)
            for half in range(2):
                for o in range(4):
                    oc = half * 4 + o
                    bank = self.ps[half * 4 + o]

                    def mm(e, OT=OT, oc=oc, bank=bank):
                        last = None
                        for hh in range(NC_):
                            last = e.matmul(bank[:], lhsT=wo[:, hh, oc * 128:(oc + 1) * 128], rhs=OT[:, hh, :],
                                            start=(hh == 0), stop=(hh == NC_ - 1))
                        return last
                    op("pe", mm, r=["wo", ok], w=["ps%d" % (half * 4 + o)])
                if half == 0:
                    for o in range(4):
                        op("act", lambda e, o=o: e.copy(out=fTa[:, o, :], in_=self.ps[o][:]), r=["ps%d" % o], w=[("fTa", o)])
                else:
                    for o in range(4):
                        op("dve", lambda e, o=o: e.tensor_copy(out=fTb[:, o, :], in_=self.ps[4 + o][:]), r=["ps%d" % (4 + o)],
                           w=[("fTb", o)])
            srcs = [(fTa[:, o, :], ("fTa", o)) for o in range(4)] + [(fTb[:, o, :], ("fTb", o)) for o in range(4)]
            self.postnorm(tt, l, 1, srcs, sqh, rstd2, 0)


    def setup_B(self, stack, tmp):
        op = self.op
        A = lambda n, s, d: self.alloc(stack, n, s, d)
        TA = lambda n, s, d: self.alloc(tmp, n, s, d)
        pi = TA("posI", [128, 1], mybir.dt.int32)
        pf = TA("posF", [128, 1], F32)
        bvrow = TA("bvrow", [1, 128], F32)
        skrow = TA("skrow", [1, 32], F32)
        skbc = TA("skbc", [128, 32], F32)
        sl = alibi_slopes(16)
        for j in range(2):
            self.load_cols(self.bqb_d[j].rearrange("two g d -> g two d"), 8, self.bqT[:, j, :], 0, st_view=(8, 2, 64))
        self.load_cols(self.bkv_d, 2, self.bkc, 0)
        op("sp", lambda e: e.dma_start(out=bvrow[:], in_=self.bkv_d[1:2, :]), w=["bvrow"], dma="bvrow")
        self.bcast_row(bvrow[:], 128, self.bvbc[:], "bvbc", "bvrow")
        op("pool", lambda e: e.iota(pi[:], pattern=[[0, 1]], base=-64, channel_multiplier=1), w=["posI"])
        op("dve", lambda e: e.tensor_copy(out=pf[:], in_=pi[:]), r=["posI"], w=["posF"])
        for h in range(16):
            op("dve", lambda e, h=h: e.tensor_scalar(out=self.biasB[:, 1, h:h + 1], in0=pf[:], scalar1=float(sl[h]), scalar2=None,
                                                     op0=ALU.mult), r=["posF"], w=["biasB"])
            op("dve", lambda e, h=h: e.tensor_scalar(out=self.biasB[:, 0, h:h + 1], in0=pf[:], scalar1=float(sl[h]),
                                                     scalar2=float(-128.0 * sl[h]), op0=ALU.mult, op1=ALU.add),
               r=["posF"], w=["biasB"])
        op("sp", lambda e: e.dma_start(out=skrow[:], in_=self.sinks_d), w=["skrow"], dma="skrow")
        self.bcast_row(skrow[:], 32, skbc[:], "skbc", "skrow")
        for j in range(2):
            op("dve", lambda e, j=j: e.tensor_tensor(out=self.sinkf[:, j, :], in0=skbc[:, j * 16:(j + 1) * 16], in1=self.biasB[:, 1, :],
                                                     op=ALU.add), r=["skbc", "biasB"], w=["sinkf"])
        op("act", lambda e: e.activation(out=self.sinkf[:], in_=self.sinkf[:], func=AF.Exp), r=["sinkf"], w=["sinkf"])

    def attnB(self, stack, l):
        op = self.op
        j = l - NA
        self.use_ln = True
        A = lambda n, s, d: self.alloc(stack, n, s, d)
        KT = A("b_KT", [128, S], BF16)
        Vb = A("b_V", [128, NB, 2, 66], BF16)
        wqs = [A("b_wq%d" % i, [128, NC_, 2, 128], BF16) for i in range(2)]
        wos = [A("b_wo%d" % i, [128, 512], BF16) for i in range(4)]
        xb = A("b_xb", [128, NC_, T], BF16)
        sqh = A("b_sqh", [128, 4, T], BF16)
        QT = A("b_QT", [128, 8, T], BF16)
        PTs = [A("b_PT%d" % i, [128, 2, 2, 128], BF16) for i in range(2)]
        On = A("b_On", [128, 16, HD], BF16)
        OT = A("b_OT", [128, NC_, T], BF16)
        fTa = QT[:].rearrange("p g t -> p (g t)").bitcast(F32).rearrange("p (o t) -> p o t", o=4)
        rstd1 = A("b_rstd1", [128, T], F32)
        self.ptmp = A("b_ptmp", [128, T], F32)
        den = A("b_den", [128, 16], F32)
        xbkeys = [("xb", c) for c in range(NC_)]
        ps4b = self.ps[4][:].bitcast(BF16)
        obank = lambda hd: (2, 3, 7)[hd // 7]
        ooff = lambda hd: (hd % 7) * 65
        if l == NA:
            wkv = A("b_wkv", [128, NC_, 256], BF16)
            op("pool", lambda e: e.dma_start(out=wkv[:], in_=self.wkv_d.rearrange("(c p) n -> p c n", p=128)), w=["wkv"], dma="wkv")
            op("pool", lambda e: e.memset(Vb[:, :, :, 64:66], 1.0), w=[("V", t_) for t_ in range(NT)])
        else:
            op("sp", lambda e: e.dma_start(out=KT[:], in_=self.kvT_d), r=[("kvTd", t_) for t_ in range(NT)],
               w=[("KT", t_) for t_ in range(NT)], dma="KTld")
            op("sp", lambda e: e.dma_start(out=Vb[:].rearrange("p a b c -> p (a b c)"), in_=self.kvV_d),
               r=[("kvVd", t_) for t_ in range(NT)], w=[("V", t_) for t_ in range(NT)], dma="Vld")
        wqi = 0
        woi = 0
        p4 = self.ps[4]

        def prep(tt):
            cols = slice(tt * T, (tt + 1) * T)
            self.stats_xT(tt, sqh, 4, rstd1, "rstd1")
            if l == NA:
                self.make_xb(tt, l, 0, xb, xbkeys, rstd1, "rstd1", gains=self.kvg)

                def mmk(e):
                    last = None
                    for c in range(NC_):
                        last = e.matmul(p4[:], lhsT=wkv[:, c, 0:128], rhs=xb[:, c, :], start=(c == 0), stop=(c == NC_ - 1))
                    return last
                op("pe", mmk, r=["wkv"] + xbkeys, w=["ps4"])
                op("act", lambda e: e.activation(out=KT[:, cols], in_=p4[:], func=AF.Identity, bias=self.bkc[:, 0:1]),
                   r=["ps4", "bkc"], w=[("KT", tt)])

                def mmv(e):
                    last = None
                    for blk in range(4):
                        for c in range(NC_):
                            last = e.matmul(p4[:, blk * 128:(blk + 1) * 128], lhsT=xb[:, c, blk * 128:(blk + 1) * 128],
                                            rhs=wkv[:, c, 128:256], start=(c == 0), stop=(c == NC_ - 1))
                    return last
                op("pe", mmv, r=["wkv"] + xbkeys, w=["ps4"])
                for blk in range(4):
                    op("dve", lambda e, blk=blk: e.tensor_tensor(
                        out=Vb[:, tt * 4 + blk, :, 0:64], in0=p4[:, blk * 128:(blk + 1) * 128].rearrange("p (k d) -> p k d", k=2),
                        in1=self.bvbc[:].rearrange("p (k d) -> p k d", k=2), op=ALU.add), r=["ps4", "bvbc"], w=[("V", tt)])
                op("sp", lambda e: e.dma_start(out=self.kvT_d[:, cols], in_=KT[:, cols]), r=[("KT", tt)],
                   w=[("kvTd", tt)], dma="kvst")
                op("sp", lambda e: e.dma_start(out=self.kvV_d[:, tt * 4 * 132:(tt + 1) * 4 * 132],
                                               in_=Vb[:, tt * 4:(tt + 1) * 4, :, :].rearrange("p a b c -> p (a b c)")),
                   r=[("V", tt)], w=[("kvVd", tt)], dma="kvst2")
            self.make_xb(tt, l, 0, xb, xbkeys, rstd1, "rstd1")

        prep(0)
        for tt in range(NT):
            cols = slice(tt * T, (tt + 1) * T)
            for g in range(8):
                if g % 2 == 0:
                    slot = wqs[wqi % 2]
                    sk = "wq%d" % (wqi % 2)
                    wqi += 1

                    def ldq(e, slot=slot, g=g):
                        a = e.dma_start(out=slot[:, :, 0, :],
                                        in_=self.wqb_d[j, :, g * 64:(g + 2) * 64].rearrange("(c p) n -> p c n", p=128))
                        b_ = e.dma_start(out=slot[:, :, 1, :],
                                         in_=self.wqb_d[j, :, 512 + g * 64:512 + (g + 2) * 64].rearrange("(c p) n -> p c n", p=128))
                        return [a, b_]
                    op("pool", ldq, w=[sk], dma=sk, ndma=2)
                pq = self.ps[5 + g % 2]
                pqk = "ps%d" % (5 + g % 2)

                def mmq(e, slot=slot, pq=pq, g=g):
                    last = None
                    for c in range(NC_):
                        o_ = (g % 2) * 64
                        e.matmul(pq[0:64, :], lhsT=slot[:, c, 0, o_:o_ + 64], rhs=xb[:, c, :],
                                 start=(c == 0), stop=(c == NC_ - 1), tile_position=(0, 0))
                        last = e.matmul(pq[64:128, :], lhsT=slot[:, c, 1, o_:o_ + 64], rhs=xb[:, c, :],
                                        start=(c == 0), stop=(c == NC_ - 1), tile_position=(0, 64))
                    return last
                op("pe", mmq, r=[sk] + xbkeys, w=[pqk])
                op("act", lambda e, g=g, pq=pq: e.activation(out=QT[:, g, :], in_=pq[:], func=AF.Identity, bias=self.bqT[:, j, g:g + 1]),
                   r=[pqk, "bqT"], w=[("QT", g)])
            if tt + 1 < NT:
                prep(tt + 1)
            tail = None
            for qb in range(4):
                gq = tt * 4 + qb
                kts = ([(0, gq - 1)] if gq > 0 else []) + [(1, gq)]
                started = set()
                pend = None
                for g in range(8):
                    sbanks = (0, 1) if g % 2 == 0 else (5, 6)
                    Sbs = [self.ps[sbanks[0]], self.ps[sbanks[1]]]
                    sks = ["ps%d" % sbanks[0], "ps%d" % sbanks[1]]
                    PT = PTs[g % 2]
                    pk = "bPT%d" % (g % 2)

                    def mms(e, g=g, Sbs=Sbs, qb=qb, kts=kts):
                        last = None
                        for (ki, kt) in kts:
                            for hh in range(2):
                                last = e.matmul(Sbs[hh][:, ki * 128:(ki + 1) * 128],
                                                lhsT=KT[hh * 64:(hh + 1) * 64, kt * 128:(kt + 1) * 128],
                                                rhs=QT[hh * 64:(hh + 1) * 64, g, qb * 128:(qb + 1) * 128], start=True, stop=True)
                        return last
                    op("pe", mms, r=[("KT", kt // 4) for (_, kt) in kts] + [("QT", g)], w=sks)
                    if pend is not None:
                        op("pe", pend[0], r=pend[1], w=pend[2])
                        pend = None
                    if g == 1 and tail is not None:
                        tail()
                        tail = None

                    def ex(e, g=g, Sbs=Sbs, PT=PT, kts=kts):
                        last = None
                        for (ki, kt) in kts:
                            for hh in range(2):
                                hd = g + 8 * hh
                                last = e.activation(out=PT[:, ki, hh, :], in_=Sbs[hh][:, ki * 128:(ki + 1) * 128],
                                                    func=AF.Exp, bias=self.biasB[:, ki, hd:hd + 1], scale=0.125)
                        return last
                    op("act", ex, r=sks + ["biasB"], w=[pk])

                    def msk(e, PT=PT, kts=kts):
                        last = None
                        for (ki, kt) in kts:
                            m = self.mprev if ki == 0 else self.mcur
                            last = e.tensor_tensor(out=PT[:, ki, :, :], in0=PT[:, ki, :, :],
                                                   in1=m[:].unsqueeze(1).to_broadcast([128, 2, 128]), op=ALU.mult)
                        return last
                    op("dve", msk, r=[pk, "mcur", "mprev"], w=[pk])
                    pvl = []
                    for hh in range(2):
                        hd = g + 8 * hh
                        bk_ = obank(hd)
                        for n_, (ki, kt) in enumerate(kts):
                            st = (n_ == 0) and (bk_ not in started)
                            if n_ == 0:
                                started.add(bk_)
                            pvl.append((hh, hd, bk_, ki, kt, st, n_ == len(kts) - 1))

                    def pv(e, PT=PT, pvl=pvl):
                        last = None
                        for (hh, hd, bk_, ki, kt, st, sp_) in pvl:
                            last = e.matmul(self.ps[bk_][:, ooff(hd):ooff(hd) + 65], lhsT=PT[:, ki, hh, :],
                                            rhs=Vb[:, kt, hh, 0:65], start=st, stop=sp_, skip_group_check=True)
                        return last
                    pend = (pv, [pk] + [("V", kt // 4) for (_, kt) in kts], ["ps2", "ps3", "ps7"])
                op("pe", pend[0], r=pend[1], w=pend[2])
                for bk_, h0, h1 in ((2, 0, 7), (3, 7, 14), (7, 14, 16)):
                    nh = h1 - h0
                    ov = self.ps[bk_][:, 0:nh * 65].rearrange("p (h e) -> p h e", e=65)
                    op("dve", lambda e, ov=ov, h0=h0, h1=h1: e.tensor_tensor(out=den[:, h0:h1], in0=ov[:, :, 64],
                                                                             in1=self.sinkf[:, j, h0:h1], op=ALU.add),
                       r=["ps%d" % bk_, "sinkf"], w=["den"])
                op("dve", lambda e: e.reciprocal(out=den[:], in_=den[:]), r=["den"], w=["den"])
                for bk_, h0, h1 in ((2, 0, 7), (3, 7, 14), (7, 14, 16)):
                    nh = h1 - h0
                    ov = self.ps[bk_][:, 0:nh * 65].rearrange("p (h e) -> p h e", e=65)
                    op("dve", lambda e, ov=ov, h0=h0, h1=h1, nh=nh: e.tensor_tensor(
                        out=On[:, h0:h1, :], in0=ov[:, :, 0:64], in1=den[:, h0:h1].unsqueeze(2).to_broadcast([128, nh, 64]),
                        op=ALU.mult), r=["ps%d" % bk_, "den"], w=["On"])

                def mk_tail(qb=qb):
                    def tail_():
                        def trs(e):
                            last = None
                            for c in range(NC_):
                                last = e.transpose(out=ps4b[:, c * 128:(c + 1) * 128], in_=On[:, 2 * c:2 * c + 2, :],
                                                   identity=self.identb[:])
                            return last
                        op("pe", trs, r=["On", "identb"], w=["ps4"])
                        op("act", lambda e: e.copy(out=OT[:, :, qb * 128:(qb + 1) * 128],
                                                   in_=ps4b[:, 0:1024].rearrange("p (c t) -> p c t", c=8)),
                           r=["ps4"], w=[("OT", qb)])
                    return tail_
                tail = mk_tail()
            tail()
            for half in range(2):
                for hh in range(NC_):
                    slot = wos[woi % 4]
                    sk = "wo%d" % (woi % 4)
                    woi += 1
                    op("pool", lambda e, slot=slot, hh=hh, half=half: e.dma_start(
                        out=slot[:], in_=self.wob_d[j, hh * 128:(hh + 1) * 128, half * 512:(half + 1) * 512]), w=[sk], dma=sk)

                    def mmo(e, slot=slot, hh=hh, half=half):
                        last = None
                        for o in range(4):
                            last = e.matmul(self.ps[half * 4 + o][:], lhsT=slot[:, o * 128:(o + 1) * 128], rhs=OT[:, hh, :],
                                            start=(hh == 0), stop=(hh == NC_ - 1))
                        return last
                    op("pe", mmo, r=[sk] + [("OT", q_) for q_ in range(4)], w=["ps%d" % (half * 4 + o) for o in range(4)])
                if half == 0:
                    for o in range(4):
                        op("act", lambda e, o=o: e.copy(out=fTa[:, o, :], in_=self.ps[o][:]), r=["ps%d" % o],
                           w=[("QT", 2 * o), ("QT", 2 * o + 1)])
            srcs = [(fTa[:, o, :], [("QT", 2 * o), ("QT", 2 * o + 1)]) for o in range(4)] + \
                   [(self.ps[4 + o][:], "ps%d" % (4 + o)) for o in range(4)]
            self.postnorm(tt, l, 1, srcs, sqh, rstd1, 0, rkey="rstd1")
        self.use_ln = False

    def dbgq(self, stack, mode):
        op = self.op
        A = lambda n, s, d: self.alloc(stack, n, s, d)
        slot = A("dq_slot", [128, NC_, 2, 128], BF16)
        slotf = A("dq_slotf", [128, NC_, 128], F32)
        if mode == 0:
            op("pool", lambda e: e.dma_start(out=slot[:, :, 0, :], in_=self.wqb_d[0, :, 0:128].rearrange("(c p) n -> p c n", p=128)),
               w=["dq"], dma="dq")
        elif mode == 1:
            op("sp", lambda e: e.dma_start(out=slotf[:], in_=self.wqb_d[0, :, 0:128].rearrange("(c p) n -> p c n", p=128)),
               w=["dq"], dma="dq")
        elif mode == 2:
            op("sp", lambda e: e.dma_start(out=slotf[:, 0, :], in_=self.wqb_d[0, 0:128, 0:128]), w=["dq"], dma="dq")
        elif mode == 3:
            op("sp", lambda e: e.dma_start(out=slotf[:, 0, :], in_=self.wqb_d[1, 0:128, 0:128]), w=["dq"], dma="dq")

def build_program(plan):
    from contextlib import ExitStack
    nc = bass.Bass("TRN2", target_bir_lowering=False)
    b = Builder(nc, plan)
    b.declare()
    with ExitStack() as stack:
        b.alloc_perm(stack)
        with ExitStack() as tmp:
            b.setup(stack, tmp)
            b.setup_A(stack, tmp)
            b.setup_B(stack, tmp)
        b.sc.barrier()
        with ExitStack() as s2:
            b.load_x(s2)
        for item in plan:
            b.sc.barrier()
            b.uid += 1
            with ExitStack() as s2:
                getattr(b, item[0])(s2, *item[1:])
        b.sc.barrier()
        b.uid += 1
        with ExitStack() as s2:
            b.store_x(s2)
        b.sc.emit(nc, stack)
    return nc


FULL_PLAN = [("attnA", 0), ("woA", 0), ("ffn", 0), ("attnA", 1), ("woA", 1), ("ffn", 1),
             ("attnB", 2), ("ffn", 2), ("attnB", 3), ("ffn", 3)]

_IN_ORDER = ["x", "norm_gains", "w_qkv_a", "lambda_qk_a", "subln_a", "w_o_a", "kv_norm", "w_kv_b", "b_kv_b",
             "w_q_b", "b_q_b", "sinks_b", "w_o_b", "w_up", "conv_w", "conv_b", "w_down"]
_SHAPES = {
    "norm_gains": (DEPTH * 4 * NC_, 128), "lambda_qk_a": (1, NA * 4 * HD), "subln_a": (1, NA * 128),
    "kv_norm": (NC_, 128), "b_kv_b": (2, 128), "b_q_b": (2, 2, 8, HD), "sinks_b": (1, 32),
    "conv_w": (DEPTH * 3 * NFF, 128), "conv_b": (DEPTH * NFF, 128),
}


def run(inputs, plan, n_cores=8, trace=False):
    nc = build_program(plan)
    shared = {}
    for k in _IN_ORDER:
        if k == "x":
            continue
        a = np.ascontiguousarray(np.asarray(inputs[k], dtype=np.float32))
        if k in _SHAPES:
            a = a.reshape(_SHAPES[k])
        shared[k] = a
    x = np.asarray(inputs["x"], dtype=np.float32)
    in_maps = []
    for c in range(n_cores):
        m = dict(shared)
        m["x"] = np.ascontiguousarray(x[c])
        in_maps.append(m)
    res = run_bass_kernel_spmd(nc, in_maps, core_ids=list(range(n_cores)), trace=trace)
    out = np.stack([np.asarray(r["out"]) for r in res.results], axis=0)
    return out.astype(np.float32), res


def kernel(**inputs):
    out, _ = run(inputs, FULL_PLAN)
    return out
```

```python
import math
import numpy as np
import concourse.bass as bass
import concourse.mybir as mybir
from concourse.bass_utils import run_bass_kernel_spmd

F32 = mybir.dt.float32
BF16 = mybir.dt.bfloat16
AF = mybir.ActivationFunctionType
ALU = mybir.AluOpType
AX = mybir.AxisListType

DEBUG_OT = False
DBG_B = 99
DBG_NOQDMA = False
D = 1024
S = 4096
DEPTH = 4
NA = 2
DFF = 2816
NFF = DFF // 128
HD = 64
NC_ = 8
T = 512
NT = S // T
NB = S // 128
NORM_EPS = 1e-6
SUBLN_EPS = 1e-5


def _ap(t):
    try:
        return t[:]
    except Exception:
        return t


def alibi_slopes(n):
    return [2.0 ** (-8.0 * (i + 1) / n) for i in range(n)]


class _Op:
    __slots__ = ("eng", "fn", "deps", "dma", "idx", "flag", "done_sem", "done_val", "ndma")


class Sched:
    def __init__(self):
        self.ops = []
        self.last_w = {}
        self.readers = {}
        self.last_dma = {}
        self.barrier_idx = None

    def barrier(self):
        last = {}
        for op in self.ops:
            last[(op.eng, op.dma)] = op.idx
        self.bar_deps = set(last.values())

    def add(self, eng, fn, r=(), w=(), dma=None, ndma=1, insure=True):
        op = _Op()
        op.eng, op.fn, op.dma, op.idx, op.flag, op.ndma = eng, fn, dma, len(self.ops), False, ndma
        deps = set(getattr(self, "bar_deps", ()))
        for k in r:
            if k in self.last_w:
                deps.add(self.last_w[k])
        for k in w:
            if k in self.last_w:
                deps.add(self.last_w[k])
            deps.update(self.readers.get(k, ()))
        if insure and dma is not None and dma in self.last_dma:
            deps.add(self.last_dma[dma])
        for k in r:
            self.readers.setdefault(k, []).append(op.idx)
        for k in w:
            self.last_w[k] = op.idx
            self.readers[k] = []
        if dma is not None:
            self.last_dma[dma] = op.idx
        deps.discard(op.idx)
        op.deps = deps
        self.ops.append(op)
        return op.idx

    def emit(self, nc, stack):
        ops = self.ops
        for op in ops:
            for d in op.deps:
                dd = ops[d]
                if dd.eng == "pe" and op.eng == "pe" and dd.dma is None and op.dma is None:
                    continue
                dd.flag = True
        last_of = {}
        for op in ops:
            last_of[(op.eng, op.dma is not None)] = op
        for op in last_of.values():
            op.flag = True
        cnt = {}
        for e in ("pe", "act", "dve", "pool"):
            cnt[e] = stack.enter_context(nc.semaphore("cnt_" + e))
        dkeys = []
        for op in ops:
            if op.dma is not None and op.dma not in dkeys:
                dkeys.append(op.dma)
        dsem = {k: stack.enter_context(nc.semaphore("dma_%d" % i)) for i, k in enumerate(dkeys)}
        seq = {e: 0 for e in cnt}
        dcount = {k: 0 for k in dkeys}
        for op in ops:
            if op.dma is not None:
                dcount[op.dma] += 16 * op.ndma
                op.done_sem, op.done_val = dsem[op.dma], dcount[op.dma]
            elif op.flag:
                seq[op.eng] += 1
                op.done_sem, op.done_val = cnt[op.eng], seq[op.eng]
            else:
                op.done_sem, op.done_val = None, None
        per_eng = {e: [] for e in ("pe", "act", "dve", "pool", "sp")}
        for op in ops:
            per_eng[op.eng].append(op)
        final_waits = [(op.done_sem, op.done_val) for op in last_of.values()]

        def run(engname, eng):
            waited = {}
            for op in per_eng[engname]:
                for d in sorted(op.deps):
                    dd = ops[d]
                    if dd.done_sem is None:
                        continue
                    if dd.eng == "pe" and engname == "pe" and dd.dma is None and op.dma is None:
                        continue
                    key = id(dd.done_sem)
                    if waited.get(key, (None, 0))[1] < dd.done_val:
                        eng.wait_ge(dd.done_sem, dd.done_val)
                        waited[key] = (dd.done_sem, dd.done_val)
                res = op.fn(eng)
                if op.dma is not None:
                    lst = res if isinstance(res, (list, tuple)) else [res]
                    assert len(lst) == op.ndma, (len(lst), op.ndma)
                    for ins in lst:
                        ins.then_inc(op.done_sem, 16)
                elif op.flag:
                    last = res[-1] if isinstance(res, (list, tuple)) else res
                    last.then_inc(op.done_sem, 1)
            if engname == "sp":
                for sem, val in final_waits:
                    eng.wait_ge(sem, val)

        with nc.Block() as block:
            @block.tensor
            def _(e):
                run("pe", e)

            @block.scalar
            def _(e):
                run("act", e)

            @block.vector
            def _(e):
                run("dve", e)

            @block.gpsimd
            def _(e):
                run("pool", e)

            @block.sync
            def _(e):
                run("sp", e)


class Builder:
    def __init__(self, nc, plan):
        self.nc = nc
        self.plan = plan
        self.sc = Sched()
        self.uid = 0
        self.pc_list = None
        self.use_ln = False

    def op(self, eng, fn, r=(), w=(), dma=None, ndma=1, insure=True):
        return self.sc.add(eng, fn, r, w, dma, ndma, insure)

    def barrier_keys(self):
        return list(self.sc.last_w.keys())

    def declare(self):
        nc = self.nc
        dt = nc.dram_tensor
        self.x_d = dt("x", [S, D], F32, kind="ExternalInput").ap()
        self.out_d = dt("out", [S, D], F32, kind="ExternalOutput").ap()
        self.ng_d = dt("norm_gains", [DEPTH * 4 * NC_, 128], F32, kind="ExternalInput").ap()
        self.wqkv_d = dt("w_qkv_a", [NA, D, 3 * D], F32, kind="ExternalInput").ap()
        self.lam_d = dt("lambda_qk_a", [1, NA * 4 * HD], F32, kind="ExternalInput").ap()
        self.subln_d = dt("subln_a", [1, NA * 128], F32, kind="ExternalInput").ap()
        self.woa_d = dt("w_o_a", [NA, D, D], F32, kind="ExternalInput").ap()
        self.kvn_d = dt("kv_norm", [NC_, 128], F32, kind="ExternalInput").ap()
        self.wkv_d = dt("w_kv_b", [D, 256], F32, kind="ExternalInput").ap()
        self.bkv_d = dt("b_kv_b", [2, 128], F32, kind="ExternalInput").ap()
        self.wqb_d = dt("w_q_b", [2, D, D], F32, kind="ExternalInput").ap()
        self.bqb_d = dt("b_q_b", [2, 2, 8, HD], F32, kind="ExternalInput").ap()
        self.sinks_d = dt("sinks_b", [1, 32], F32, kind="ExternalInput").ap()
        self.wob_d = dt("w_o_b", [2, D, D], F32, kind="ExternalInput").ap()
        self.wup_d = dt("w_up", [DEPTH, D, 2 * DFF], F32, kind="ExternalInput").ap()
        self.convw_d = dt("conv_w", [DEPTH * 3 * NFF, 128], F32, kind="ExternalInput").ap()
        self.convb_d = dt("conv_b", [DEPTH * NFF, 128], F32, kind="ExternalInput").ap()
        self.wdn_d = dt("w_down", [DEPTH, DFF, D], F32, kind="ExternalInput").ap()
        self.wupT_d = dt("wupT_scratch", [DEPTH, NFF // 2, 128, NC_ * 512], BF16).ap()
        self.wdnT_d = dt("wdnT_scratch", [DEPTH, 2, NFF, 128, 512], BF16).ap()
        self.kvT_d = dt("kvT_scratch", [128, S], BF16).ap()
        self.kvV_d = dt("kvV_scratch", [128, NB * 2 * 66], BF16).ap()
        self.ot_d = (dt("ot_scratch", [8, 128, S], BF16, kind="ExternalOutput") if DEBUG_OT else dt("ot_scratch", [8, 128, S], BF16)).ap()


    def alloc(self, stack, name, shape, dtype):
        return stack.enter_context(self.nc.sbuf_tensor("%s_p%d" % (name, self.uid), shape, dtype))

    def palloc(self, stack, name, shape, dtype=F32):
        return stack.enter_context(self.nc.psum_tensor(name, shape, dtype))

    def alloc_perm(self, stack):
        A = lambda n, s, d: self.alloc(stack, n, s, d)
        self.xT = A("xT", [128, NC_, S], F32)
        self.onesf = A("onesf", [128, 128], F32)
        self.identf = A("identf", [128, 128], F32)
        self.identb = A("identb", [128, 128], BF16)
        self.onesb = A("onesb", [128, 128], BF16)
        self.mcur = A("mcur", [128, 128], BF16)
        self.mprev = A("mprev", [128, 128], BF16)
        self.gT = A("gT", [128, 128], F32)
        self.kvg = A("kvg", [128, NC_], F32)
        self.convw = A("convw", [128, DEPTH * 3 * NFF], F32)
        self.convb = A("convb", [128, DEPTH * NFF], F32)
        self.epst = A("epst", [128, 2], F32)
        self.psall = self.palloc(stack, "psall", [128, 8, 512], F32)
        self.ps = [self.psall[:, i, :] for i in range(8)]
        self.tabA = A("tabA", [128, 8, 35], F32)
        self.sgbc = A("sgbc", [128, NA, 128], F32)
        self.nlam = A("nlam", [128, 2], F32)
        self.bqT = A("bqT", [128, 2, 8], F32)
        self.bkc = A("bkc", [128, 2], F32)
        self.bvbc = A("bvbc", [128, 128], F32)
        self.biasB = A("biasB", [128, 2, 16], F32)
        self.sinkf = A("sinkf", [128, 2, 16], F32)

    def setup(self, stack, tmp):
        nc = self.nc
        A = lambda n, s, d: self.alloc(stack, n, s, d)
        TA = lambda n, s, d: self.alloc(tmp, n, s, d)
        self.pstage = TA("pstage", [128, 128], F32)

        op = self.op
        op("pool", lambda e: e.memset(self.onesf[:], 1.0), w=["onesf"])
        op("pool", lambda e: e.memset(self.onesb[:], 1.0), w=["onesb"])
        op("pool", lambda e: e.memset(self.epst[:, 0:1], NORM_EPS), w=["epsc"])
        op("pool", lambda e: e.memset(self.epst[:, 1:2], SUBLN_EPS), w=["epsc"])
        op("pool", lambda e: e.affine_select(out=self.identf[:], in_=self.onesf[:], pattern=[[-1, 128]],
                                             compare_op=ALU.is_equal, fill=0.0, base=0, channel_multiplier=1),
           r=["onesf"], w=["identf"])
        op("pool", lambda e: e.affine_select(out=self.mcur[:], in_=self.onesb[:], pattern=[[1, 128]],
                                             compare_op=ALU.is_ge, fill=0.0, base=0, channel_multiplier=-1),
           r=["onesb"], w=["mcur"])
        op("pool", lambda e: e.affine_select(out=self.mprev[:], in_=self.onesb[:], pattern=[[-1, 128]],
                                             compare_op=ALU.is_gt, fill=0.0, base=0, channel_multiplier=1),
           r=["onesb"], w=["mprev"])
        op("dve", lambda e: e.tensor_copy(out=self.identb[:], in_=self.identf[:]), r=["identf"], w=["identb"])
        self.load_cols(self.ng_d, 128, self.gT, 0)
        self.load_cols(self.kvn_d, NC_, self.kvg, 0)
        for i0 in range(0, DEPTH * 3 * NFF, 128):
            n = min(128, DEPTH * 3 * NFF - i0)
            self.load_cols(self.convw_d[i0:i0 + n, :], n, self.convw, i0)
        self.load_cols(self.convb_d, DEPTH * NFF, self.convb, 0)

    def _colkey(self, dst):
        try:
            return dst.name
        except Exception:
            return "anon"

    def load_cols(self, rows_ap, R, dst, col0, st_view=None):
        op = self.op
        st, ps = self.pstage, self.ps[0]
        if st_view is None:
            op("sp", lambda e: e.dma_start(out=st[0:R, :], in_=rows_ap), w=["pstage"], dma="pstage")
        else:
            op("sp", lambda e: e.dma_start(out=st[0:R, :].rearrange("r (a b) -> r a b", a=st_view[1]), in_=rows_ap),
               w=["pstage"], dma="pstage")
        op("pe", lambda e: e.transpose(out=ps[:, 0:R], in_=st[0:R, :], identity=self.identf[0:R, 0:R]),
           r=["pstage", "identf"], w=["ps0"])
        op("dve", lambda e: e.tensor_copy(out=dst[:, col0:col0 + R], in_=ps[:, 0:R]), r=["ps0"], w=[("cols", self._colkey(dst))])

    def load_x(self, stack):
        op = self.op
        xin = [self.alloc(stack, "xin%d" % i, [128, D], F32) for i in range(2)]
        for tb in range(NB):
            st = xin[tb % 2]
            sk = "xin%d" % (tb % 2)
            op("sp", lambda e, st=st, tb=tb: e.dma_start(out=st[:], in_=self.x_d[tb * 128:(tb + 1) * 128, :]),
               w=[sk], dma=sk)
            for half in range(2):
                ps = self.ps[2 * (tb % 2) + half]
                pk = "ps%d" % (2 * (tb % 2) + half)

                def tr(e, st=st, ps=ps, half=half):
                    last = None
                    for i in range(4):
                        c = half * 4 + i
                        last = e.transpose(out=ps[:, i * 128:(i + 1) * 128], in_=st[:, c * 128:(c + 1) * 128],
                                           identity=self.identf[:])
                    return last
                op("pe", tr, r=[sk, "identf"], w=[pk])
                dst = self.xT[:, half * 4:half * 4 + 4, tb * 128:(tb + 1) * 128]
                eng = "act" if half == 0 else "dve"

                def ev(e, ps=ps, dst=dst, eng=eng):
                    src = ps[:].rearrange("p (c t) -> p c t", c=4)
                    if eng == "act":
                        return e.copy(out=dst, in_=src)
                    return e.tensor_copy(out=dst, in_=src)
                op(eng, ev, r=[pk], w=[("xT", tb // 4)])

    def store_x(self, stack):
        op = self.op
        xo = [self.alloc(stack, "xout%d" % i, [128, D], F32) for i in range(2)]
        for tb in range(NB):
            st = xo[tb % 2]
            sk = "xout%d" % (tb % 2)
            for half in range(2):
                ps = self.ps[2 * (tb % 2) + half]
                pk = "ps%d" % (2 * (tb % 2) + half)

                def tr(e, ps=ps, half=half, tb=tb):
                    last = None
                    for i in range(4):
                        c = half * 4 + i
                        last = e.transpose(out=ps[:, i * 128:(i + 1) * 128],
                                           in_=self.xT[:, c, tb * 128:(tb + 1) * 128], identity=self.identf[:])
                    return last
                op("pe", tr, r=[("xT", tb // 4), "identf"], w=[pk])
                eng = "act" if half == 0 else "dve"

                def ev(e, ps=ps, st=st, half=half, eng=eng):
                    if eng == "act":
                        return e.copy(out=st[:, half * 512:(half + 1) * 512], in_=ps[:])
                    return e.tensor_copy(out=st[:, half * 512:(half + 1) * 512], in_=ps[:])
                op(eng, ev, r=[pk], w=[(sk, half)])
            op("sp", lambda e, st=st, tb=tb: e.dma_start(out=self.out_d[tb * 128:(tb + 1) * 128, :], in_=st[:]),
               r=[(sk, 0), (sk, 1)], dma=sk)


    def gcol(self, l, j, c):
        k = l * 32 + j * 8 + c
        return self.gT[:, k:k + 1]

    def rstd_from_ps(self, psb, pkey, rstd, rkey, n, eps):
        if self.use_ln:
            self.op("act", lambda e: e.activation(out=_ap(rstd), in_=psb[:], func=AF.Ln, bias=self.epsc(eps), scale=1.0 / n),
                    r=[pkey, "epsc"], w=[rkey])
            self.op("act", lambda e: e.activation(out=_ap(rstd), in_=_ap(rstd), func=AF.Exp, scale=-0.5), r=[rkey], w=[rkey])
            return
        self.op("act", lambda e: e.activation(out=_ap(rstd), in_=psb[:], func=AF.Sqrt, bias=self.epsc(eps), scale=1.0 / n),
                r=[pkey, "epsc"], w=[rkey])
        self.op("dve", lambda e: e.reciprocal(out=_ap(rstd), in_=_ap(rstd)), r=[rkey], w=[rkey])

    def epsc(self, eps):
        return self.epst[:, 0:1] if eps == NORM_EPS else self.epst[:, 1:2]

    def stats_xT(self, tt, sqh, psi, rstd, rkey):
        op = self.op
        cols = slice(tt * T, (tt + 1) * T)
        psb, pkey = self.ps[psi], "ps%d" % psi
        for hf in range(2):
            op("act", lambda e, hf=hf: e.activation(out=sqh[:], in_=self.xT[:, hf * 4:hf * 4 + 4, cols], func=AF.Square),
               r=[("xT", tt)], w=[("sqh", i) for i in range(4)])

            def mm(e, hf=hf):
                last = None
                for i in range(4):
                    last = e.matmul(psb[:], lhsT=self.onesb[:], rhs=sqh[:, i, :], start=(hf == 0 and i == 0),
                                    stop=(hf == 1 and i == 3))
                return last
            op("pe", mm, r=[("sqh", i) for i in range(4)] + ["onesb"], w=[pkey])
        self.rstd_from_ps(psb, pkey, rstd, rkey, float(D), NORM_EPS)

    def make_xb(self, tt, l, j, xb, xbkeys, rstd, rkey, eng="dve", gains=None):
        cols = slice(tt * T, (tt + 1) * T)
        for c in range(NC_):
            g = gains[:, c:c + 1] if gains is not None else self.gcol(l, j, c)
            self.op(eng, lambda e, c=c, g=g: e.scalar_tensor_tensor(out=xb[:, c, :], in0=self.xT[:, c, cols], scalar=g,
                                                                     in1=_ap(rstd), op0=ALU.mult, op1=ALU.mult),
                    r=[("xT", tt), rkey, "gT"], w=[xbkeys[c]])

    def postnorm(self, tt, l, j, srcs, sqh, rstd2, psi, rkey="rstd2"):
        op = self.op
        cols = slice(tt * T, (tt + 1) * T)
        psb, pkey = self.ps[psi], "ps%d" % psi
        for hf in range(2):
            for i in range(4):
                ap, key = srcs[hf * 4 + i]
                key = key if isinstance(key, list) else [key]
                in_psum = isinstance(key[0], str) and key[0].startswith("ps")
                if in_psum or i % 2 == 0:
                    op("act", lambda e, ap=ap, i=i: e.activation(out=sqh[:, i, :], in_=ap, func=AF.Square),
                       r=key, w=[("sqh", i)])
                else:
                    op("dve", lambda e, ap=ap, i=i: e.tensor_tensor(out=sqh[:, i, :], in0=ap, in1=ap, op=ALU.mult),
                       r=key, w=[("sqh", i)])

            def mm(e, hf=hf):
                last = None
                for i in range(4):
                    last = e.matmul(psb[:], lhsT=self.onesb[:], rhs=sqh[:, i, :], start=(hf == 0 and i == 0),
                                    stop=(hf == 1 and i == 3))
                return last
            op("pe", mm, r=[("sqh", i) for i in range(4)] + ["onesb"], w=[pkey] + [("sqh", i) for i in range(4)])
        self.rstd_from_ps(psb, pkey, rstd2, rkey, float(D), NORM_EPS)
        for c in range(NC_):
            ap, key = srcs[c]
            key = key if isinstance(key, list) else [key]
            op("dve", lambda e, ap=ap, c=c: e.scalar_tensor_tensor(out=self.ptmp[:], in0=ap, scalar=self.gcol(l, j, c),
                                                                    in1=rstd2[:], op0=ALU.mult, op1=ALU.mult),
               r=key + [rkey, "gT"], w=["ptmp"])
            op("dve", lambda e, c=c: e.tensor_tensor(out=self.xT[:, c, cols], in0=self.xT[:, c, cols], in1=self.ptmp[:],
                                                     op=ALU.add),
               r=["ptmp", ("xT", tt)], w=[("xT", tt)])

    def precast_ops(self):
        ops = []
        for l in range(DEPTH):
            for jj in range(NFF // 2):
                def f(l=l, jj=jj):
                    def dm(e):
                        dst = self.wupT_d[l, jj].rearrange("p (c n) -> p c n", c=NC_)
                        a = e.dma_start(out=dst[:, :, 0:256],
                                        in_=self.wup_d[l, :, jj * 256:(jj + 1) * 256].rearrange("(c p) n -> p c n", p=128))
                        b_ = e.dma_start(out=dst[:, :, 256:512],
                                         in_=self.wup_d[l, :, DFF + jj * 256:DFF + (jj + 1) * 256].rearrange("(c p) n -> p c n", p=128))
                        return [a, b_]
                    self.op("pool", dm, w=[("pc", l, "u", jj)], dma="pc%d" % l, ndma=2, insure=False)
                ops.append(f)
            for half in range(2):
                def f(l=l, half=half):
                    last = (half == 1)
                    self.op("pool", lambda e: e.dma_start(
                        out=self.wdnT_d[l, half],
                        in_=self.wdn_d[l, :, half * 512:(half + 1) * 512].rearrange("(j p) n -> j p n", p=128)),
                        w=[("pc", l, "d", half)] + ([("pcdone", l)] if last else []), dma="pc%d" % l, insure=False)
                ops.append(f)
        return ops

    def precast_take(self, k):
        if self.pc_list is None:
            self.pc_list = self.precast_ops()
        for _ in range(k):
            if self.pc_list:
                self.pc_list.pop(0)()

    def ffn(self, stack, l):
        op = self.op
        self.precast_take(10 ** 6)
        A = lambda n, s, d: self.alloc(stack, n, s, d)
        xb = A("f_xb", [128, NC_, T], BF16)
        sqh = A("f_sqh", [128, 4, T], BF16)
        fTa = A("f_fTa", [128, 4, T], F32)
        aT = A("f_aT", [128, NFF, T], BF16)
        wu = [A("f_wu%d" % i, [128, NC_, 512], BF16) for i in range(2)]
        wd = [A("f_wd%d" % i, [128, 512], BF16) for i in range(4)]
        Cb = [A("f_C%d" % i, [128, T], F32) for i in range(2)]
        rstd1 = A("f_rstd1", [128, T], F32)
        rstd2 = A("f_rstd2", [128, T], F32)
        self.ptmp = A("f_ptmp", [128, T], F32)
        halo = A("f_halo", [128, NFF, 2], F32)
        xbkeys = [("xb", c) for c in range(NC_)]
        wcol = lambda k, j: self.convw[:, l * 66 + k * 22 + j: l * 66 + k * 22 + j + 1]
        bcol = lambda j: self.convb[:, l * 22 + j: l * 22 + j + 1]
        wdi = 0
        self.stats_xT(0, sqh, 4, rstd1, "rstd1")
        self.make_xb(0, l, 2, xb, xbkeys, rstd1, "rstd1")
        for tt in range(NT):
            cols = slice(tt * T, (tt + 1) * T)
            for jj in range(NFF // 2):
                slot = wu[jj % 2]
                sk = "wu%d" % (jj % 2)
                op("sp", lambda e, slot=slot, jj=jj: e.dma_start(out=slot[:].rearrange("p c n -> p (c n)"), in_=self.wupT_d[l, jj]),
                   r=[("pcdone", l)], w=[sk], dma=sk)
                for s in range(2):
                    j = 2 * jj + s
                    gi, ui = j % 2, 2 + j % 2
                    G, U = self.ps[gi], self.ps[ui]
                    gk, uk = "ps%d" % gi, "ps%d" % ui
                    C = Cb[j % 2]
                    ck = "C%d" % (j % 2)

                    def mmg(e, slot=slot, s=s, G=G):
                        last = None
                        for c in range(NC_):
                            last = e.matmul(G[:], lhsT=slot[:, c, s * 128:(s + 1) * 128], rhs=xb[:, c, :],
                                            start=(c == 0), stop=(c == NC_ - 1))
                        return last

                    def mmu(e, slot=slot, s=s, U=U):
                        last = None
                        for c in range(NC_):
                            last = e.matmul(U[:], lhsT=slot[:, c, 256 + s * 128:256 + (s + 1) * 128], rhs=xb[:, c, :],
                                            start=(c == 0), stop=(c == NC_ - 1))
                        return last
                    op("pe", mmg, r=[sk] + xbkeys, w=[gk])
                    op("pe", mmu, r=[sk] + xbkeys, w=[uk])
                    op("act", lambda e, C=C, G=G, j=j: e.activation(out=C[:], in_=G[:], func=AF.Identity,
                                                                    bias=bcol(j), scale=wcol(2, j)),
                       r=[gk, "convw", "convb"], w=[ck])
                    op("dve", lambda e, C=C, G=G, j=j: e.scalar_tensor_tensor(out=C[:, 1:T], in0=G[:, 0:T - 1], scalar=wcol(1, j),
                                                                              in1=C[:, 1:T], op0=ALU.mult, op1=ALU.add),
                       r=[gk, ck, "convw"], w=[ck])
                    op("dve", lambda e, C=C, G=G, j=j: e.scalar_tensor_tensor(out=C[:, 2:T], in0=G[:, 0:T - 2], scalar=wcol(0, j),
                                                                              in1=C[:, 2:T], op0=ALU.mult, op1=ALU.add),
                       r=[gk, ck, "convw"], w=[ck])
                    if tt > 0:
                        op("dve", lambda e, C=C, j=j: e.scalar_tensor_tensor(out=C[:, 0:2], in0=halo[:, j, 0:2], scalar=wcol(0, j),
                                                                             in1=C[:, 0:2], op0=ALU.mult, op1=ALU.add),
                           r=[("halo", j), ck, "convw"], w=[ck])
                        op("dve", lambda e, C=C, j=j: e.scalar_tensor_tensor(out=C[:, 0:1], in0=halo[:, j, 1:2], scalar=wcol(1, j),
                                                                             in1=C[:, 0:1], op0=ALU.mult, op1=ALU.add),
                           r=[("halo", j), ck, "convw"], w=[ck])
                    if tt < NT - 1:
                        op("dve", lambda e, G=G, j=j: e.tensor_copy(out=halo[:, j, :], in_=G[:, T - 2:T]),
                           r=[gk], w=[("halo", j)])
                    op("act", lambda e, C=C: e.activation(out=C[:], in_=C[:], func=AF.Gelu_apprx_tanh), r=[ck], w=[ck])
                    op("dve", lambda e, C=C, U=U, j=j: e.tensor_tensor(out=aT[:, j, :], in0=C[:], in1=U[:], op=ALU.mult),
                       r=[ck, uk], w=[("aT", j)])
                if tt + 1 < NT and jj in (4, 5, 6):
                    nt_ = tt + 1
                    ncols = slice(nt_ * T, (nt_ + 1) * T)
                    p4 = self.ps[4]

                    def sq_half(hf):
                        op("act", lambda e, hf=hf, ncols=ncols: e.activation(out=sqh[:], in_=self.xT[:, hf * 4:hf * 4 + 4, ncols],
                                                                           func=AF.Square),
                           r=[("xT", nt_)], w=[("sqh", i) for i in range(4)])

                    def mm_half(hf):
                        def mm(e, hf=hf):
                            last = None
                            for i in range(4):
                                last = e.matmul(p4[:], lhsT=self.onesb[:], rhs=sqh[:, i, :], start=(hf == 0 and i == 0),
                                                stop=(hf == 1 and i == 3))
                            return last
                        op("pe", mm, r=[("sqh", i) for i in range(4)] + ["onesb"], w=["ps4"])
                    if jj == 4:
                        sq_half(0)
                    elif jj == 5:
                        mm_half(0)
                        sq_half(1)
                    else:
                        mm_half(1)
                        self.rstd_from_ps(p4, "ps4", rstd1, "rstd1", float(D), NORM_EPS)
            if tt + 1 < NT:
                self.make_xb(tt + 1, l, 2, xb, xbkeys, rstd1, "rstd1")
            for half in range(2):
                for j in range(NFF):
                    slot = wd[wdi % 4]
                    sk = "wd%d" % (wdi % 4)
                    wdi += 1
                    op("sp", lambda e, slot=slot, j=j, half=half: e.dma_start(out=slot[:], in_=self.wdnT_d[l, half, j]),
                       r=[("pcdone", l)], w=[sk], dma=sk)

                    def mmd(e, slot=slot, j=j, half=half):
                        last = None
                        for o in range(4):
                            last = e.matmul(self.ps[half * 4 + o][:], lhsT=slot[:, o * 128:(o + 1) * 128], rhs=aT[:, j, :],
                                            start=(j == 0), stop=(j == NFF - 1))
                        return last
                    op("pe", mmd, r=[sk, ("aT", j)], w=["ps%d" % (half * 4 + o) for o in range(4)])
                if half == 0:
                    for o in range(4):
                        op("act", lambda e, o=o: e.copy(out=fTa[:, o, :], in_=self.ps[o][:]), r=["ps%d" % o], w=[("fTa", o)])
            srcs = [(fTa[:, o, :], ("fTa", o)) for o in range(4)] + [(self.ps[4 + o][:], "ps%d" % (4 + o)) for o in range(4)]
            self.postnorm(tt, l, 3, srcs, sqh, rstd2, 0)

    def bcast_row(self, row_ap, n, dst_ap, dkey, rkey, scale=None):
        ps = self.ps[1]
        self.op("pe", lambda e: e.matmul(ps[:, 0:n], lhsT=self.onesf[0:1, :], rhs=row_ap, start=True, stop=True),
                r=[rkey, "onesf"], w=["ps1"])
        if scale is None:
            self.op("dve", lambda e: e.tensor_copy(out=dst_ap, in_=ps[:, 0:n]), r=["ps1"], w=[dkey])
        else:
            self.op("act", lambda e: e.activation(out=dst_ap, in_=ps[:, 0:n], func=AF.Copy, scale=float(scale)),
                    r=["ps1"], w=[dkey])

    def setup_A(self, stack, tmp):
        op = self.op
        A = lambda n, s, d: self.alloc(stack, n, s, d)
        TA = lambda n, s, d: self.alloc(tmp, n, s, d)
        ii = TA("iotaI", [128, 35], mybir.dt.int32)
        fi = TA("iotaF", [128, 35], F32)
        lamr = TA("lamr", [1, 2, 2, 2, HD], F32)
        prod = TA("lamprod", [1, 4, HD], F32)
        red = TA("lamred", [1, 4], F32)
        lrow = TA("lamrow", [1, 2], F32)
        sgrow = TA("sgrow", [1, NA * 128], F32)
        op("pool", lambda e: e.iota(ii[:], pattern=[[-128, 35]], base=384, channel_multiplier=1), w=["iotaI"])
        op("dve", lambda e: e.tensor_copy(out=fi[:], in_=ii[:]), r=["iotaI"], w=["iotaF"])
        sl = alibi_slopes(8)
        for h in range(8):
            W = self.widthA(h)
            op("dve", lambda e, h=h, W=W: e.tensor_scalar(out=self.tabA[:, h, :], in0=fi[:], scalar1=float(W // 2),
                                                          scalar2=float(sl[h]), op0=ALU.subtract, op1=ALU.mult),
               r=["iotaF"], w=["tabA"])
        op("sp", lambda e: e.dma_start(out=lamr[:].rearrange("o l a b d -> o (l a b d)"), in_=self.lam_d), w=["lamr"], dma="lamr")
        op("dve", lambda e: e.tensor_tensor(out=prod[:].rearrange("o (l a) d -> o l a d", l=2), in0=lamr[:, :, :, 0, :],
                                            in1=lamr[:, :, :, 1, :], op=ALU.mult), r=["lamr"], w=["lamprod"])
        op("dve", lambda e: e.reduce_sum(out=red[:], in_=prod[:], axis=AX.X), r=["lamprod"], w=["lamred"])
        op("act", lambda e: e.activation(out=red[:], in_=red[:], func=AF.Exp), r=["lamred"], w=["lamred"])
        rv = red[:].rearrange("o (l a) -> o l a", l=2)
        op("dve", lambda e: e.tensor_tensor(out=lrow[:], in0=rv[:, :, 1], in1=rv[:, :, 0], op=ALU.subtract),
           r=["lamred"], w=["lamrow"])
        for l in range(NA):
            li = 0.8 - 0.6 * math.exp(-0.3 * l)
            op("dve", lambda e, l=l, li=li: e.tensor_scalar(out=lrow[:, l:l + 1], in0=lrow[:, l:l + 1], scalar1=float(li),
                                                            scalar2=None, op0=ALU.subtract), r=["lamrow"], w=["lamrow"])
        self.bcast_row(lrow[:], 2, self.nlam[:], "nlam", "lamrow")
        op("sp", lambda e: e.dma_start(out=sgrow[:], in_=self.subln_d), w=["sgrow"], dma="sgrow")
        for l in range(NA):
            li = 0.8 - 0.6 * math.exp(-0.3 * l)
            self.bcast_row(sgrow[:, l * 128:(l + 1) * 128], 128, self.sgbc[:, l, :], ("sgbc", l), "sgrow", scale=1.0 - li)

    def widthA(self, h):
        s = alibi_slopes(8)[h]
        return int(min(512, max(128, 64.0 / s)))

    def attnA(self, stack, l):
        op = self.op
        A = lambda n, s, d: self.alloc(stack, n, s, d)
        rstdA = A("a_rstd", [128, S], F32)
        sqh = A("a_sqh", [128, 4, T], BF16)
        xbs = [A("a_xb%d" % i, [128, NC_, T], BF16) for i in range(2)]
        wqkv = A("a_wqkv", [128, NC_, 384], BF16)
        KT = A("a_KT", [128, S], BF16)
        Va = A("a_V", [128, NB, 130], BF16)
        QTs = [A("a_QT%d" % i, [128, T], BF16) for i in range(2)]
        PTs = [A("a_PT%d" % i, [128, 2, T], BF16) for i in range(3)]
        Osb = A("a_Osb", [128, 1032], F32)
        rz = A("a_rz", [128, 2, 4], F32)
        ssq = A("a_ssq", [128, 2, 4], F32)
        on = A("a_on", [128, 4, 128], BF16)
        OTst = [A("a_OT%d" % i, [128, T], BF16) for i in range(1)]
        ps5b = self.ps[5][:].bitcast(BF16)
        scr = sqh[:].rearrange("p a t -> p (a t)").bitcast(F32)
        t0 = scr[:, 0:512].rearrange("p (b e) -> p b e", b=4)
        osb = scr[:, 512:1024].rearrange("p (b e) -> p b e", b=4)
        Ov = Osb[:].rearrange("p (s b e) -> p s b e", s=2, b=4)

        op("pool", lambda e: e.memset(Va[:, :, 128:130], 1.0), w=[("V", t_) for t_ in range(NT)])
        for tt in range(NT):
            self.stats_xT(tt, sqh, 5, rstdA[:, tt * T:(tt + 1) * T], ("rstdA", tt))
        p5 = self.ps[5]
        seq = [(h, tt) for h in range(8) for tt in range(NT)]

        def ldw(h):
            def f(e):
                out = []
                for i in range(3):
                    out.append(e.dma_start(out=wqkv[:, :, i * 128:(i + 1) * 128],
                                           in_=self.wqkv_d[l, :, i * D + h * 128:i * D + (h + 1) * 128].rearrange("(c p) n -> p c n", p=128)))
                return out
            op("pool", f, w=["wqkv"], dma="wqkv", ndma=3)
            self.precast_take(7)

        def emit_xb(n):
            h, tt = seq[n]
            xb = xbs[n % 2]
            xbkeys = [("xb%d" % (n % 2), c) for c in range(NC_)]
            self.make_xb(tt, l, 0, xb, xbkeys, rstdA[:, tt * T:(tt + 1) * T], ("rstdA", tt), eng="dve")

        def qkv_parts(n):
            h, tt = seq[n]
            cols = slice(tt * T, (tt + 1) * T)
            xb = xbs[n % 2]
            xbkeys = [("xb%d" % (n % 2), c) for c in range(NC_)]
            QT = QTs[n % 2]
            qk = "QT%d" % (n % 2)

            def part_q():
                def mmq(e):
                    last = None
                    for c in range(NC_):
                        last = e.matmul(p5[:], lhsT=wqkv[:, c, 0:128], rhs=xb[:, c, :], start=(c == 0), stop=(c == NC_ - 1))
                    return last
                op("pe", mmq, r=["wqkv"] + xbkeys, w=["ps5"])
                op("dve", lambda e: e.tensor_copy(out=QT[:], in_=p5[:]), r=["ps5"], w=[qk])

            def part_k():
                def mmk(e):
                    last = None
                    for c in range(NC_):
                        last = e.matmul(p5[:], lhsT=wqkv[:, c, 128:256], rhs=xb[:, c, :], start=(c == 0), stop=(c == NC_ - 1))
                    return last
                op("pe", mmk, r=["wqkv"] + xbkeys, w=["ps5"])
                op("dve", lambda e: e.tensor_copy(out=KT[:, cols], in_=p5[:]), r=["ps5"], w=[("KT", tt)])

            def part_v():
                def mmv(e):
                    last = None
                    for blk in range(4):
                        for c in range(NC_):
                            last = e.matmul(p5[:, blk * 128:(blk + 1) * 128], lhsT=xb[:, c, blk * 128:(blk + 1) * 128],
                                            rhs=wqkv[:, c, 256:384], start=(c == 0), stop=(c == NC_ - 1))
                    return last
                op("pe", mmv, r=["wqkv"] + xbkeys, w=["ps5"])
                op("dve", lambda e: e.tensor_copy(out=Va[:, tt * 4:(tt + 1) * 4, 0:128],
                                                  in_=p5[:].rearrange("p (b e) -> p b e", b=4)),
                   r=["ps5"], w=[("V", tt)])
            return [part_q, part_k, part_v]

        def emit_stream(n, fillers):
            h, tt = seq[n]
            W = self.widthA(h)
            slope = alibi_slopes(8)[h]
            QT = QTs[n % 2]
            qk = "QT%d" % (n % 2)
            nk = 4 * tt + 4
            kts = [kt for kt in range(nk) if slope * (tt * 512 - (kt * 128 + 127)) < 160.0]
            first_kt = kts[0]
            pvs = []
            for ii, kt in enumerate(kts):
                diag = kt >= 4 * tt
                kb = kt - 4 * tt if diag else 0
                qlo = kb * 128
                b0 = 0 if ii % 2 == 0 else 6
                sks = ["ps%d" % b0, "ps%d" % (b0 + 1)]
                PT = PTs[ii % 3]
                pk = "PT%d" % (ii % 3)

                def mms(e, kt=kt, qlo=qlo, b0=b0):
                    last = None
                    for sub in range(2):
                        last = e.matmul(self.ps[b0 + sub][:, qlo:T], lhsT=KT[sub * 64:(sub + 1) * 64, kt * 128:(kt + 1) * 128],
                                        rhs=QT[sub * 64:(sub + 1) * 64, qlo:T], start=True, stop=True)
                    return last
                op("pe", mms, r=[("KT", kt // 4), qk], w=sks)
                if ii >= 2:
                    f_, r_, w_ = pvs[ii - 2]
                    op("pe", f_, r=r_, w=w_)
                if ii in fillers:
                    fillers[ii]()

                def ex(e, kt=kt, qlo=qlo, b0=b0, PT=PT):
                    last = None
                    for gs in range(0, T, W):
                        lo = max(gs, qlo)
                        if lo >= gs + W:
                            continue
                        j = 4 * tt + gs // 128 - kt + 3
                        last = e.activation(out=PT[:, :, lo:gs + W], in_=self.psall[:, b0:b0 + 2, lo:gs + W], func=AF.Exp,
                                            bias=self.tabA[:, h, j:j + 1], scale=0.125)
                    return last
                op("act", ex, r=sks + ["tabA"], w=[pk])
                if diag:
                    op("dve", lambda e, PT=PT, kb=kb: e.tensor_tensor(
                        out=PT[:, :, kb * 128:(kb + 1) * 128], in0=PT[:, :, kb * 128:(kb + 1) * 128],
                        in1=self.mcur[:].unsqueeze(1).to_broadcast([128, 2, 128]), op=ALU.mult), r=[pk, "mcur"], w=[pk])

                def pv(e, kt=kt, kb=kb, PT=PT):
                    last = None
                    for sub in range(2):
                        for qb in range(kb, 4):
                            a_ = sub * 4 + qb
                            O = self.ps[2 + a_ // 3][:, (a_ % 3) * 129:(a_ % 3) * 129 + 129]
                            last = e.matmul(O, lhsT=PT[:, sub, qb * 128:(qb + 1) * 128], rhs=Va[:, kt, 0:129],
                                            start=(kt == first_kt and a_ % 3 == 0), stop=(kt == 4 * tt + qb), skip_group_check=True)
                    return last
                pvs.append((pv, [pk, ("V", kt // 4)], ["ps2", "ps3", "ps4"]))
            for kk in range(max(0, len(kts) - 2), len(kts)):
                f_, r_, w_ = pvs[kk]
                op("pe", f_, r=r_, w=w_)
            for b_, wd_ in ((0, 387), (1, 387), (2, 258)):
                op("dve", lambda e, b_=b_, wd_=wd_: e.tensor_copy(out=Osb[:, b_ * 387:b_ * 387 + wd_], in_=self.ps[2 + b_][:, 0:wd_]),
                   r=["ps%d" % (2 + b_)], w=[("Osb", b_)])

        def emit_norm(n):
            h, tt = seq[n]
            cols = slice(tt * T, (tt + 1) * T)
            OTs = OTst[0]
            otk = "OTst0"
            okeys = [("Osb", b_) for b_ in range(3)]
            zview = Ov[:, :, :, 128]
            op("dve", lambda e: e.reciprocal(out=rz[:], in_=zview), r=okeys, w=["rz"])
            op("dve", lambda e: e.tensor_scalar(out=rz[:, 1, :], in0=rz[:, 1, :], scalar1=self.nlam[:, l:l + 1], scalar2=None,
                                                op0=ALU.mult), r=["rz", "nlam"], w=["rz"])
            o0 = Ov[:, 0, :, 0:128]
            o1 = Ov[:, 1, :, 0:128]
            op("dve", lambda e: e.tensor_tensor(out=t0, in0=o0, in1=rz[:, 0, :].unsqueeze(2).to_broadcast([128, 4, 128]),
                                                op=ALU.mult), r=okeys + ["rz"], w=["t0"])
            op("dve", lambda e: e.tensor_tensor(out=osb, in0=o1, in1=rz[:, 1, :].unsqueeze(2).to_broadcast([128, 4, 128]),
                                                op=ALU.mult), r=okeys + ["rz"], w=["osb"])
            op("dve", lambda e: e.tensor_tensor(out=osb, in0=osb, in1=t0, op=ALU.add), r=["osb", "t0"], w=["osb"])
            op("dve", lambda e: e.tensor_tensor(out=t0, in0=osb, in1=osb, op=ALU.mult), r=["osb"], w=["t0"])
            op("dve", lambda e: e.reduce_sum(out=ssq[:, 0, :], in_=t0, axis=AX.X), r=["t0"], w=["ssq"])
            def part_b():
                op("act", lambda e: e.activation(out=ssq[:, 1, :], in_=ssq[:, 0, :], func=AF.Ln, bias=self.epsc(SUBLN_EPS),
                                                 scale=1.0 / 128), r=["ssq", "epsc"], w=["ssq"])
                op("act", lambda e: e.activation(out=ssq[:, 1, :], in_=ssq[:, 1, :], func=AF.Exp, scale=-0.5), r=["ssq"], w=["ssq"])
                op("dve", lambda e: e.tensor_tensor(out=osb, in0=osb, in1=ssq[:, 1, :].unsqueeze(2).to_broadcast([128, 4, 128]),
                                                    op=ALU.mult), r=["osb", "ssq"], w=["osb"])
                op("dve", lambda e: e.tensor_tensor(out=on[:], in0=osb, in1=self.sgbc[:, l, :].unsqueeze(1).to_broadcast([128, 4, 128]),
                                                    op=ALU.mult), r=["osb", ("sgbc", l)], w=["on"])

            def tail():
                def trs(e):
                    last = None
                    for qb in range(4):
                        last = e.transpose(out=ps5b[:, qb * 128:(qb + 1) * 128], in_=on[:, qb, :], identity=self.identb[:])
                    return last
                op("pe", trs, r=["on", "identb"], w=["ps5"])
                op("dve", lambda e: e.tensor_copy(out=OTs[:], in_=ps5b[:, 0:T]), r=["ps5"], w=[otk])
                op("sp", lambda e: e.dma_start(out=self.ot_d[h, :, cols], in_=OTs[:]), r=[otk], w=[("otd", h, tt)], dma=otk)
            return part_b, tail

        N = len(seq)
        self.use_ln = True
        ldw(0)
        emit_xb(0)
        for part in qkv_parts(0):
            part()
        emit_xb(1)
        tail = None
        partb = None
        for n in range(N):
            h, tt = seq[n]
            nitems = len([kt for kt in range(4 * tt + 4) if alibi_slopes(8)[h] * (tt * 512 - (kt * 128 + 127)) < 160.0])
            fillers = {}
            if n + 1 < N:
                k0 = 1 if nitems > 4 else 0
                if seq[n + 1][0] != h:
                    ldw(seq[n + 1][0])
                    k0 = 8 if nitems >= 12 else 3
                for i_, part in enumerate(qkv_parts(n + 1)):
                    fillers[k0 + i_] = part
            if partb is not None:
                pb_slot = min(2, nitems - 2)
                prev_ = fillers.get(pb_slot)
                fillers[pb_slot] = (lambda a=prev_, b=partb: ((a() if a is not None else None), b()))
            if tail is not None:
                ts_ = min(6, nitems - 1)
                while ts_ in fillers:
                    ts_ += 1
                if ts_ >= nitems:
                    ts_ = nitems - 1
                    prev = fillers.get(ts_)
                    t_ = tail
                    fillers[ts_] = (lambda a=prev, b=t_: (a(), b())) if prev is not None else t_
                else:
                    fillers[ts_] = tail
            if n + 2 < N:
                xs_ = min(5, nitems - 1)
                prev_ = fillers.get(xs_)
                fillers[xs_] = (lambda a=prev_, m=n + 2: ((a() if a is not None else None), emit_xb(m)))
            emit_stream(n, fillers)
            partb, tail = emit_norm(n)
        partb()
        tail()
        self.use_ln = False

    def woA(self, stack, l):
        op = self.op
        A = lambda n, s, d: self.alloc(stack, n, s, d)
        wo = A("o_wo", [128, NC_, D], BF16)
        OTin = [A("o_OT%d" % i, [128, NC_, T], BF16) for i in range(2)]
        fTa = A("o_fTa", [128, 4, T], F32)
        fTb = A("o_fTb", [128, 4, T], F32)
        sqh = A("o_sqh", [128, 4, T], BF16)
        rstd2 = A("o_rstd2", [128, T], F32)
        self.ptmp = A("o_ptmp", [128, T], F32)

        def ldw(e):
            out = []
            for i in range(4):
                out.append(e.dma_start(out=wo[:, 2 * i:2 * i + 2, :],
                                       in_=self.woa_d[l, i * 256:(i + 1) * 256, :].rearrange("(c p) n -> p c n", p=128)))
            return out
        op("pool", ldw, w=["wo"], dma="wo", ndma=4)
        for tt in range(NT):
            cols = slice(tt * T, (tt + 1) * T)
            OT = OTin[tt % 2]
            ok = "OTin%d" % (tt % 2)
            op("sp", lambda e, OT=OT, cols=cols: [e.dma_start(out=OT[:, h, :], in_=self.ot_d[h, :, cols]) for h in range(8)],
               r=[("otd", h, tt) for h in range(8)], w=[ok], dma=ok, ndma=8)
            for half in range(2):
                for o in range(4):
                    oc = half * 4 + o
                    bank = self.ps[half * 4 + o]

                    def mm(e, OT=OT, oc=oc, bank=bank):
                        last = None
                        for hh in range(NC_):
                            last = e.matmul(bank[:], lhsT=wo[:, hh, oc * 128:(oc + 1) * 128], rhs=OT[:, hh, :],
                                            start=(hh == 0), stop=(hh == NC_ - 1))
                        return last
                    op("pe", mm, r=["wo", ok], w=["ps%d" % (half * 4 + o)])
                if half == 0:
                    for o in range(4):
                        op("act", lambda e, o=o: e.copy(out=fTa[:, o, :], in_=self.ps[o][:]), r=["ps%d" % o], w=[("fTa", o)])
                else:
                    for o in range(4):
                        op("dve", lambda e, o=o: e.tensor_copy(out=fTb[:, o, :], in_=self.ps[4 + o][:]), r=["ps%d" % (4 + o)],
                           w=[("fTb", o)])
            srcs = [(fTa[:, o, :], ("fTa", o)) for o in range(4)] + [(fTb[:, o, :], ("fTb", o)) for o in range(4)]
            self.postnorm(tt, l, 1, srcs, sqh, rstd2, 0)


    def setup_B(self, stack, tmp):
        op = self.op
        A = lambda n, s, d: self.alloc(stack, n, s, d)
        TA = lambda n, s, d: self.alloc(tmp, n, s, d)
        pi = TA("posI", [128, 1], mybir.dt.int32)
        pf = TA("posF", [128, 1], F32)
        bvrow = TA("bvrow", [1, 128], F32)
        skrow = TA("skrow", [1, 32], F32)
        skbc = TA("skbc", [128, 32], F32)
        sl = alibi_slopes(16)
        for j in range(2):
            self.load_cols(self.bqb_d[j].rearrange("two g d -> g two d"), 8, self.bqT[:, j, :], 0, st_view=(8, 2, 64))
        self.load_cols(self.bkv_d, 2, self.bkc, 0)
        op("sp", lambda e: e.dma_start(out=bvrow[:], in_=self.bkv_d[1:2, :]), w=["bvrow"], dma="bvrow")
        self.bcast_row(bvrow[:], 128, self.bvbc[:], "bvbc", "bvrow")
        op("pool", lambda e: e.iota(pi[:], pattern=[[0, 1]], base=-64, channel_multiplier=1), w=["posI"])
        op("dve", lambda e: e.tensor_copy(out=pf[:], in_=pi[:]), r=["posI"], w=["posF"])
        for h in range(16):
            op("dve", lambda e, h=h: e.tensor_scalar(out=self.biasB[:, 1, h:h + 1], in0=pf[:], scalar1=float(sl[h]), scalar2=None,
                                                     op0=ALU.mult), r=["posF"], w=["biasB"])
            op("dve", lambda e, h=h: e.tensor_scalar(out=self.biasB[:, 0, h:h + 1], in0=pf[:], scalar1=float(sl[h]),
                                                     scalar2=float(-128.0 * sl[h]), op0=ALU.mult, op1=ALU.add),
               r=["posF"], w=["biasB"])
        op("sp", lambda e: e.dma_start(out=skrow[:], in_=self.sinks_d), w=["skrow"], dma="skrow")
        self.bcast_row(skrow[:], 32, skbc[:], "skbc", "skrow")
        for j in range(2):
            op("dve", lambda e, j=j: e.tensor_tensor(out=self.sinkf[:, j, :], in0=skbc[:, j * 16:(j + 1) * 16], in1=self.biasB[:, 1, :],
                                                     op=ALU.add), r=["skbc", "biasB"], w=["sinkf"])
        op("act", lambda e: e.activation(out=self.sinkf[:], in_=self.sinkf[:], func=AF.Exp), r=["sinkf"], w=["sinkf"])

    def attnB(self, stack, l):
        op = self.op
        j = l - NA
        self.use_ln = True
        A = lambda n, s, d: self.alloc(stack, n, s, d)
        KT = A("b_KT", [128, S], BF16)
        Vb = A("b_V", [128, NB, 2, 66], BF16)
        wqs = [A("b_wq%d" % i, [128, NC_, 2, 128], BF16) for i in range(2)]
        wos = [A("b_wo%d" % i, [128, 512], BF16) for i in range(4)]
        xb = A("b_xb", [128, NC_, T], BF16)
        sqh = A("b_sqh", [128, 4, T], BF16)
        QT = A("b_QT", [128, 8, T], BF16)
        PTs = [A("b_PT%d" % i, [128, 2, 2, 128], BF16) for i in range(2)]
        On = A("b_On", [128, 16, HD], BF16)
        OT = A("b_OT", [128, NC_, T], BF16)
        fTa = QT[:].rearrange("p g t -> p (g t)").bitcast(F32).rearrange("p (o t) -> p o t", o=4)
        rstd1 = A("b_rstd1", [128, T], F32)
        self.ptmp = A("b_ptmp", [128, T], F32)
        den = A("b_den", [128, 16], F32)
        xbkeys = [("xb", c) for c in range(NC_)]
        ps4b = self.ps[4][:].bitcast(BF16)
        obank = lambda hd: (2, 3, 7)[hd // 7]
        ooff = lambda hd: (hd % 7) * 65
        if l == NA:
            wkv = A("b_wkv", [128, NC_, 256], BF16)
            op("pool", lambda e: e.dma_start(out=wkv[:], in_=self.wkv_d.rearrange("(c p) n -> p c n", p=128)), w=["wkv"], dma="wkv")
            op("pool", lambda e: e.memset(Vb[:, :, :, 64:66], 1.0), w=[("V", t_) for t_ in range(NT)])
        else:
            op("sp", lambda e: e.dma_start(out=KT[:], in_=self.kvT_d), r=[("kvTd", t_) for t_ in range(NT)],
               w=[("KT", t_) for t_ in range(NT)], dma="KTld")
            op("sp", lambda e: e.dma_start(out=Vb[:].rearrange("p a b c -> p (a b c)"), in_=self.kvV_d),
               r=[("kvVd", t_) for t_ in range(NT)], w=[("V", t_) for t_ in range(NT)], dma="Vld")
        wqi = 0
        woi = 0
        p4 = self.ps[4]

        def prep(tt):
            cols = slice(tt * T, (tt + 1) * T)
            self.stats_xT(tt, sqh, 4, rstd1, "rstd1")
            if l == NA:
                self.make_xb(tt, l, 0, xb, xbkeys, rstd1, "rstd1", gains=self.kvg)

                def mmk(e):
                    last = None
                    for c in range(NC_):
                        last = e.matmul(p4[:], lhsT=wkv[:, c, 0:128], rhs=xb[:, c, :], start=(c == 0), stop=(c == NC_ - 1))
                    return last
                op("pe", mmk, r=["wkv"] + xbkeys, w=["ps4"])
                op("act", lambda e: e.activation(out=KT[:, cols], in_=p4[:], func=AF.Identity, bias=self.bkc[:, 0:1]),
                   r=["ps4", "bkc"], w=[("KT", tt)])

                def mmv(e):
                    last = None
                    for blk in range(4):
                        for c in range(NC_):
                            last = e.matmul(p4[:, blk * 128:(blk + 1) * 128], lhsT=xb[:, c, blk * 128:(blk + 1) * 128],
                                            rhs=wkv[:, c, 128:256], start=(c == 0), stop=(c == NC_ - 1))
                    return last
                op("pe", mmv, r=["wkv"] + xbkeys, w=["ps4"])
                for blk in range(4):
                    op("dve", lambda e, blk=blk: e.tensor_tensor(
                        out=Vb[:, tt * 4 + blk, :, 0:64], in0=p4[:, blk * 128:(blk + 1) * 128].rearrange("p (k d) -> p k d", k=2),
                        in1=self.bvbc[:].rearrange("p (k d) -> p k d", k=2), op=ALU.add), r=["ps4", "bvbc"], w=[("V", tt)])
                op("sp", lambda e: e.dma_start(out=self.kvT_d[:, cols], in_=KT[:, cols]), r=[("KT", tt)],
                   w=[("kvTd", tt)], dma="kvst")
                op("sp", lambda e: e.dma_start(out=self.kvV_d[:, tt * 4 * 132:(tt + 1) * 4 * 132],
                                               in_=Vb[:, tt * 4:(tt + 1) * 4, :, :].rearrange("p a b c -> p (a b c)")),
                   r=[("V", tt)], w=[("kvVd", tt)], dma="kvst2")
            self.make_xb(tt, l, 0, xb, xbkeys, rstd1, "rstd1")

        prep(0)
        for tt in range(NT):
            cols = slice(tt * T, (tt + 1) * T)
            for g in range(8):
                if g % 2 == 0:
                    slot = wqs[wqi % 2]
                    sk = "wq%d" % (wqi % 2)
                    wqi += 1

                    def ldq(e, slot=slot, g=g):
                        a = e.dma_start(out=slot[:, :, 0, :],
                                        in_=self.wqb_d[j, :, g * 64:(g + 2) * 64].rearrange("(c p) n -> p c n", p=128))
                        b_ = e.dma_start(out=slot[:, :, 1, :],
                                         in_=self.wqb_d[j, :, 512 + g * 64:512 + (g + 2) * 64].rearrange("(c p) n -> p c n", p=128))
                        return [a, b_]
                    op("pool", ldq, w=[sk], dma=sk, ndma=2)
                pq = self.ps[5 + g % 2]
                pqk = "ps%d" % (5 + g % 2)

                def mmq(e, slot=slot, pq=pq, g=g):
                    last = None
                    for c in range(NC_):
                        o_ = (g % 2) * 64
                        e.matmul(pq[0:64, :], lhsT=slot[:, c, 0, o_:o_ + 64], rhs=xb[:, c, :],
                                 start=(c == 0), stop=(c == NC_ - 1), tile_position=(0, 0))
                        last = e.matmul(pq[64:128, :], lhsT=slot[:, c, 1, o_:o_ + 64], rhs=xb[:, c, :],
                                        start=(c == 0), stop=(c == NC_ - 1), tile_position=(0, 64))
                    return last
                op("pe", mmq, r=[sk] + xbkeys, w=[pqk])
                op("act", lambda e, g=g, pq=pq: e.activation(out=QT[:, g, :], in_=pq[:], func=AF.Identity, bias=self.bqT[:, j, g:g + 1]),
                   r=[pqk, "bqT"], w=[("QT", g)])
            if tt + 1 < NT:
                prep(tt + 1)
            tail = None
            for qb in range(4):
                gq = tt * 4 + qb
                kts = ([(0, gq - 1)] if gq > 0 else []) + [(1, gq)]
                started = set()
                pend = None
                for g in range(8):
                    sbanks = (0, 1) if g % 2 == 0 else (5, 6)
                    Sbs = [self.ps[sbanks[0]], self.ps[sbanks[1]]]
                    sks = ["ps%d" % sbanks[0], "ps%d" % sbanks[1]]
                    PT = PTs[g % 2]
                    pk = "bPT%d" % (g % 2)

                    def mms(e, g=g, Sbs=Sbs, qb=qb, kts=kts):
                        last = None
                        for (ki, kt) in kts:
                            for hh in range(2):
                                last = e.matmul(Sbs[hh][:, ki * 128:(ki + 1) * 128],
                                                lhsT=KT[hh * 64:(hh + 1) * 64, kt * 128:(kt + 1) * 128],
                                                rhs=QT[hh * 64:(hh + 1) * 64, g, qb * 128:(qb + 1) * 128], start=True, stop=True)
                        return last
                    op("pe", mms, r=[("KT", kt // 4) for (_, kt) in kts] + [("QT", g)], w=sks)
                    if pend is not None:
                        op("pe", pend[0], r=pend[1], w=pend[2])
                        pend = None
                    if g == 1 and tail is not None:
                        tail()
                        tail = None

                    def ex(e, g=g, Sbs=Sbs, PT=PT, kts=kts):
                        last = None
                        for (ki, kt) in kts:
                            for hh in range(2):
                                hd = g + 8 * hh
                                last = e.activation(out=PT[:, ki, hh, :], in_=Sbs[hh][:, ki * 128:(ki + 1) * 128],
                                                    func=AF.Exp, bias=self.biasB[:, ki, hd:hd + 1], scale=0.125)
                        return last
                    op("act", ex, r=sks + ["biasB"], w=[pk])

                    def msk(e, PT=PT, kts=kts):
                        last = None
                        for (ki, kt) in kts:
                            m = self.mprev if ki == 0 else self.mcur
                            last = e.tensor_tensor(out=PT[:, ki, :, :], in0=PT[:, ki, :, :],
                                                   in1=m[:].unsqueeze(1).to_broadcast([128, 2, 128]), op=ALU.mult)
                        return last
                    op("dve", msk, r=[pk, "mcur", "mprev"], w=[pk])
                    pvl = []
                    for hh in range(2):
                        hd = g + 8 * hh
                        bk_ = obank(hd)
                        for n_, (ki, kt) in enumerate(kts):
                            st = (n_ == 0) and (bk_ not in started)
                            if n_ == 0:
                                started.add(bk_)
                            pvl.append((hh, hd, bk_, ki, kt, st, n_ == len(kts) - 1))

                    def pv(e, PT=PT, pvl=pvl):
                        last = None
                        for (hh, hd, bk_, ki, kt, st, sp_) in pvl:
                            last = e.matmul(self.ps[bk_][:, ooff(hd):ooff(hd) + 65], lhsT=PT[:, ki, hh, :],
                                            rhs=Vb[:, kt, hh, 0:65], start=st, stop=sp_, skip_group_check=True)
                        return last
                    pend = (pv, [pk] + [("V", kt // 4) for (_, kt) in kts], ["ps2", "ps3", "ps7"])
                op("pe", pend[0], r=pend[1], w=pend[2])
                for bk_, h0, h1 in ((2, 0, 7), (3, 7, 14), (7, 14, 16)):
                    nh = h1 - h0
                    ov = self.ps[bk_][:, 0:nh * 65].rearrange("p (h e) -> p h e", e=65)
                    op("dve", lambda e, ov=ov, h0=h0, h1=h1: e.tensor_tensor(out=den[:, h0:h1], in0=ov[:, :, 64],
                                                                             in1=self.sinkf[:, j, h0:h1], op=ALU.add),
                       r=["ps%d" % bk_, "sinkf"], w=["den"])
                op("dve", lambda e: e.reciprocal(out=den[:], in_=den[:]), r=["den"], w=["den"])
                for bk_, h0, h1 in ((2, 0, 7), (3, 7, 14), (7, 14, 16)):
                    nh = h1 - h0
                    ov = self.ps[bk_][:, 0:nh * 65].rearrange("p (h e) -> p h e", e=65)
                    op("dve", lambda e, ov=ov, h0=h0, h1=h1, nh=nh: e.tensor_tensor(
                        out=On[:, h0:h1, :], in0=ov[:, :, 0:64], in1=den[:, h0:h1].unsqueeze(2).to_broadcast([128, nh, 64]),
                        op=ALU.mult), r=["ps%d" % bk_, "den"], w=["On"])

                def mk_tail(qb=qb):
                    def tail_():
                        def trs(e):
                            last = None
                            for c in range(NC_):
                                last = e.transpose(out=ps4b[:, c * 128:(c + 1) * 128], in_=On[:, 2 * c:2 * c + 2, :],
                                                   identity=self.identb[:])
                            return last
                        op("pe", trs, r=["On", "identb"], w=["ps4"])
                        op("act", lambda e: e.copy(out=OT[:, :, qb * 128:(qb + 1) * 128],
                                                   in_=ps4b[:, 0:1024].rearrange("p (c t) -> p c t", c=8)),
                           r=["ps4"], w=[("OT", qb)])
                    return tail_
                tail = mk_tail()
            tail()
            for half in range(2):
                for hh in range(NC_):
                    slot = wos[woi % 4]
                    sk = "wo%d" % (woi % 4)
                    woi += 1
                    op("pool", lambda e, slot=slot, hh=hh, half=half: e.dma_start(
                        out=slot[:], in_=self.wob_d[j, hh * 128:(hh + 1) * 128, half * 512:(half + 1) * 512]), w=[sk], dma=sk)

                    def mmo(e, slot=slot, hh=hh, half=half):
                        last = None
                        for o in range(4):
                            last = e.matmul(self.ps[half * 4 + o][:], lhsT=slot[:, o * 128:(o + 1) * 128], rhs=OT[:, hh, :],
                                            start=(hh == 0), stop=(hh == NC_ - 1))
                        return last
                    op("pe", mmo, r=[sk] + [("OT", q_) for q_ in range(4)], w=["ps%d" % (half * 4 + o) for o in range(4)])
                if half == 0:
                    for o in range(4):
                        op("act", lambda e, o=o: e.copy(out=fTa[:, o, :], in_=self.ps[o][:]), r=["ps%d" % o],
                           w=[("QT", 2 * o), ("QT", 2 * o + 1)])
            srcs = [(fTa[:, o, :], [("QT", 2 * o), ("QT", 2 * o + 1)]) for o in range(4)] + \
                   [(self.ps[4 + o][:], "ps%d" % (4 + o)) for o in range(4)]
            self.postnorm(tt, l, 1, srcs, sqh, rstd1, 0, rkey="rstd1")
        self.use_ln = False

    def dbgq(self, stack, mode):
        op = self.op
        A = lambda n, s, d: self.alloc(stack, n, s, d)
        slot = A("dq_slot", [128, NC_, 2, 128], BF16)
        slotf = A("dq_slotf", [128, NC_, 128], F32)
        if mode == 0:
            op("pool", lambda e: e.dma_start(out=slot[:, :, 0, :], in_=self.wqb_d[0, :, 0:128].rearrange("(c p) n -> p c n", p=128)),
               w=["dq"], dma="dq")
        elif mode == 1:
            op("sp", lambda e: e.dma_start(out=slotf[:], in_=self.wqb_d[0, :, 0:128].rearrange("(c p) n -> p c n", p=128)),
               w=["dq"], dma="dq")
        elif mode == 2:
            op("sp", lambda e: e.dma_start(out=slotf[:, 0, :], in_=self.wqb_d[0, 0:128, 0:128]), w=["dq"], dma="dq")
        elif mode == 3:
            op("sp", lambda e: e.dma_start(out=slotf[:, 0, :], in_=self.wqb_d[1, 0:128, 0:128]), w=["dq"], dma="dq")

def build_program(plan):
    from contextlib import ExitStack
    nc = bass.Bass("TRN2", target_bir_lowering=False)
    b = Builder(nc, plan)
    b.declare()
    with ExitStack() as stack:
        b.alloc_perm(stack)
        with ExitStack() as tmp:
            b.setup(stack, tmp)
            b.setup_A(stack, tmp)
            b.setup_B(stack, tmp)
        b.sc.barrier()
        with ExitStack() as s2:
            b.load_x(s2)
        for item in plan:
            b.sc.barrier()
            b.uid += 1
            with ExitStack() as s2:
                getattr(b, item[0])(s2, *item[1:])
        b.sc.barrier()
        b.uid += 1
        with ExitStack() as s2:
            b.store_x(s2)
        b.sc.emit(nc, stack)
    return nc


FULL_PLAN = [("attnA", 0), ("woA", 0), ("ffn", 0), ("attnA", 1), ("woA", 1), ("ffn", 1),
             ("attnB", 2), ("ffn", 2), ("attnB", 3), ("ffn", 3)]

_IN_ORDER = ["x", "norm_gains", "w_qkv_a", "lambda_qk_a", "subln_a", "w_o_a", "kv_norm", "w_kv_b", "b_kv_b",
             "w_q_b", "b_q_b", "sinks_b", "w_o_b", "w_up", "conv_w", "conv_b", "w_down"]
_SHAPES = {
    "norm_gains": (DEPTH * 4 * NC_, 128), "lambda_qk_a": (1, NA * 4 * HD), "subln_a": (1, NA * 128),
    "kv_norm": (NC_, 128), "b_kv_b": (2, 128), "b_q_b": (2, 2, 8, HD), "sinks_b": (1, 32),
    "conv_w": (DEPTH * 3 * NFF, 128), "conv_b": (DEPTH * NFF, 128),
}


def run(inputs, plan, n_cores=8, trace=False):
    nc = build_program(plan)
    shared = {}
    for k in _IN_ORDER:
        if k == "x":
            continue
        a = np.ascontiguousarray(np.asarray(inputs[k], dtype=np.float32))
        if k in _SHAPES:
            a = a.reshape(_SHAPES[k])
        shared[k] = a
    x = np.asarray(inputs["x"], dtype=np.float32)
    in_maps = []
    for c in range(n_cores):
        m = dict(shared)
        m["x"] = np.ascontiguousarray(x[c])
        in_maps.append(m)
    res = run_bass_kernel_spmd(nc, in_maps, core_ids=list(range(n_cores)), trace=trace)
    out = np.stack([np.asarray(r["out"]) for r in res.results], axis=0)
    return out.astype(np.float32), res


def kernel(**inputs):
    out, _ = run(inputs, FULL_PLAN)
    return out
```
